# Optimizing a Trainium2 kernel written in Bass

```python
import math
import jax, jax.numpy as jnp
from jax import lax
import numpy as np

D_MODEL = 1024
BATCH = 2
SEQ = 16384
DEPTH = 4

GRID_W = 64
CTX_LEN = 256
D_MIX = D_MODEL
F_GROUPS = 4
F_GROUP_DIM = 64
F_WIDTH = F_GROUPS * F_GROUP_DIM
MLA_HEADS = 8
QK_NOPE_DIM = 64
QK_ROPE_DIM = 32
V_HEAD_DIM = 64
MLA_WIDTH = MLA_HEADS * V_HEAD_DIM
Q_LORA_RANK = 256
KV_LORA_RANK = 128
CONV_WIDTH = D_MIX - F_WIDTH - MLA_WIDTH
CONV_KERNEL = 31
D_IN = F_WIDTH + Q_LORA_RANK + KV_LORA_RANK + QK_ROPE_DIM + 2 * CONV_WIDTH
SPLITS = (F_WIDTH,
          F_WIDTH + Q_LORA_RANK,
          F_WIDTH + Q_LORA_RANK + KV_LORA_RANK,
          F_WIDTH + Q_LORA_RANK + KV_LORA_RANK + QK_ROPE_DIM)
D_FF = 4 * D_MODEL
N_MOD = 6
ROPE_BASE = 10000.0
ROPE_AXIS_DIM = QK_ROPE_DIM // 2
ATTN_SCALE = (QK_NOPE_DIM + QK_ROPE_DIM) ** -0.5
Q_BLOCK = 128
EPS = 1e-6

kernel_name = "hybrid_fourier_mla_conformer_dit"


def rms_norm(x, g):
    xf = x.astype(jnp.float32)
    y = xf * lax.rsqrt(jnp.mean(jnp.square(xf), axis=-1, keepdims=True) + EPS)
    return (y * g.astype(jnp.float32)).astype(x.dtype)


def layer_norm(x, g, b):
    xf = x.astype(jnp.float32)
    mu = jnp.mean(xf, axis=-1, keepdims=True)
    xc = xf - mu
    var = jnp.mean(jnp.square(xc), axis=-1, keepdims=True)
    y = xc * lax.rsqrt(var + EPS) * g.astype(jnp.float32) + b.astype(jnp.float32)
    return y.astype(x.dtype)


def modulate(h, shift, scale):
    return h * (1 + scale) + shift


def axial_rope_angles(n_tokens):
    rows = n_tokens // GRID_W
    t = jnp.arange(rows * GRID_W)
    row = (t // GRID_W).astype(jnp.float32)
    col = (t % GRID_W).astype(jnp.float32)
    inv = ROPE_BASE ** (-jnp.arange(0, ROPE_AXIS_DIM, 2, dtype=jnp.float32) / ROPE_AXIS_DIM)
    ang = jnp.concatenate([row[:, None] * inv, col[:, None] * inv], axis=-1)
    return jnp.cos(ang), jnp.sin(ang)


def apply_rope(x, cos, sin):
    half = QK_ROPE_DIM // 2
    x1, x2 = x[..., :half], x[..., half:]
    cos = cos.astype(x.dtype)
    sin = sin.astype(x.dtype)
    return jnp.concatenate([x1 * cos - x2 * sin, x1 * sin + x2 * cos], axis=-1)


def fourier_mix(zf, w_f):
    B, N, _ = zf.shape
    u = zf.reshape(B, N, F_GROUPS, F_GROUP_DIM).astype(jnp.float32)
    y = jnp.fft.fftn(u, axes=(1, 3), norm="ortho").real
    return y.reshape(B, N, F_WIDTH).astype(zf.dtype) @ w_f


def conformer_conv(zg, w_dw, b_dw, ln_g, ln_b, w_pw2):
    a, g = jnp.split(zg, 2, axis=-1)
    u = a * jax.nn.sigmoid(g)
    pad = CONV_KERNEL // 2
    u = lax.conv_general_dilated(u, w_dw[:, None, :], window_strides=(1,),
                                 padding=((pad, pad),),
                                 dimension_numbers=("NWC", "WIO", "NWC"),
                                 feature_group_count=CONV_WIDTH) + b_dw
    u = jax.nn.silu(layer_norm(u, ln_g, ln_b))
    return u @ w_pw2


def mla_queries(cq, q_g, w_uq):
    B, N, _ = cq.shape
    q = (rms_norm(cq, q_g) @ w_uq).reshape(B, N, MLA_HEADS, QK_NOPE_DIM + QK_ROPE_DIM)
    return q[..., :QK_NOPE_DIM], q[..., QK_NOPE_DIM:]


def mla_keys_values(ckv, kv_g, w_ukv):
    B, N, _ = ckv.shape
    kv = (rms_norm(ckv, kv_g) @ w_ukv).reshape(B, N, MLA_HEADS, QK_NOPE_DIM + V_HEAD_DIM)
    return kv[..., :QK_NOPE_DIM], kv[..., QK_NOPE_DIM:]


def mla_attend(qn, qr, kn, kr, v):
    s = jnp.einsum("bqhd,bkhd->bhqk", qn, kn) + jnp.einsum("bqhr,bkr->bhqk", qr, kr)
    p = jax.nn.softmax(s.astype(jnp.float32) * ATTN_SCALE, axis=-1).astype(v.dtype)
    return jnp.einsum("bhqk,bkhd->bqhd", p, v)


def blocked_mla_attend(qn, qr, kn, kr, v):
    B, N, H, _ = qn.shape
    nblk = N // Q_BLOCK

    def to_blocks(q):
        return q.reshape(B, nblk, Q_BLOCK, H, q.shape[-1]).transpose(1, 0, 2, 3, 4)

    out = lax.map(lambda qs: mla_attend(qs[0], qs[1], kn, kr, v), (to_blocks(qn), to_blocks(qr)))
    return out.transpose(1, 0, 2, 3, 4).reshape(B, N, H * V_HEAD_DIM)


def setup_inputs(seed: int = 0) -> dict:
    key = jax.random.key(seed)
    ks = jax.random.split(key, 24)
    f32 = jnp.float32
    L = DEPTH

    def nrm(k, shape, scale):
        return jax.random.normal(k, shape, f32) * scale

    return {
        "x": nrm(ks[0], (BATCH, SEQ, D_MODEL), 1.0),
        "c": nrm(ks[1], (BATCH, D_MODEL), 1.0),
        "ctx": nrm(ks[2], (BATCH, CTX_LEN, D_MODEL), 1.0),
        "c_ctx": nrm(ks[3], (D_MODEL,), 1.0),
        "w_mod": nrm(ks[4], (L, D_MODEL, N_MOD * D_MODEL), 0.5 * D_MODEL ** -0.5),
        "b_mod": nrm(ks[5], (L, N_MOD * D_MODEL), 0.01),
        "norm1_g": 1.0 + nrm(ks[6], (L, D_MODEL), 0.05),
        "w_in": nrm(ks[7], (L, D_MODEL, D_IN), D_MODEL ** -0.5),
        "q_norm_g": 1.0 + nrm(ks[8], (L, Q_LORA_RANK), 0.05),
        "w_uq": nrm(ks[9], (L, Q_LORA_RANK, MLA_HEADS * (QK_NOPE_DIM + QK_ROPE_DIM)), Q_LORA_RANK ** -0.5),
        "kv_norm_g": 1.0 + nrm(ks[10], (L, KV_LORA_RANK), 0.05),
        "w_ukv": nrm(ks[11], (L, KV_LORA_RANK, MLA_HEADS * (QK_NOPE_DIM + V_HEAD_DIM)), KV_LORA_RANK ** -0.5),
        "w_fourier": nrm(ks[12], (L, F_WIDTH, F_WIDTH), F_WIDTH ** -0.5),
        "w_dw": nrm(ks[13], (L, CONV_KERNEL, CONV_WIDTH), CONV_KERNEL ** -0.5),
        "b_dw": nrm(ks[14], (L, CONV_WIDTH), 0.01),
        "conv_ln_g": 1.0 + nrm(ks[15], (L, CONV_WIDTH), 0.05),
        "conv_ln_b": nrm(ks[16], (L, CONV_WIDTH), 0.01),
        "w_pw2": nrm(ks[17], (L, CONV_WIDTH, CONV_WIDTH), CONV_WIDTH ** -0.5),
        "w_o": nrm(ks[18], (L, D_MIX, D_MODEL), D_MIX ** -0.5),
        "norm2_g": 1.0 + nrm(ks[19], (L, D_MODEL), 0.05),
        "w_mlp1": nrm(ks[20], (L, D_MODEL, D_FF), D_MODEL ** -0.5),
        "w_mlp2": nrm(ks[21], (L, D_FF, D_MODEL), D_FF ** -0.5),
        "final_norm_g": 1.0 + nrm(ks[22], (D_MODEL,), 0.05),
    }


def reference(x, c, ctx, c_ctx, w_mod, b_mod, norm1_g, w_in, q_norm_g, w_uq, kv_norm_g, w_ukv,
              w_fourier, w_dw, b_dw, conv_ln_g, conv_ln_b, w_pw2, w_o, norm2_g, w_mlp1, w_mlp2,
              final_norm_g):
    B, S, D = x.shape
    cos, sin = axial_rope_angles(S)
    silu_c = jax.nn.silu(c)
    silu_cc = jax.nn.silu(c_ctx)

    def mixer_rest(zf, zg, attn_out, l):
        y_f = fourier_mix(zf, w_fourier[l])
        y_c = conformer_conv(zg, w_dw[l], b_dw[l], conv_ln_g[l], conv_ln_b[l], w_pw2[l])
        return jnp.concatenate([y_f, attn_out, y_c], axis=-1) @ w_o[l]

    def channel_mlp(h, l):
        return jnp.square(jax.nn.relu(h @ w_mlp1[l])) @ w_mlp2[l]

    for l in range(DEPTH):
        last = l == DEPTH - 1
        mx = jnp.split((silu_c @ w_mod[l] + b_mod[l])[:, None, :], N_MOD, axis=-1)
        mc = jnp.split(silu_cc @ w_mod[l] + b_mod[l], N_MOD, axis=-1)

        zx = modulate(rms_norm(x, norm1_g[l]), mx[0], mx[1]) @ w_in[l]
        zc = modulate(rms_norm(ctx, norm1_g[l]), mc[0], mc[1]) @ w_in[l]
        fx, cqx, ckvx, krx, gx = jnp.split(zx, SPLITS, axis=-1)
        fc, cqc, ckvc, krc, gc = jnp.split(zc, SPLITS, axis=-1)

        kn_c, v_c = mla_keys_values(ckvc, kv_norm_g[l], w_ukv[l])
        kn_x, v_x = mla_keys_values(ckvx, kv_norm_g[l], w_ukv[l])
        kr_x = apply_rope(krx, cos, sin)
        qn_x, qr_x = mla_queries(cqx, q_norm_g[l], w_uq[l])
        qr_x = apply_rope(qr_x, cos[:, None, :], sin[:, None, :])

        kn_all = jnp.concatenate([kn_c, kn_x], axis=1)
        kr_all = jnp.concatenate([krc, kr_x], axis=1)
        v_all = jnp.concatenate([v_c, v_x], axis=1)
        attn_x = blocked_mla_attend(qn_x, qr_x, kn_all, kr_all, v_all)

        if not last:
            qn_c, qr_c = mla_queries(cqc, q_norm_g[l], w_uq[l])
            attn_c = mla_attend(qn_c, qr_c, kn_c, krc, v_c).reshape(B, ctx.shape[1], MLA_WIDTH)
            ctx = ctx + mc[2] * mixer_rest(fc, gc, attn_c, l)
            hc = modulate(rms_norm(ctx, norm2_g[l]), mc[3], mc[4])
            ctx = ctx + mc[5] * channel_mlp(hc, l)

        x = x + mx[2] * mixer_rest(fx, gx, attn_x, l)
        hx = modulate(rms_norm(x, norm2_g[l]), mx[3], mx[4])
        x = x + mx[5] * channel_mlp(hx, l)

    return rms_norm(x, final_norm_g)
```

```python
import contextlib
import os
SKIP = set(os.environ.get('KSKIP', '').split(','))
from types import SimpleNamespace
import numpy as np
import ml_dtypes
import concourse.bass as bass
import concourse.mybir as mybir
from concourse.bass_utils import run_bass_kernel_spmd

F32 = mybir.dt.float32
BF16 = mybir.dt.bfloat16
AF = mybir.ActivationFunctionType
ALU = mybir.AluOpType

ENGS = ("pe", "act", "dve", "pool", "sp")
EPOCH = 20000
NDMASEM = 12
NCCSEM = 3


class Op:
    __slots__ = ("eng", "fn", "deps", "signal", "dma", "sem", "val", "idx", "dslot", "inc")

    def __init__(self, eng, fn, dma):
        self.eng = eng
        self.fn = fn
        self.deps = []
        self.signal = False
        self.dma = dma
        self.sem = None
        self.val = 0
        self.idx = 0
        self.dslot = None
        self.inc = 16


class Prog:
    def __init__(self):
        self.ops = {e: [] for e in ENGS}
        self.last_w = {}
        self.readers = {}
        self.pending_dma = []
        self.all_last = {e: None for e in ENGS}
        self.persist_w = {}

    def op(self, eng, fn, reads=(), writes=(), dma=False, extra_deps=(), inc=16, persist=False):
        o = Op(eng, fn, dma)
        o.inc = inc
        pr = [k for k in reads if isinstance(k, str) and k.startswith("ps") and k[2:].isdigit()]
        if pr:
            writes = list(writes) + pr
        o.idx = len(self.ops[eng])
        deps = []
        for k in reads:
            w = self.last_w.get(k)
            if w is not None:
                deps.append((w, "raw"))
        for k in writes:
            w = self.last_w.get(k)
            if w is not None:
                deps.append((w, "waw"))
            for r in self.readers.get(k, ()):
                deps.append((r, "war"))
        for d in extra_deps:
            if d is not None:
                deps.append((d, "raw"))
        seen = set()
        for d, kind in deps:
            if d is o or id(d) in seen:
                continue
            need = True
            if d.eng == eng and not d.dma and not dma:
                if kind != "raw" or eng == "pe":
                    need = False
            if need:
                seen.add(id(d))
                o.deps.append(d)
                d.signal = True
        for k in reads:
            self.readers.setdefault(k, []).append(o)
        for k in writes:
            self.last_w[k] = o
            self.readers[k] = []
        self.ops[eng].append(o)
        if persist:
            for k in writes:
                self.persist_w[k] = o
        else:
            self.all_last[eng] = o
            if dma:
                self.pending_dma.append(o)
        return o

    def barrier(self):
        lasts = [self.all_last[e] for e in ENGS if self.all_last[e] is not None]
        pend = list(self.pending_dma)
        self.pending_dma = []
        for e in ENGS:
            self.op(e, None, extra_deps=lasts + pend)
        self.last_w = dict(self.persist_w)
        self.readers = {}

    def emit(self, nc):
        n_sems_needed = 0
        plan = {}
        for e in ENGS:
            cnt = 0
            epoch = 0
            for o in self.ops[e]:
                if o.dma:
                    continue
                if o.signal:
                    if cnt >= EPOCH:
                        epoch += 1
                        cnt = 0
                    cnt += 1
                    o.sem = (e, epoch)
                    o.val = cnt
            plan[e] = epoch + 1
        for e in ENGS:
            k = 0
            kc_ = 0
            for o in self.ops[e]:
                if o.dma:
                    if o.inc == 16:
                        o.dslot = k % NDMASEM
                        k += 1
                    else:
                        o.dslot = NDMASEM + (kc_ % NCCSEM)
                        kc_ += 1
        with contextlib.ExitStack() as st:
            sems = {}
            for e in ENGS:
                for ep in range(plan[e]):
                    sems[(e, ep)] = st.enter_context(nc.semaphore(f"s_{e}_{ep}"))
            dsems = {}
            for e in ENGS:
                if any(o.dma for o in self.ops[e]):
                    for k in range(NDMASEM + NCCSEM):
                        dsems[(e, k)] = st.enter_context(nc.semaphore(f"d_{e}_{k}"))
            block = st.enter_context(nc.Block())
            engmap = {"pe": block.tensor, "act": block.scalar, "dve": block.vector,
                      "pool": block.gpsimd, "sp": block.sync}

            def make(e):
                def body(eng):
                    waited = {}
                    dcount = {k: 0 for k in range(NDMASEM + NCCSEM)}
                    dlast = {}
                    for o in self.ops[e]:
                        for d in o.deps:
                            if d.dma:
                                key = ("d", d.eng, d.dslot)
                                s = dsems[(d.eng, d.dslot)]
                                v = d.val
                            else:
                                key = d.sem
                                s = sems[d.sem]
                                v = d.val
                            if waited.get(key, 0) >= v:
                                continue
                            waited[key] = v
                            eng.wait_ge(s, v)
                        if o.dma:
                            key = ("d", e, o.dslot)
                            prev = dcount[o.dslot]
                            if prev > 0 and waited.get(key, 0) < prev:
                                eng.wait_ge(dsems[(e, o.dslot)], prev)
                                waited[key] = prev
                            ins = o.fn(eng)
                            dcount[o.dslot] = prev + o.inc
                            o.val = prev + o.inc
                            ins.then_inc(dsems[(e, o.dslot)], o.inc)
                        else:
                            if o.fn is None:
                                if o.signal:
                                    eng.nop().then_inc(sems[o.sem], 1)
                                continue
                            ins = o.fn(eng)
                            if o.signal:
                                ins.then_inc(sems[o.sem], 1)
                return body

            for e in ENGS:
                dc = {k: 0 for k in range(NDMASEM + NCCSEM)}
                for o in self.ops[e]:
                    if o.dma:
                        dc[o.dslot] += o.inc
                        o.val = dc[o.dslot]
            for e in ENGS:
                if self.ops[e]:
                    engmap[e](make(e))


D = 1024
KC = 8
DIN = 1184
DFF = 4096
NH = 8
CT = 256
EPS = 1e-6
ATTN_SCALE = 96.0 ** -0.5
SB_BASE = 16512
SB_TOP = 229376 - 64


def _dtsize(dt):
    return 4 if dt == F32 else 2


class Ctx:
    def __init__(self, nc):
        self.nc = nc
        self.P = Prog()
        self.off = SB_BASE
        self.limit = SB_TOP
        self.uid = 0
        self.ps = [nc.alloc_psum_tensor(f"ps{i}", [128, 512], F32) for i in range(8)]
        self.psi = 0

    def alloc(self, name, shape, dt):
        n = 1
        for s in shape[1:]:
            n *= s
        nbytes = n * _dtsize(dt)
        off = (self.off + 63) // 64 * 64
        assert off + nbytes <= self.limit, (name, off, nbytes, self.limit)
        self.off = off + nbytes
        self.uid += 1
        return self.nc.alloc_sbuf_tensor_at(f"{name}_{self.uid}", list(shape), dt, offset=off)

    def alloc_at(self, name, shape, dt, off):
        self.uid += 1
        return self.nc.alloc_sbuf_tensor_at(f"{name}_{self.uid}", list(shape), dt, offset=off)

    def mark(self):
        return self.off

    def reset(self, m):
        self.P.barrier()
        self.off = m

    def bank(self, lo=0, hi=8):
        i = lo + (self.psi % (hi - lo))
        self.psi += 1
        return i

    def mm(self, out, lhsT, rhs, start, stop, r, w):
        return self.P.op("pe", lambda e: e.matmul(out, lhsT=lhsT, rhs=rhs, start=start, stop=stop),
                         reads=r, writes=w)

    def tr(self, out, in_, idn, r, w):
        return self.P.op("pe", lambda e: e.transpose(out, in_, idn), reads=r, writes=w)

    def act(self, out, in_, func, r, w, bias=None, scale=None):
        kw = {}
        if bias is not None:
            kw["bias"] = bias
        if scale is not None:
            kw["scale"] = scale
        return self.P.op("act", lambda e: e.activation(out, in_, func, **kw), reads=r, writes=w)

    def tt(self, eng, out, in0, in1, op, r, w):
        return self.P.op(eng, lambda e: e.tensor_tensor(out, in0, in1, op), reads=r, writes=w)

    def ts(self, eng, out, in0, s1, s2, op0, op1, r, w):
        if s2 is None:
            return self.P.op(eng, lambda e: e.tensor_scalar(out, in0, s1, None, op0), reads=r, writes=w)
        return self.P.op(eng, lambda e: e.tensor_scalar(out, in0, s1, s2, op0, op1), reads=r, writes=w)

    def stt(self, eng, out, in0, scalar, in1, op0, op1, r, w):
        return self.P.op(eng, lambda e: e.scalar_tensor_tensor(out, in0, scalar, in1, op0, op1),
                         reads=r, writes=w)

    def cp(self, eng, out, in_, r, w):
        if eng == "act":
            return self.P.op("act", lambda e: e.copy(out, in_), reads=r, writes=w)
        return self.P.op(eng, lambda e: e.tensor_copy(out, in_), reads=r, writes=w)

    def recip(self, out, in_, r, w):
        return self.P.op("dve", lambda e: e.reciprocal(out, in_), reads=r, writes=w)

    def memset(self, eng, ap, val, w):
        return self.P.op(eng, lambda e: e.memset(ap, val), writes=w)

    def dma(self, q, out, in_, r, w, persist=False):
        return self.P.op(q, lambda e: e.dma_start(out=out, in_=in_), reads=r, writes=w, dma=True, persist=persist)

    def allgather(self, src, dst, r, w, persist=False):
        return self.P.op("pool", lambda e: e.collective_compute(
            "AllGather", ALU.bypass, replica_groups=[[0, 1, 2, 3], [4, 5, 6, 7]],
            ins=[src], outs=[dst]), reads=r, writes=w, dma=True, inc=1, persist=persist)


def build_program(NT, L, dbg=None, dump=None):
    S = 128 * NT
    TC = S // 4
    NB = TC // 512
    TCt = TC // 128
    NKT = 2 + NT
    NK = NKT * 128
    assert TC % 512 == 0

    nc = bass.Bass("TRN2", target_bir_lowering=False)
    C = Ctx(nc)
    P = C.P

    def din(name, shape, dt=F32):
        return nc.dram_tensor(name, list(shape), dt, kind="ExternalInput")

    def dtmp(name, shape, dt):
        if dump and name in dump:
            return nc.dram_tensor(name, list(shape), dt, kind="ExternalOutput")
        return nc.dram_tensor(name, list(shape), dt)

    x_in = din("x_in", [TC, D])
    ctx_in = din("ctx_in", [CT, D])
    cvec = din("cvec", [2, D])
    w_mod = din("w_mod", [L, D, 1536]); b_mod = din("b_mod", [L, 6 * D])
    norm1_g = din("norm1_g", [L, D]); w_in = din("w_in", [L, D, DIN])
    q_norm_g = din("q_norm_g", [L, 256]); w_uq = din("w_uq", [L, 256, 768])
    kv_norm_g = din("kv_norm_g", [L, 128]); w_ukv = din("w_ukv", [L, 128, 1024])
    w_fourier = din("w_fourier", [L, 256, 256]); w_dw = din("w_dw", [L, 31, 256])
    b_dw = din("b_dw", [L, 256]); conv_ln_g = din("conv_ln_g", [L, 256]); conv_ln_b = din("conv_ln_b", [L, 256])
    w_pw2 = din("w_pw2", [L, 256, 256]); w_o = din("w_o", [L, D, D]); norm2_g = din("norm2_g", [L, D])
    w_mlp1 = din("w_mlp1", [L, D, DFF]); w_mlp2 = din("w_mlp2", [L, DFF, D])
    final_norm_g = din("final_norm_g", [D])
    rope1 = din("rope1", [32, TC]); rope2 = din("rope2", [32, TC])
    fft_cs = din("fft_cs", [NT, 2 * NT], BF16)
    fft_r1 = din("fft_r1", [128, NT * 64], BF16); fft_r2 = din("fft_r2", [128, NT * 64], BF16)
    ccb_d = din("ccb", [128, 128], BF16); mscb_d = din("mscb", [128, 128], BF16)
    w256_d = din("w256", [256, 512], BF16)
    ccbc_d = din("ccbc", [128, 128], BF16); mscbc_d = din("mscbc", [128, 128], BF16)
    ident_d = din("ident", [128, 128]); masks_d = din("masks", [128, 8])
    out_d = nc.dram_tensor("out", [TC, D], F32, kind="ExternalOutput")

    xT = dtmp("xT", [KC, 128, TC], F32); cT = dtmp("cT", [KC, 128, CT], F32)
    h2x = dtmp("h2x", [KC, 128, TC], BF16); h2c = dtmp("h2c", [KC, 128, CT], BF16)
    qx = dtmp("qx", [NH, 96, TC], BF16); qc = dtmp("qc", [NH, 96, CT], BF16)
    TCq = min(TC, 1024); NKV = TC // TCq
    FR = min(2 * TC, 2048); NF = 2 * TC // FR
    kv_own = [dtmp(f"kv_own{i}", [160, TCq], BF16) for i in range(NKV)]
    kv_all = [dtmp(f"kv_all{i}", [640, TCq], BF16) for i in range(NKV)]
    kvc = dtmp("kvc", [160, CT], BF16)
    f_own = [dtmp(f"f_own{i}", [FR, 128], BF16) for i in range(NF)]
    f_all = [dtmp(f"f_all{i}", [4 * FR, 128], BF16) for i in range(NF)]
    fc = dtmp("fc", [CT, 256], BF16)
    upx = dtmp("upx", [2, 128, TC + 32], BF16); upc = dtmp("upc", [2, 128, CT + 32], BF16)
    e_own = dtmp("e_own", [256, 32], BF16); e_all = dtmp("e_all", [1024, 32], BF16)
    mixx = dtmp("mixx", [KC, 128, TC], BF16); mixc = dtmp("mixc", [KC, 128, CT], BF16)

    kinds = {
        "x": dict(j=0, T=TC, xT=xT, h2=h2x, q=qx, kv=None, up=upx, mix=mixx),
        "c": dict(j=1, T=CT, xT=cT, h2=h2c, q=qc, kv=kvc, up=upc, mix=mixc),
    }

    def blocks(with_ctx=True):
        bl = []
        if with_ctx:
            bl.append(("c", 0, CT))
        for b in range(NB):
            bl.append(("x", b * 512, 512))
        return bl

    def tview(dt_, t0, n):
        return dt_.ap().rearrange("kc p n -> p kc n")[:, :, t0:t0 + n]

    ident = C.alloc("ident", [128, 128], F32)
    ones_bf = C.alloc("ones_bf", [128, 128], BF16)
    ones_f = C.alloc("ones_f", [128, 128], F32)
    eps_t = C.alloc("eps", [128, 1], F32)
    masks = C.alloc("masks", [128, 8], F32)
    modT = C.alloc("modT", [128, L, 2, 48], F32)
    bmodT = C.alloc("bmodT", [128, L, 48], F32)
    n1gT = C.alloc("n1gT", [128, L, 8], F32); n2gT = C.alloc("n2gT", [128, L, 8], F32)
    gm1 = C.alloc("gm1", [128, L, 2, 8], F32); gm2 = C.alloc("gm2", [128, L, 2, 8], F32)
    qgT = C.alloc("qgT", [128, L, 2], F32); kvgT = C.alloc("kvgT", [128, L, 1], F32)
    bdwT = C.alloc("bdwT", [128, L, 2], F32); lngT = C.alloc("lngT", [128, L, 2], F32)
    lnbT = C.alloc("lnbT", [128, L, 2], F32); wdwT = C.alloc("wdwT", [128, L, 62], F32)
    fngT = C.alloc("fngT", [128, 8], F32); cvT = C.alloc("cvT", [128, 16], F32)
    scT = C.alloc("scT", [128, 16], BF16)
    wukv_t = C.alloc("wukv", [128, 1024], BF16)
    PERSIST = C.mark()

    C.dma("sp", ident[:, :], ident_d.ap(), [], ["ident"])
    C.dma("sp", masks[:, :], masks_d.ap(), [], ["masks"])
    C.memset("dve", ones_bf[:, :], 1.0, ["ones_bf"])
    C.memset("dve", ones_f[:, :], 1.0, ["ones_f"])
    C.memset("dve", eps_t[:, :], EPS, ["eps"])

    stg = C.alloc("stg", [128, 128], F32)
    stg2 = C.alloc("stg2", [128, 128], F32)

    def rows128(ap1d, n):
        return ap1d.rearrange("(r p) -> r p", p=128)

    def vec_transpose(stage, key, nrows, dsts):
        b = C.bank()
        C.tr(C.ps[b][:, 0:nrows], stage[0:nrows, :], ident[0:nrows, 0:nrows], [key, "ident"], [f"ps{b}"])
        for dst, c0, c1, wk in dsts:
            C.cp("dve", dst, C.ps[b][:, c0:c1], [f"ps{b}"], [wk])

    for l in range(L):
        specs = [(rows128(b_mod.ap()[l], 48), 48), (rows128(norm1_g.ap()[l], 8), 8),
                 (rows128(norm2_g.ap()[l], 8), 8), (rows128(q_norm_g.ap()[l], 2), 2),
                 (rows128(kv_norm_g.ap()[l], 1), 1), (rows128(b_dw.ap()[l], 2), 2),
                 (rows128(conv_ln_g.ap()[l], 2), 2), (rows128(conv_ln_b.ap()[l], 2), 2)]
        r0 = 0
        for src, n in specs:
            C.dma("sp", stg[r0:r0 + n, :], src, [], ["stg"])
            r0 += n
        vec_transpose(stg, "stg", 73, [
            (bmodT[:, l, :], 0, 48, "bmodT"), (n1gT[:, l, :], 48, 56, "n1gT"), (n2gT[:, l, :], 56, 64, "n2gT"),
            (qgT[:, l, :], 64, 66, "qgT"), (kvgT[:, l, :], 66, 67, "kvgT"), (bdwT[:, l, :], 67, 69, "bdwT"),
            (lngT[:, l, :], 69, 71, "lngT"), (lnbT[:, l, :], 71, 73, "lnbT")])
        C.dma("sp", stg2[0:62, :], w_dw.ap()[l].rearrange("k (j p) -> (k j) p", p=128), [], ["stg2"])
        vec_transpose(stg2, "stg2", 62, [(wdwT[:, l, :], 0, 62, "wdwT")])
    C.dma("sp", stg[0:16, :], cvec.ap().rearrange("j (kc p) -> (j kc) p", p=128), [], ["stg"])
    C.dma("sp", stg[16:24, :], rows128(final_norm_g.ap(), 8), [], ["stg"])
    vec_transpose(stg, "stg", 24, [(cvT[:, :], 0, 16, "cvT"), (fngT[:, :], 16, 24, "fngT")])
    C.act(scT[:, :], cvT[:, :], AF.Silu, ["cvT"], ["scT"])

    m0 = C.mark()
    wm = [C.alloc(f"wm{i}", [128, 8, 1536], BF16) for i in range(2)]
    mpart = C.alloc("mpart", [128, L, 2, 12], F32)
    mp_own = dtmp("mp_own", [128, L * 24], F32)
    mp_all = dtmp("mp_all", [512, L * 24], F32)
    for l in range(L):
        wmt = wm[l % 2]
        for kc in range(8):
            C.dma("pool", wmt[:, kc, :], w_mod.ap()[l, kc * 128:(kc + 1) * 128, :], [], [("wm", l % 2, kc)])
        bm = C.bank()
        for oc in range(12):
            for kc in range(8):
                C.mm(C.ps[bm][:, oc * 2:oc * 2 + 2], wmt[:, kc, oc * 128:(oc + 1) * 128],
                     bass.AP(scT, kc, [[16, 128], [8, 2]]), kc == 0, kc == 7,
                     [("wm", l % 2, kc), "scT"], [f"ps{bm}"])
        for j in range(2):
            C.cp("dve", mpart[:, l, j, :], bass.AP(C.ps[bm], j, [[512, 128], [2, 12]]), [f"ps{bm}"], ["mpart"])
    C.dma("sp", mp_own.ap(), mpart[:, :, :, :].rearrange("p l j o -> p (l j o)"), ["mpart"], ["mp_own"])
    C.allgather(mp_own.ap(), mp_all.ap(), ["mp_own"], ["mp_all"])
    for r in range(4):
        C.dma("sp", modT[:, :, :, r * 12:(r + 1) * 12],
              mp_all.ap()[r * 128:(r + 1) * 128, :].rearrange("p (l j o) -> p l j o", l=L, j=2), ["mp_all"], [("modraw", r)])
    mrk = [("modraw", r) for r in range(4)]
    for l in range(L):
        for j in range(2):
            C.tt("dve", modT[:, l, j, :], modT[:, l, j, :], bmodT[:, l, :], ALU.add, mrk + ["bmodT"], ["modT"])
        for j in range(2):
            C.stt("dve", gm1[:, l, j, :], modT[:, l, j, 8:16], 1.0, n1gT[:, l, :], ALU.add, ALU.mult,
                  ["modT", "n1gT"], ["gm1"])
            C.stt("dve", gm2[:, l, j, :], modT[:, l, j, 32:40], 1.0, n2gT[:, l, :], ALU.add, ALU.mult,
                  ["modT", "n2gT"], ["gm2"])
    C.reset(m0)

    m0 = C.mark()
    xtok = [C.alloc(f"xtok{i}", [128, 4, D], F32) for i in range(2)]
    xbt = [C.alloc(f"xbt{i}", [128, 8, 512], F32) for i in range(2)]
    for bi, (kind, t0, n) in enumerate(blocks()):
        pb = bi % 2
        src = ctx_in if kind == "c" else x_in
        ntt = n // 128
        C.dma("sp", xtok[pb][:, 0:ntt, :], src.ap()[t0:t0 + n, :].rearrange("(tt p) d -> p tt d", p=128),
              [], [f"xtok{pb}"])
        for kc in range(8):
            b = C.bank()
            for tt_ in range(ntt):
                C.tr(C.ps[b][:, tt_ * 128:(tt_ + 1) * 128], xtok[pb][:, tt_, kc * 128:(kc + 1) * 128], ident[:, :],
                     [f"xtok{pb}", "ident"], [f"ps{b}"])
            C.cp("act" if kc % 2 else "dve", xbt[pb][:, kc, 0:n], C.ps[b][:, 0:n], [f"ps{b}"], [f"xbt{pb}"])
        C.dma("sp", tview(kinds[kind]["xT"], t0, n), xbt[pb][:, :, 0:n], [f"xbt{pb}"], [("xT", kind, t0)])
    zt = C.alloc("zt", [128, 2, 16], BF16)
    C.memset("dve", zt[:, :, :], 0.0, ["zt"])
    C.dma("sp", upc.ap().rearrange("j p n -> p j n")[:, :, 0:16], zt[:, :, :], ["zt"], ["upc_h"])
    C.dma("sp", upc.ap().rearrange("j p n -> p j n")[:, :, CT + 16:CT + 32], zt[:, :, :], ["zt"], ["upc_h"])
    C.reset(m0)

    def rms_stats(src3, n, nk, inv_n, sqt, sskey_r, tag):
        C.act(sqt[:, 0:nk, 0:n], src3, AF.Square, [sskey_r], ["sq" + tag])
        b = C.bank()
        for kc in range(nk):
            C.mm(C.ps[b][:, 0:n], ones_bf[:, :], sqt[:, kc, 0:n], kc == 0, kc == nk - 1,
                 ["sq" + tag, "ones_bf"], [f"ps{b}"])
        return b

    for l in range(L):
        last = (l == L - 1)
        if dbg == "P":
            break
        mA = C.mark()
        w_in_t = C.alloc("w_in_t", [128, 8, DIN], BF16)
        krsw = C.alloc("krsw", [128, 8, 32], BF16)
        w_uq_t = C.alloc("w_uq_t", [128, 2, 768], BF16)
        w_uq_s = C.alloc("w_uq_s", [128, 2, 768], BF16)
        for kc in range(8):
            C.dma("pool", w_in_t[:, kc, :], w_in.ap()[l, kc * 128:(kc + 1) * 128, :], [], [("w_in", kc)])
        C.dma("pool", wukv_t[:, :], w_ukv.ap()[l], [], ["wukv"], persist=True)
        C.dma("pool", krsw[:, :, 0:16], w_in.ap()[l].rearrange("(kc p) n -> p kc n", p=128)[:, :, 656:672], [], ["krsw0"])
        C.dma("pool", krsw[:, :, 16:32], w_in.ap()[l].rearrange("(kc p) n -> p kc n", p=128)[:, :, 640:656], [], ["krsw1"])
        uq_v = w_uq.ap()[l].rearrange("(j p) n -> p j n", p=128)
        C.dma("pool", w_uq_t[:, :, :], uq_v, [], ["w_uq"])
        C.dma("pool", w_uq_s[:, :, :], uq_v, [], ["w_uq_s"])
        uq_v4 = w_uq.ap()[l].rearrange("(j p) (h e) -> p j h e", p=128, e=96)
        s4 = w_uq_s[:, :, :].rearrange("p j (h e) -> p j h e", e=96)
        for j in range(2):
            C.dma("pool", s4[:, j, :, 64:80], uq_v4[:, j, :, 80:96], [], ["w_uq_s"])
            C.dma("pool", s4[:, j, :, 80:96], uq_v4[:, j, :, 64:80], [], ["w_uq_s"])

        xb = [C.alloc(f"xbA{i}", [128, 8, 512], F32) for i in range(2)]
        sq = C.alloc("sqA", [128, 8, 512], BF16)
        rs = C.alloc("rsA", [128, 512], F32)
        rstd = C.alloc("rstdA", [128, 512], F32)
        tmpA = [C.alloc(f"tmpA{i}", [128, 512], F32) for i in range(2)]
        hA = [C.alloc(f"hA{i}", [128, 8, 512], BF16) for i in range(2)]
        ftok = C.alloc("ftok", [128, 4, 256], BF16)
        cq = C.alloc("cq", [128, 2, 512], F32)
        sqq = C.alloc("sqq", [128, 2, 512], BF16)
        rq = C.alloc("rq", [128, 512], F32)
        cqn = C.alloc("cqn", [128, 2, 512], BF16)
        qst = C.alloc("qst", [128, NH, 512], BF16)
        rt1 = C.alloc("rt1", [128, 512], F32)
        rt2 = C.alloc("rt2", [128, 512], F32)
        rp1 = C.alloc("rp1", [128, 512], F32)
        rp2 = C.alloc("rp2", [128, 512], F32)
        ckv = C.alloc("ckv", [128, 512], F32)
        ckvn = C.alloc("ckvn", [128, 512], BF16)
        krr = C.alloc("krr", [128, 512], BF16)
        sg = C.alloc("sg", [128, 512], F32)
        ut = C.alloc("ut", [128, 2, 512], BF16)

        bl = blocks()
        def loadA(bi):
            kind, t0, n = bl[bi]
            C.dma("sp", xb[bi % 2][:, :, 0:n], tview(kinds[kind]["xT"], t0, n), [("xT", kind, t0)], [f"xbA{bi % 2}"])
        rsN = C.alloc("rsN", [128, 512], F32)

        def normA(bi):
            kind, t0, n = bl[bi]
            pb = bi % 2
            j = kinds[kind]["j"]
            b = rms_stats(xb[pb][:, :, 0:n], n, 8, None, sq, f"xbA{pb}", "A")
            C.act(rsN[:, 0:n], C.ps[b][:, 0:n], AF.Sqrt, [f"ps{b}", "eps"], ["rsN"], bias=eps_t[:, 0:1], scale=1.0 / D)
            C.recip(rstd[:, 0:n], rsN[:, 0:n], ["rsN"], ["rstdA"])
            for kc in range(8):
                C.tt("dve", tmpA[kc % 2][:, 0:n], xb[pb][:, kc, 0:n], rstd[:, 0:n], ALU.mult,
                     [f"xbA{pb}", "rstdA"], [f"tmpA{kc % 2}"])
                C.act(hA[pb][:, kc, 0:n], tmpA[kc % 2][:, 0:n], AF.Identity, [f"tmpA{kc % 2}", "gm1", "modT"], [f"hA{pb}"],
                      bias=modT[:, l, j, kc:kc + 1], scale=gm1[:, l, j, kc:kc + 1])

        loadA(0)
        if len(bl) > 1:
            loadA(1)
        normA(0)
        pend_ag = []
        defer_ag = []
        for bi, (kind, t0, n) in enumerate(bl):
            pb = bi % 2
            kd = kinds[kind]
            j = kd["j"]
            if bi + 1 < len(bl):
                normA(bi + 1)
            if bi + 2 < len(bl):
                loadA(bi + 2)
            for ag in pend_ag:
                C.allgather(*ag)
            pend_ag = []
            if kind == "x":
                C.dma("sp", rp1[64:96, 0:n], rope1.ap()[:, t0:t0 + n], [], ["rp1q"])
                C.dma("sp", rp2[64:96, 0:n], rope2.ap()[:, t0:t0 + n], [], ["rp2q"])
                C.dma("sp", rp1[0:32, 0:n], rope1.ap()[:, t0:t0 + n], [], ["rp1k"])
                C.dma("sp", rp2[0:32, 0:n], rope2.ap()[:, t0:t0 + n], [], ["rp2k"])
            hk = f"hA{pb}"
            h = hA[pb]
            do_rest = not (kind == "c" and last)
            if do_rest and 'f' not in SKIP:
                ntt = n // 128
                for t2 in range(0, ntt, 2):
                    b = C.bank()
                    for tt_ in range(t2, min(t2 + 2, ntt)):
                        for kc in range(8):
                            C.mm(C.ps[b][:, (tt_ - t2) * 256:(tt_ - t2 + 1) * 256], h[:, kc, tt_ * 128:(tt_ + 1) * 128],
                                 w_in_t[:, kc, 0:256], kc == 0, kc == 7, [hk, ("w_in", kc)], [f"ps{b}"])
                    nn = min(2, ntt - t2)
                    C.cp("act", ftok[:, t2:t2 + nn, :], C.ps[b][:, 0:nn * 256].rearrange("p (t c) -> p t c", c=256),
                         [f"ps{b}"], ["ftok"])
                if kind == "x":
                    for gp in range(2):
                        g0 = gp * TC + t0
                        C.dma("sp", f_own[g0 // FR].ap()[g0 % FR:g0 % FR + n, :].rearrange("(tt p) c -> p tt c", p=128),
                              ftok[:, 0:ntt, gp * 128:(gp + 1) * 128], ["ftok"], [("fo", g0 // FR, g0 % FR)])
                else:
                    C.dma("sp", fc.ap().rearrange("(tt p) c -> p tt c", p=128), ftok[:, 0:ntt, :], ["ftok"], ["fc"])
            if do_rest and 'q' not in SKIP:
                for jq in range(2):
                    b = C.bank()
                    for kc in range(8):
                        C.mm(C.ps[b][:, 0:n], w_in_t[:, kc, 256 + jq * 128:256 + (jq + 1) * 128], h[:, kc, 0:n],
                             kc == 0, kc == 7, [hk, ("w_in", kc)], [f"ps{b}"])
                    C.cp("dve", cq[:, jq, 0:n], C.ps[b][:, 0:n], [f"ps{b}"], ["cq"])
                    C.act(sqq[:, jq, 0:n], C.ps[b][:, 0:n], AF.Square, [f"ps{b}"], ["sqq"])
                b = C.bank()
                for jq in range(2):
                    C.mm(C.ps[b][:, 0:n], ones_bf[:, :], sqq[:, jq, 0:n], jq == 0, jq == 1, ["sqq", "ones_bf"], [f"ps{b}"])
                C.act(rs[:, 0:n], C.ps[b][:, 0:n], AF.Sqrt, [f"ps{b}", "eps"], ["rsA"], bias=eps_t[:, 0:1], scale=1.0 / 256)
                C.recip(rq[:, 0:n], rs[:, 0:n], ["rsA"], ["rq"])
                for jq in range(2):
                    C.stt("dve", cqn[:, jq, 0:n], cq[:, jq, 0:n], qgT[:, l, jq:jq + 1], rq[:, 0:n], ALU.mult, ALU.mult,
                          ["cq", "rq", "qgT"], ["cqn"])
                for hh in range(NH):
                    ba = C.bank()
                    for jq in range(2):
                        C.mm(C.ps[ba][0:96, 0:n], w_uq_t[:, jq, hh * 96:(hh + 1) * 96], cqn[:, jq, 0:n], jq == 0, jq == 1,
                             ["cqn", "w_uq"], [f"ps{ba}"])
                    C.cp("act", qst[0:64, hh, 0:n], C.ps[ba][0:64, 0:n], [f"ps{ba}"], [("qst", "n")])
                    if kind == "x":
                        bb = C.bank()
                        for jq in range(2):
                            C.mm(C.ps[bb][0:96, 0:n], w_uq_s[:, jq, hh * 96:(hh + 1) * 96], cqn[:, jq, 0:n], jq == 0, jq == 1,
                                 ["cqn", "w_uq_s"], [f"ps{bb}"])
                        C.tt("dve", rt1[64:96, 0:n], C.ps[ba][64:96, 0:n], rp1[64:96, 0:n], ALU.mult,
                             [f"ps{ba}", "rp1q"], ["rt1"])
                        C.tt("dve", rt2[64:96, 0:n], C.ps[bb][64:96, 0:n], rp2[64:96, 0:n], ALU.mult,
                             [f"ps{bb}", "rp2q"], ["rt2"])
                        C.tt("pool", qst[64:96, hh, 0:n], rt1[64:96, 0:n], rt2[64:96, 0:n], ALU.add,
                             ["rt1", "rt2"], [("qst", "r")])
                    else:
                        C.cp("act", qst[64:96, hh, 0:n], C.ps[ba][64:96, 0:n], [f"ps{ba}"], [("qst", "r")])
                C.dma("sp", kd["q"].ap().rearrange("h e n -> e h n")[:, :, t0:t0 + n], qst[0:96, :, 0:n],
                      [("qst", "n"), ("qst", "r")], [("q", kind)])
            if 'kv' in SKIP:
                continue
            b = C.bank()
            for kc in range(8):
                C.mm(C.ps[b][:, 0:n], w_in_t[:, kc, 512:640], h[:, kc, 0:n], kc == 0, kc == 7, [hk, ("w_in", kc)], [f"ps{b}"])
            C.cp("dve", ckv[:, 0:n], C.ps[b][:, 0:n], [f"ps{b}"], ["ckv"])
            C.act(sqq[:, 0, 0:n], C.ps[b][:, 0:n], AF.Square, [f"ps{b}"], ["sqq"])
            b = C.bank()
            C.mm(C.ps[b][:, 0:n], ones_bf[:, :], sqq[:, 0, 0:n], True, True, ["sqq", "ones_bf"], [f"ps{b}"])
            C.act(rs[:, 0:n], C.ps[b][:, 0:n], AF.Sqrt, [f"ps{b}", "eps"], ["rsA"], bias=eps_t[:, 0:1], scale=1.0 / 128)
            C.recip(rq[:, 0:n], rs[:, 0:n], ["rsA"], ["rq"])
            C.stt("dve", ckvn[:, 0:n], ckv[:, 0:n], kvgT[:, l, 0:1], rq[:, 0:n], ALU.mult, ALU.mult,
                  ["ckv", "rq", "kvgT"], ["ckvn"])
            kvdst = kvc.ap()[:, 0:n] if kind == "c" else kv_own[t0 // TCq].ap()[:, t0 % TCq:t0 % TCq + n]
            C.dma("sp", kvdst[0:128, :], ckvn[:, 0:n], ["ckvn"], [("kvo", kind, t0, 0)])
            if 'kr' in SKIP:
                continue
            ba = C.bank()
            for kc in range(8):
                C.mm(C.ps[ba][0:32, 0:n], w_in_t[:, kc, 640:672], h[:, kc, 0:n], kc == 0, kc == 7, [hk, ("w_in", kc)], [f"ps{ba}"])
            if kind == "x":
                bb = C.bank()
                for kc in range(8):
                    C.mm(C.ps[bb][0:32, 0:n], krsw[:, kc, :], h[:, kc, 0:n], kc == 0, kc == 7, [hk, "krsw0", "krsw1"], [f"ps{bb}"])
                C.tt("dve", rt1[0:32, 0:n], C.ps[ba][0:32, 0:n], rp1[0:32, 0:n], ALU.mult, [f"ps{ba}", "rp1k"], ["rt1"])
                C.tt("dve", rt2[0:32, 0:n], C.ps[bb][0:32, 0:n], rp2[0:32, 0:n], ALU.mult, [f"ps{bb}", "rp2k"], ["rt2"])
                C.tt("pool", krr[0:32, 0:n], rt1[0:32, 0:n], rt2[0:32, 0:n], ALU.add, ["rt1", "rt2"], ["krr"])
            else:
                C.cp("act", krr[0:32, 0:n], C.ps[ba][0:32, 0:n], [f"ps{ba}"], ["krr"])
            C.dma("sp", kvdst[128:160, :], krr[0:32, 0:n], ["krr"], [("kvo", kind, t0, 1)])
            if kind == "x" and 'ag' not in SKIP:
                if (t0 + n) % TCq == 0:
                    ci = t0 // TCq
                    rk = [("kvo", "x", tt0, pp) for tt0 in range(ci * TCq, (ci + 1) * TCq, 512) for pp in range(2)]
                    pend_ag.append((kv_own[ci].ap(), kv_all[ci].ap(), rk, []))
                if do_rest and 'f' not in SKIP:
                    for gp in range(2):
                        g1 = gp * TC + t0 + n
                        if g1 % FR == 0:
                            ci = g1 // FR - 1
                            rk = [("fo", ci, off) for off in range(0, FR, 512)]
                            defer_ag.append((f_own[ci].ap(), f_all[ci].ap(), rk, [("f_all", ci)]))

            if do_rest and 'glu' not in SKIP:
                for jg in range(2):
                    ba = C.bank()
                    bb = C.bank()
                    for kc in range(8):
                        C.mm(C.ps[ba][:, 0:n], w_in_t[:, kc, 672 + jg * 128:672 + (jg + 1) * 128], h[:, kc, 0:n],
                             kc == 0, kc == 7, [hk, ("w_in", kc)], [f"ps{ba}"])
                    for kc in range(8):
                        C.mm(C.ps[bb][:, 0:n], w_in_t[:, kc, 928 + jg * 128:928 + (jg + 1) * 128], h[:, kc, 0:n],
                             kc == 0, kc == 7, [hk, ("w_in", kc)], [f"ps{bb}"])
                    C.act(sg[:, 0:n], C.ps[bb][:, 0:n], AF.Sigmoid, [f"ps{bb}"], ["sg"])
                    C.tt("dve", ut[:, jg, 0:n], C.ps[ba][:, 0:n], sg[:, 0:n], ALU.mult, [f"ps{ba}", "sg"], ["ut"])
                C.dma("sp", kd["up"].ap().rearrange("j p n -> p j n")[:, :, 16 + t0:16 + t0 + n], ut[:, :, 0:n],
                      ["ut"], [("up", kind)])
        for ag in pend_ag:
            C.allgather(*ag)
        pend_ag = []
        upv = upx.ap().rearrange("j p n -> (j p) n")
        if 'edge' not in SKIP:
            C.dma("sp", e_own.ap()[:, 0:16], upv[:, 16:32], [("up", "x")], ["e_own"])
            C.dma("sp", e_own.ap()[:, 16:32], upv[:, TC:TC + 16], [("up", "x")], ["e_own"])
        if 'ag' not in SKIP:
            for ag in defer_ag:
                C.allgather(*ag, persist=True)
            C.allgather(e_own.ap(), e_all.ap(), ["e_own"], ["e_all"], persist=True)
        C.reset(mA)
        if dbg == "A":
            break
        V = SimpleNamespace(**locals())
        phase_attention(C, V)
        if dbg == "C":
            break
        phase_fourier(C, V)
        W2_OFF = SB_TOP - 65536
        W1_OFF = W2_OFF - 65536
        WO_OFF = W1_OFF - 16384
        V.wo = C.alloc_at("wo", [128, 8, D], BF16, WO_OFF)
        V.w1 = C.alloc_at("w1", [128, 8, DFF], BF16, W1_OFF)
        V.w2 = C.alloc_at("w2", [128, 32, D], BF16, W2_OFF)
        for kc in range(8):
            C.dma("pool", V.wo[:, kc, :], w_o.ap()[l, kc * 128:(kc + 1) * 128, :], [], [("wo", kc)], persist=True)
        for kc in range(8):
            C.dma("pool", V.w1[:, kc, :], w_mlp1.ap()[l, kc * 128:(kc + 1) * 128, :], [], [("w1", kc)], persist=True)
        for jc in range(32):
            C.dma("pool", V.w2[:, jc, :], w_mlp2.ap()[l, jc * 128:(jc + 1) * 128, :], [], [("w2", jc)], persist=True)
        C.limit = WO_OFF
        phase_conv(C, V)
        if dbg == "E":
            break
        phase_out(C, V)
        C.limit = W1_OFF
        phase_mlp(C, V)
        C.limit = SB_TOP
        C.P.persist_w = {}

    if dbg is None:
        phase_final(C, SimpleNamespace(**locals()))
    P.barrier()
    P.emit(nc)
    return nc


def phase_attention(C, V):
    l, last, TC, NB, NK, NKT = V.l, V.last, V.TC, V.NB, V.NK, V.NKT
    m = C.mark()
    wukv = V.wukv_t
    KVn = C.alloc("KVn", [128, NK], BF16)
    Kt = [C.alloc(f"Kt{i}", [128, NK], BF16) for i in range(2)]
    Vt = [C.alloc(f"Vt{i}", [128, NKT, 128], BF16) for i in range(2)]
    Qt = [C.alloc(f"Qt{i}", [128, TC], BF16) for i in range(2)]
    Qc = C.alloc("Qc", [128, CT], BF16)
    Pt = [C.alloc(f"Pt{i}", [128, 512], BF16) for i in range(4)]
    rinv = C.alloc("rinv", [128, 512], F32)
    aost = [C.alloc(f"aost{i}", [128, 512], BF16) for i in range(2)]
    TCq, NKV = V.TCq, V.NKV
    kvkeys = [("KVn", i) for i in range(1 + 4 * NKV)]
    C.dma("sp", KVn[:, 0:CT], V.kvc.ap()[0:128, :], [], [kvkeys[0]])
    for r in range(4):
        for cj in range(NKV):
            c0 = CT + r * TC + cj * TCq
            C.dma("sp", KVn[:, c0:c0 + TCq], V.kv_all[cj].ap()[r * 160:r * 160 + 128, :], [], [kvkeys[1 + r * NKV + cj]])
    krkeys = {}
    for i in range(2):
        krkeys[i] = [("Kr", i, k) for k in range(1 + 4 * NKV)]
        C.dma("sp", Kt[i][64:96, 0:CT], V.kvc.ap()[128:160, :], [], [krkeys[i][0]])
        for r in range(4):
            for cj in range(NKV):
                c0 = CT + r * TC + cj * TCq
                C.dma("sp", Kt[i][64:96, c0:c0 + TCq], V.kv_all[cj].ap()[r * 160 + 128:r * 160 + 160, :],
                      [], [krkeys[i][1 + r * NKV + cj]])
        C.memset("dve", Vt[i][:, :, 64:128], 1.0, [("Vone", i)])
    state = {"ob": 0, "ao": 0}

    def attn_block(hp, qap, qkey, n, nkt, dst):
        ob = 4 + state["ob"] % 2
        state["ob"] += 1
        LOOK = 2
        for i in range(nkt + LOOK):
            if i < nkt:
                sbk = i % 4
                C.mm(C.ps[sbk][:, 0:n], Kt[hp][0:96, i * 128:(i + 1) * 128], qap, True, True,
                     [("Kn", hp), qkey] + krkeys[hp], [f"ps{sbk}"])
                C.act(Pt[sbk][:, 0:n], C.ps[sbk][:, 0:n], AF.Exp, [f"ps{sbk}"], [f"Pt{sbk}"], scale=ATTN_SCALE)
            ii = i - LOOK
            if ii >= 0:
                C.mm(C.ps[ob][:, 0:n], Vt[hp][:, ii, :], Pt[ii % 4][:, 0:n], ii == 0, ii == nkt - 1,
                     [f"Pt{ii % 4}", ("Vv", hp), ("Vone", hp)], [f"ps{ob}"])
        a = state["ao"] % 2
        state["ao"] += 1
        C.recip(rinv[64:128, 0:n], C.ps[ob][64:128, 0:n], [f"ps{ob}"], ["rinv"])
        C.tt("dve", aost[a][0:64, 0:n], C.ps[ob][0:64, 0:n], rinv[64:128, 0:n], ALU.mult, [f"ps{ob}", "rinv"], [f"aost{a}"])
        C.dma("sp", dst, aost[a][0:64, 0:n], [f"aost{a}"], [])

    for h in range(NH):
        hp = h % 2
        k = 0
        for c0 in range(0, NK, 512):
            nb = min(512, NK - c0)
            b = C.bank(6, 8)
            C.mm(C.ps[b][0:64, 0:nb], wukv[:, h * 128:h * 128 + 64], KVn[:, c0:c0 + nb], True, True,
                 kvkeys + ["wukv"], [f"ps{b}"])
            C.cp("dve" if k % 2 else "act", Kt[hp][0:64, c0:c0 + nb], C.ps[b][0:64, 0:nb], [f"ps{b}"], [("Kn", hp)])
            k += 1
        for g in range(0, NKT, 8):
            ng = min(8, NKT - g)
            b = C.bank(6, 8)
            for kt in range(g, g + ng):
                C.mm(C.ps[b][:, (kt - g) * 64:(kt - g + 1) * 64], KVn[:, kt * 128:(kt + 1) * 128],
                     wukv[:, h * 128 + 64:h * 128 + 128], True, True, kvkeys + ["wukv"], [f"ps{b}"])
            C.cp("dve", Vt[hp][:, g:g + ng, 0:64], C.ps[b][:, 0:ng * 64].rearrange("p (t c) -> p t c", c=64),
                 [f"ps{b}"], [("Vv", hp)])
        C.dma("sp", Qt[hp][0:96, :], V.qx.ap()[h], [], [("Q", hp)])
        for qb in range(NB):
            dst = V.mixx.ap()[2 + h // 2, (h % 2) * 64:(h % 2) * 64 + 64, qb * 512:(qb + 1) * 512]
            attn_block(hp, Qt[hp][0:96, qb * 512:(qb + 1) * 512], ("Q", hp), 512, NKT, dst)
        if not last:
            C.dma("sp", Qc[0:96, :], V.qc.ap()[h], [], ["Qc"])
            dst = V.mixc.ap()[2 + h // 2, (h % 2) * 64:(h % 2) * 64 + 64, :]
            attn_block(hp, Qc[0:96, :], "Qc", CT, 2, dst)
    C.reset(m)


def phase_fourier(C, V):
    l, last, TC, NB, NT, TCt = V.l, V.last, V.TC, V.NB, V.NT, V.TCt
    m = C.mark()
    cs = C.alloc("cs", [128, 2 * NT], BF16)
    r1 = C.alloc("r1", [128, NT, 64], BF16)
    r2 = C.alloc("r2", [128, NT, 64], BF16)
    ccb = C.alloc("ccb", [128, 128], BF16); mscb = C.alloc("mscb", [128, 128], BF16)
    wf = C.alloc("wf", [128, 2, 256], BF16)
    Yt = C.alloc("Yt", [128, 2, TC], BF16)
    Z = C.alloc("Z", [128, 128, 128], BF16)
    T = C.alloc("T", [128, 128, 2 * NT], BF16)
    Ure = C.alloc("Ure", [128, TC], BF16)
    Vv = C.alloc("Vv", [128, TC], BF16)
    yst = [C.alloc(f"yst{i}", [128, 2, 512], BF16) for i in range(2)]
    C.dma("sp", cs[0:NT, :], V.fft_cs.ap(), [], ["cs"])
    C.dma("sp", r1[:, :, :], V.fft_r1.ap().rearrange("p (k c) -> p k c", c=64), [], ["r1"])
    C.dma("sp", r2[:, :, :], V.fft_r2.ap().rearrange("p (k c) -> p k c", c=64), [], ["r2"])
    C.dma("sp", ccb[:, :], V.ccb_d.ap(), [], ["ccb"])
    C.dma("sp", mscb[:, :], V.mscb_d.ap(), [], ["mscb"])
    C.dma("pool", wf[:, :, :], V.w_fourier.ap()[l].rearrange("(j p) n -> p j n", p=128), [], ["wf"])
    cnt = 0
    for gp in range(2):
        FR, NF = V.FR, V.NF
        zkeys = []
        for r in range(4):
            rows_per = min(FR, TC)
            for cj in range(TC // rows_per):
                g0 = gp * TC + cj * rows_per
                ch, off = g0 // FR, g0 % FR
                p0 = r * TCt + cj * (rows_per // 128)
                zk = ("Z", r, cj)
                zkeys.append(zk)
                C.dma("sp", Z[p0:p0 + rows_per // 128, :, :],
                      V.f_all[ch].ap()[r * FR + off:r * FR + off + rows_per, :].rearrange("(a b) c -> a b c", b=128),
                      [("f_all", ch)], [zk])
        cpb = 512 // (2 * NT)
        cpb = min(cpb, 128)
        for c0 in range(0, 128, cpb):
            b = C.bank()
            for c in range(c0, c0 + cpb):
                C.mm(C.ps[b][:, (c - c0) * 2 * NT:(c - c0 + 1) * 2 * NT], Z[0:NT, :, c], cs[0:NT, :], True, True,
                     zkeys + ["cs"], [f"ps{b}"])
            C.cp("act" if cnt % 2 else "dve", T[:, c0:c0 + cpb, :],
                 C.ps[b][:, 0:cpb * 2 * NT].rearrange("p (c k) -> p c k", k=2 * NT), [f"ps{b}"], ["T"])
            cnt += 1
        for k0 in range(0, NT, 8):
            b = C.bank()
            for k1 in range(k0, k0 + 8):
                o = C.ps[b][:, (k1 - k0) * 64:(k1 - k0 + 1) * 64]
                C.mm(o, T[:, :, k1], r1[:, k1, :], True, False, ["T", "r1"], [f"ps{b}"])
                C.mm(o, T[:, :, NT + k1], r2[:, k1, :], False, True, ["T", "r2"], [f"ps{b}"])
            psv = C.ps[b][:, :].rearrange("p (k1 two k2) -> p two k2 k1", k1=8, two=2, k2=32)
            C.cp("dve", Ure[:, :].rearrange("p (k2 k1) -> p k2 k1", k1=NT)[:, :, k0:k0 + 8], psv[:, 0], [f"ps{b}"], ["Ure"])
            C.cp("act", Vv[:, :].rearrange("p (k2 k1) -> p k2 k1", k1=NT)[:, :, k0:k0 + 8], psv[:, 1], [f"ps{b}"], ["Vv"])
        for tb in range(NB):
            b = C.bank()
            C.mm(C.ps[b][:, :], ccb[:, :], Ure[:, tb * 512:(tb + 1) * 512], True, False, ["Ure", "ccb"], [f"ps{b}"])
            C.mm(C.ps[b][:, :], mscb[:, :], Vv[:, tb * 512:(tb + 1) * 512], False, True, ["Vv", "mscb"], [f"ps{b}"])
            C.cp("act" if tb % 2 else "dve", Yt[:, gp, tb * 512:(tb + 1) * 512], C.ps[b][:, :], [f"ps{b}"], [("Yt", gp)])
    mixv = V.mixx.ap().rearrange("kc p n -> p kc n")
    for tb in range(NB):
        y = yst[tb % 2]
        for oc in range(2):
            b = C.bank()
            for gp in range(2):
                C.mm(C.ps[b][:, :], wf[:, gp, oc * 128:(oc + 1) * 128], Yt[:, gp, tb * 512:(tb + 1) * 512], gp == 0, gp == 1,
                     [("Yt", 0), ("Yt", 1), "wf"], [f"ps{b}"])
            C.cp("act" if oc else "dve", y[:, oc, :], C.ps[b][:, :], [f"ps{b}"], [f"yst{tb % 2}"])
        C.dma("sp", mixv[:, 0:2, tb * 512:(tb + 1) * 512], y[:, :, :], [f"yst{tb % 2}"], [])
    if not last:
        Zc = C.alloc("Zc", [128, 2, 256], BF16)
        w256 = C.alloc("w256", [128, 2, 512], BF16)
        ccbc = C.alloc("ccbc", [128, 128], BF16); mscbc = C.alloc("mscbc", [128, 128], BF16)
        UVc = C.alloc("UVc", [128, 512], BF16)
        Ytc = C.alloc("Ytc", [128, 2, 256], BF16)
        ystc = C.alloc("ystc", [128, 2, 256], BF16)
        C.dma("sp", Zc[:, :, :], V.fc.ap().rearrange("(tt p) c -> p tt c", p=128), [], ["Zc"])
        C.dma("sp", w256[:, :, :], V.w256_d.ap().rearrange("(tt p) k -> p tt k", p=128), [], ["w256"])
        C.dma("sp", ccbc[:, :], V.ccbc_d.ap(), [], ["ccbc"])
        C.dma("sp", mscbc[:, :], V.mscbc_d.ap(), [], ["mscbc"])
        for cp_ in range(2):
            b = C.bank()
            for nt in range(2):
                C.mm(C.ps[b][:, :], Zc[:, nt, cp_ * 128:(cp_ + 1) * 128], w256[:, nt, :], nt == 0, nt == 1,
                     ["Zc", "w256"], [f"ps{b}"])
            C.cp("dve", UVc[:, :], C.ps[b][:, :], [f"ps{b}"], ["UVc"])
            b = C.bank()
            C.mm(C.ps[b][:, 0:256], ccbc[:, :], UVc[:, 0:256], True, False, ["UVc", "ccbc"], [f"ps{b}"])
            C.mm(C.ps[b][:, 0:256], mscbc[:, :], UVc[:, 256:512], False, True, ["UVc", "mscbc"], [f"ps{b}"])
            C.cp("act", Ytc[:, cp_, :], C.ps[b][:, 0:256], [f"ps{b}"], [("Ytc", cp_)])
        for oc in range(2):
            b = C.bank()
            for gp in range(2):
                C.mm(C.ps[b][:, 0:256], wf[:, gp, oc * 128:(oc + 1) * 128], Ytc[:, gp, :], gp == 0, gp == 1,
                     [("Ytc", 0), ("Ytc", 1), "wf"], [f"ps{b}"])
            C.cp("act" if oc else "dve", ystc[:, oc, :], C.ps[b][:, 0:256], [f"ps{b}"], ["ystc"])
        C.dma("sp", V.mixc.ap().rearrange("kc p n -> p kc n")[:, 0:2, :], ystc[:, :, :], ["ystc"], [])
    C.reset(m)


def phase_conv(C, V):
    l, last, TC = V.l, V.last, V.TC
    modT, wdwT, bdwT, lngT, lnbT = V.modT, V.wdwT, V.bdwT, V.lngT, V.lnbT
    m = C.mark()
    wpw = C.alloc("wpw", [128, 2, 256], BF16)
    C.dma("pool", wpw[:, :, :], V.w_pw2.ap()[l].rearrange("(j p) n -> p j n", p=128), [], ["wpw"])
    E = C.alloc("E", [128, 4, 2, 32], BF16)
    hl = C.alloc("hl", [128, 2, 16], F32); hr = C.alloc("hr", [128, 2, 16], F32)
    hb = C.alloc("hb", [128, 2, 32], BF16)
    C.dma("sp", E[:, :, :, :], V.e_all.ap().rearrange("(r j p) n -> p r j n", r=4, j=2), ["e_all"], ["E"])
    masks = V.masks
    C.ts("dve", hl[:, :, :], E[:, 0, :, 16:32], masks[:, 0:1], None, ALU.mult, None, ["E", "masks"], ["hl"])
    C.ts("dve", hr[:, :, :], E[:, 0, :, 0:16], masks[:, 4:5], None, ALU.mult, None, ["E", "masks"], ["hr"])
    for r in range(1, 4):
        C.stt("dve", hl[:, :, :], E[:, r, :, 16:32], masks[:, r:r + 1], hl[:, :, :], ALU.mult, ALU.add, ["E", "masks", "hl"], ["hl"])
        C.stt("dve", hr[:, :, :], E[:, r, :, 0:16], masks[:, 4 + r:5 + r], hr[:, :, :], ALU.mult, ALU.add, ["E", "masks", "hr"], ["hr"])
    C.cp("dve", hb[:, :, 0:16], hl[:, :, :], ["hl"], ["hb"])
    C.cp("dve", hb[:, :, 16:32], hr[:, :, :], ["hr"], ["hb"])
    upv = V.upx.ap().rearrange("j p n -> p j n")
    C.dma("sp", upv[:, :, 0:16], hb[:, :, 0:16], ["hb"], ["uph"])
    C.dma("sp", upv[:, :, TC + 16:TC + 32], hb[:, :, 16:32], ["hb"], ["uph"])

    U = [C.alloc(f"U{i}", [128, 2, 544], BF16) for i in range(2)]
    cv = C.alloc("cv", [128, 2, 512], F32); xc = C.alloc("xc", [128, 2, 512], F32)
    sqc = C.alloc("sqc", [128, 2, 512], F32)
    mean = C.alloc("mean", [128, 512], F32); rsc = C.alloc("rsc", [128, 512], F32)
    rstdc = C.alloc("rstdc", [128, 512], F32)
    tmpc = [C.alloc(f"tmpc{i}", [128, 512], F32) for i in range(2)]
    sl = C.alloc("sl", [128, 2, 512], BF16)
    ycst = [C.alloc(f"ycst{i}", [128, 2, 512], BF16) for i in range(2)]
    Dg = C.alloc("Dg", [128, 62, 128], BF16)
    for kj in range(62):
        C.ts("dve", Dg[:, kj, :], V.ident[:, :], wdwT[:, l, kj:kj + 1], None, ALU.mult, None,
             ["ident", "wdwT"], [("Dg", kj % 2)])
    bl = V.blocks(with_ctx=not last)

    def loadU(bi):
        kind, t0, n = bl[bi]
        up = V.kinds[kind]["up"].ap().rearrange("j p n -> p j n")
        C.dma("sp", U[bi % 2][:, :, 0:n + 32], up[:, :, t0:t0 + n + 32], ["uph"], [f"U{bi % 2}"])
    loadU(0)
    for bi, (kind, t0, n) in enumerate(bl):
        pb = bi % 2
        kd = V.kinds[kind]
        if bi + 1 < len(bl):
            loadU(bi + 1)
        Ub = U[pb]
        uk = f"U{pb}"
        for j in range(2):
            b = C.bank()
            for k in range(31):
                C.mm(C.ps[b][:, 0:n], Dg[:, k * 2 + j, :], Ub[:, j, k + 1:k + 1 + n], k == 0, k == 30,
                     [uk, ("Dg", 0), ("Dg", 1)], [f"ps{b}"])
            C.act(cv[:, j, 0:n], C.ps[b][:, 0:n], AF.Identity, [f"ps{b}", "bdwT"], [("cv", j)], bias=bdwT[:, l, j:j + 1])
        b = C.bank()
        for j in range(2):
            C.mm(C.ps[b][:, 0:n], V.ones_f[:, :], cv[:, j, 0:n], j == 0, j == 1, [("cv", j), "ones_f"], [f"ps{b}"])
        C.act(mean[:, 0:n], C.ps[b][:, 0:n], AF.Identity, [f"ps{b}"], ["mean"], scale=1.0 / 256)
        for j in range(2):
            C.tt("dve", xc[:, j, 0:n], cv[:, j, 0:n], mean[:, 0:n], ALU.subtract, [("cv", j), "mean"], ["xc"])
        C.act(sqc[:, :, 0:n], xc[:, :, 0:n], AF.Square, ["xc"], ["sqc"])
        b = C.bank()
        for j in range(2):
            C.mm(C.ps[b][:, 0:n], V.ones_f[:, :], sqc[:, j, 0:n], j == 0, j == 1, ["sqc", "ones_f"], [f"ps{b}"])
        C.act(rsc[:, 0:n], C.ps[b][:, 0:n], AF.Sqrt, [f"ps{b}", "eps"], ["rsc"], bias=V.eps_t[:, 0:1], scale=1.0 / 256)
        C.recip(rstdc[:, 0:n], rsc[:, 0:n], ["rsc"], ["rstdc"])
        for j in range(2):
            C.tt("dve", tmpc[j][:, 0:n], xc[:, j, 0:n], rstdc[:, 0:n], ALU.mult, ["xc", "rstdc"], [f"tmpc{j}"])
            C.act(sl[:, j, 0:n], tmpc[j][:, 0:n], AF.Silu, [f"tmpc{j}", "lngT", "lnbT"], ["sl"],
                  bias=lnbT[:, l, j:j + 1], scale=lngT[:, l, j:j + 1])
        for oc in range(2):
            b = C.bank()
            for j in range(2):
                C.mm(C.ps[b][:, 0:n], wpw[:, j, oc * 128:(oc + 1) * 128], sl[:, j, 0:n], j == 0, j == 1, ["sl", "wpw"], [f"ps{b}"])
            C.cp("act" if oc else "dve", ycst[pb][:, oc, 0:n], C.ps[b][:, 0:n], [f"ps{b}"], [f"ycst{pb}"])
        C.dma("sp", kd["mix"].ap().rearrange("kc p n -> p kc n")[:, 6:8, t0:t0 + n], ycst[pb][:, :, 0:n], [f"ycst{pb}"], [])
    C.reset(m)


def _norm_mod(C, V, xbt, xkey, n, l, j, gm, sh_off, sq, rs, rstd, tmp, hout, hkey):
    C.act(sq[:, :, 0:n], xbt[:, :, 0:n], AF.Square, [xkey], ["sqN"])
    b = C.bank()
    for kc in range(8):
        C.mm(C.ps[b][:, 0:n], V.ones_bf[:, :], sq[:, kc, 0:n], kc == 0, kc == 7, ["sqN", "ones_bf"], [f"ps{b}"])
    C.act(rs[:, 0:n], C.ps[b][:, 0:n], AF.Sqrt, [f"ps{b}", "eps"], ["rsN"], bias=V.eps_t[:, 0:1], scale=1.0 / D)
    C.recip(rstd[:, 0:n], rs[:, 0:n], ["rsN"], ["rstdN"])
    for kc in range(8):
        C.tt("dve", tmp[kc % 2][:, 0:n], xbt[:, kc, 0:n], rstd[:, 0:n], ALU.mult, [xkey, "rstdN"], [f"tmpN{kc % 2}"])
        C.act(hout[:, kc, 0:n], tmp[kc % 2][:, 0:n], AF.Identity, [f"tmpN{kc % 2}", "gm", "modT"], [hkey],
              bias=V.modT[:, l, j, sh_off + kc:sh_off + kc + 1], scale=gm[:, l, j, kc:kc + 1])


def phase_out(C, V):
    l, last = V.l, V.last
    m = C.mark()
    wo = V.wo
    M0 = C.alloc("M0", [128, 8, 512], BF16)
    M = [M0, M0]
    xb0 = C.alloc("xbO0", [128, 8, 512], F32)
    xb = [xb0, xb0]
    sq = C.alloc("sqO", [128, 8, 512], BF16)
    rs = C.alloc("rsO", [128, 512], F32); rstd = C.alloc("rstdO", [128, 512], F32)
    tmp = [C.alloc(f"tmpO{i}", [128, 512], F32) for i in range(2)]
    h20 = C.alloc("h2O0", [128, 8, 512], BF16)
    h2 = [h20, h20]
    bl = V.blocks(with_ctx=not last)

    def load(bi):
        kind, t0, n = bl[bi]
        kd = V.kinds[kind]
        C.dma("sp", M[bi % 2][:, :, 0:n], V.tview(kd["mix"], t0, n), [], ["M0"])
    for bi, (kind, t0, n) in enumerate(bl):
        pb = bi % 2
        kd = V.kinds[kind]
        j = kd["j"]
        load(bi)
        C.dma("sp", xb[pb][:, :, 0:n], V.tview(kd["xT"], t0, n), [("xT", kind, t0)], ["xbO0"])
        for oc in range(8):
            b = C.bank()
            for kc in range(8):
                C.mm(C.ps[b][:, 0:n], wo[:, kc, oc * 128:(oc + 1) * 128], M[pb][:, kc, 0:n], kc == 0, kc == 7,
                     ["M0", ("wo", kc)], [f"ps{b}"])
            C.stt("dve", xb[pb][:, oc, 0:n], C.ps[b][:, 0:n], V.modT[:, l, j, 16 + oc:17 + oc], xb[pb][:, oc, 0:n],
                  ALU.mult, ALU.add, [f"ps{b}", "modT", "xbO0"], ["xbO0"])
        C.dma("sp", V.tview(kd["xT"], t0, n), xb[pb][:, :, 0:n], ["xbO0"], [("xT", kind, t0)])
        _norm_mod(C, V, xb[pb], "xbO0", n, l, j, V.gm2, 24, sq, rs, rstd, tmp, h2[pb], "h2O0")
        C.dma("sp", V.tview(kd["h2"], t0, n), h2[pb][:, :, 0:n], ["h2O0"], [])
    C.reset(m)


def phase_mlp(C, V):
    l, last = V.l, V.last
    m = C.mark()
    w1 = V.w1
    w2 = V.w2
    h2 = [C.alloc(f"h2M{i}", [128, 8, 512], BF16) for i in range(2)]
    x1 = C.alloc("x1M", [128, 8, 512], F32)
    hid = C.alloc("hid", [128, 32, 512], BF16)
    rt = [C.alloc(f"rt{i}", [128, 512], F32) for i in range(2)]
    bl = V.blocks(with_ctx=not last)

    def load(bi):
        kind, t0, n = bl[bi]
        C.dma("sp", h2[bi % 2][:, :, 0:n], V.tview(V.kinds[kind]["h2"], t0, n), [], [f"h2M{bi % 2}"])
    load(0)
    for bi, (kind, t0, n) in enumerate(bl):
        pb = bi % 2
        kd = V.kinds[kind]
        j = kd["j"]
        if bi + 1 < len(bl):
            load(bi + 1)
        C.dma("sp", x1[:, :, 0:n], V.tview(kd["xT"], t0, n), [], ["x1M"])
        for jc in range(32):
            b = C.bank()
            for kc in range(8):
                C.mm(C.ps[b][:, 0:n], w1[:, kc, jc * 128:(jc + 1) * 128], h2[pb][:, kc, 0:n], kc == 0, kc == 7,
                     [f"h2M{pb}", ("w1", kc)], [f"ps{b}"])
            C.act(rt[jc % 2][:, 0:n], C.ps[b][:, 0:n], AF.Relu, [f"ps{b}"], [f"rt{jc % 2}"])
            C.tt("pool" if jc % 2 else "dve", hid[:, jc, 0:n], rt[jc % 2][:, 0:n], rt[jc % 2][:, 0:n], ALU.mult,
                 [f"rt{jc % 2}"], [("hid", jc % 2)])
        for oc in range(8):
            b = C.bank()
            for jc in range(32):
                C.mm(C.ps[b][:, 0:n], w2[:, jc, oc * 128:(oc + 1) * 128], hid[:, jc, 0:n], jc == 0, jc == 31,
                     [("hid", 0), ("hid", 1), ("w2", jc)], [f"ps{b}"])
            C.stt("dve", x1[:, oc, 0:n], C.ps[b][:, 0:n], V.modT[:, l, j, 40 + oc:41 + oc], x1[:, oc, 0:n],
                  ALU.mult, ALU.add, [f"ps{b}", "modT", "x1M"], ["x1M"])
        C.dma("sp", V.tview(kd["xT"], t0, n), x1[:, :, 0:n], ["x1M"], [])
    C.reset(m)


def phase_final(C, V):
    m = C.mark()
    xb = [C.alloc(f"xbF{i}", [128, 8, 512], F32) for i in range(2)]
    sq = C.alloc("sqF", [128, 8, 512], BF16)
    rs = C.alloc("rsF", [128, 512], F32); rstd = C.alloc("rstdF", [128, 512], F32)
    y = C.alloc("yF", [128, 8, 512], F32)
    otok = [C.alloc(f"otok{i}", [128, 4, D], F32) for i in range(2)]
    bl = V.blocks(with_ctx=False)

    def load(bi):
        kind, t0, n = bl[bi]
        C.dma("sp", xb[bi % 2][:, :, 0:n], V.tview(V.xT, t0, n), [], [f"xbF{bi % 2}"])
    load(0)
    cnt = 0
    for bi, (kind, t0, n) in enumerate(bl):
        pb = bi % 2
        if bi + 1 < len(bl):
            load(bi + 1)
        C.act(sq[:, :, 0:n], xb[pb][:, :, 0:n], AF.Square, [f"xbF{pb}"], ["sqF"])
        b = C.bank()
        for kc in range(8):
            C.mm(C.ps[b][:, 0:n], V.ones_bf[:, :], sq[:, kc, 0:n], kc == 0, kc == 7, ["sqF", "ones_bf"], [f"ps{b}"])
        C.act(rs[:, 0:n], C.ps[b][:, 0:n], AF.Sqrt, [f"ps{b}", "eps"], ["rsF"], bias=V.eps_t[:, 0:1], scale=1.0 / D)
        C.recip(rstd[:, 0:n], rs[:, 0:n], ["rsF"], ["rstdF"])
        for kc in range(8):
            C.stt("dve", y[:, kc, 0:n], xb[pb][:, kc, 0:n], V.fngT[:, kc:kc + 1], rstd[:, 0:n], ALU.mult, ALU.mult,
                  [f"xbF{pb}", "rstdF", "fngT"], ["yF"])
        for tt_ in range(n // 128):
            for kc2 in range(0, 8, 4):
                b = C.bank()
                for kc in range(kc2, kc2 + 4):
                    C.tr(C.ps[b][:, (kc - kc2) * 128:(kc - kc2 + 1) * 128], y[:, kc, tt_ * 128:(tt_ + 1) * 128], V.ident[:, :],
                         ["yF", "ident"], [f"ps{b}"])
                C.cp("act" if cnt % 2 else "dve", otok[pb][:, tt_, kc2 * 128:kc2 * 128 + 512], C.ps[b][:, :], [f"ps{b}"], [f"otok{pb}"])
                cnt += 1
        C.dma("sp", V.out_d.ap()[t0:t0 + n, :].rearrange("(tt p) d -> p tt d", p=128), otok[pb][:, 0:n // 128, :], [f"otok{pb}"], [])
    C.reset(m)


_BF = ml_dtypes.bfloat16


def _tables(NT, r):
    S = 128 * NT
    TC = S // 4
    N = S
    tb = {}
    t = np.arange(S)
    row = (t // 64).astype(np.float32)
    col = (t % 64).astype(np.float32)
    inv = (10000.0 ** (-np.arange(0, 16, 2, dtype=np.float32) / 16)).astype(np.float32)
    ang = np.concatenate([row[:, None] * inv, col[:, None] * inv], axis=-1).astype(np.float32)
    cos = np.cos(ang).astype(np.float32)[r * TC:(r + 1) * TC].T
    sin = np.sin(ang).astype(np.float32)[r * TC:(r + 1) * TC].T
    tb["rope1"] = np.ascontiguousarray(np.concatenate([cos, cos], 0))
    tb["rope2"] = np.ascontiguousarray(np.concatenate([-sin, sin], 0))
    n1 = np.arange(NT)[:, None]; k1 = np.arange(NT)[None, :]
    a = 2 * np.pi * ((n1 * k1) % NT) / NT
    tb["fft_cs"] = np.concatenate([np.cos(a), np.sin(a)], 1).astype(_BF)
    n2 = np.arange(128)[:, None, None]
    k1 = np.arange(NT)[None, :, None]
    k2 = np.arange(32)[None, None, :]
    k = k1 + NT * (32 * r + k2)
    a = 2 * np.pi * ((n2 * k) % N) / N
    wc, ws = np.cos(a), np.sin(a)
    tb["fft_r1"] = np.concatenate([wc, ws], 2).reshape(128, NT * 64).astype(_BF)
    tb["fft_r2"] = np.concatenate([-ws, wc], 2).reshape(128, NT * 64).astype(_BF)
    c = np.arange(128)[:, None]; mm_ = np.arange(128)[None, :]
    same = (c // 64) == (mm_ // 64)
    a = 2 * np.pi * (((c % 64) * (mm_ % 64)) % 64) / 64
    for nm, nn in (("", N), ("c", CT)):
        sc = 1.0 / np.sqrt(nn * 64.0)
        tb["ccb" + nm] = (np.where(same, np.cos(a), 0.0) * sc).astype(_BF)
        tb["mscb" + nm] = (np.where(same, -np.sin(a), 0.0) * sc).astype(_BF)
    n = np.arange(CT)[:, None]; kk = np.arange(CT)[None, :]
    a = 2 * np.pi * ((n * kk) % CT) / CT
    tb["w256"] = np.concatenate([np.cos(a), np.sin(a)], 1).astype(_BF)
    tb["ident"] = np.eye(128, dtype=np.float32)
    mk = np.zeros((128, 8), np.float32)
    if r > 0:
        mk[:, r - 1] = 1.0
    if r < 3:
        mk[:, 4 + r + 1] = 1.0
    tb["masks"] = mk
    return tb


_CACHE = {}


def run_model(inputs, NT, L, dbg=None, dump=None):
    key = (NT, L, dbg, tuple(sorted(dump)) if dump else None)
    if key not in _CACHE:
        _CACHE[key] = build_program(NT, L, dbg=dbg, dump=dump)
    nc = _CACHE[key]
    S = 128 * NT
    TC = S // 4
    f32 = lambda a: np.ascontiguousarray(np.asarray(a, dtype=np.float32))
    wnames = ["w_mod", "b_mod", "norm1_g", "w_in", "q_norm_g", "w_uq", "kv_norm_g", "w_ukv", "w_fourier", "w_dw",
              "b_dw", "conv_ln_g", "conv_ln_b", "w_pw2", "w_o", "norm2_g", "w_mlp1", "w_mlp2"]
    shared = {n: f32(inputs[n])[:L] for n in wnames if n != "w_mod"}
    wmod_full = f32(inputs["w_mod"])[:L]
    wmod_sh = [np.ascontiguousarray(wmod_full[:, :, r * 1536:(r + 1) * 1536]) for r in range(4)]
    shared["final_norm_g"] = f32(inputs["final_norm_g"])
    x = f32(inputs["x"]); ctx = f32(inputs["ctx"]); c = f32(inputs["c"]); cc = f32(inputs["c_ctx"])
    in_maps = []
    for core in range(8):
        b, r = core // 4, core % 4
        m = dict(shared)
        m["x_in"] = np.ascontiguousarray(x[b, r * TC:(r + 1) * TC])
        m["ctx_in"] = np.ascontiguousarray(ctx[b])
        m["cvec"] = np.ascontiguousarray(np.stack([c[b], cc], 0))
        m["w_mod"] = wmod_sh[r]
        m.update(_tables(NT, r))
        in_maps.append(m)
    res = run_bass_kernel_spmd(nc, in_maps, core_ids=list(range(8)))
    return res


def kernel(**inputs):
    res = run_model(inputs, 128, 4)
    S = 128 * 128
    TC = S // 4
    out = np.empty((2, S, D), np.float32)
    for core in range(8):
        b, r = core // 4, core % 4
        out[b, r * TC:(r + 1) * TC] = np.asarray(res.results[core]["out"], dtype=np.float32)
    return out
```

```python
import contextlib
import os
SKIP = set(os.environ.get('KSKIP', '').split(','))
from types import SimpleNamespace
import numpy as np
import ml_dtypes
import concourse.bass as bass
import concourse.mybir as mybir
from concourse.bass_utils import run_bass_kernel_spmd

F32 = mybir.dt.float32
BF16 = mybir.dt.bfloat16
AF = mybir.ActivationFunctionType
ALU = mybir.AluOpType

ENGS = ("pe", "act", "dve", "pool", "sp")
EPOCH = 20000
NDMASEM = 12
NCCSEM = 3


class Op:
    __slots__ = ("eng", "fn", "deps", "signal", "dma", "sem", "val", "idx", "dslot", "inc")

    def __init__(self, eng, fn, dma):
        self.eng = eng
        self.fn = fn
        self.deps = []
        self.signal = False
        self.dma = dma
        self.sem = None
        self.val = 0
        self.idx = 0
        self.dslot = None
        self.inc = 16


class Prog:
    def __init__(self):
        self.ops = {e: [] for e in ENGS}
        self.last_w = {}
        self.readers = {}
        self.pending_dma = []
        self.all_last = {e: None for e in ENGS}
        self.persist_w = {}

    def op(self, eng, fn, reads=(), writes=(), dma=False, extra_deps=(), inc=16, persist=False):
        o = Op(eng, fn, dma)
        o.inc = inc
        pr = [k for k in reads if isinstance(k, str) and k.startswith("ps") and k[2:].isdigit()]
        if pr:
            writes = list(writes) + pr
        o.idx = len(self.ops[eng])
        deps = []
        for k in reads:
            w = self.last_w.get(k)
            if w is not None:
                deps.append((w, "raw"))
        for k in writes:
            w = self.last_w.get(k)
            if w is not None:
                deps.append((w, "waw"))
            for r in self.readers.get(k, ()):
                deps.append((r, "war"))
        for d in extra_deps:
            if d is not None:
                deps.append((d, "raw"))
        seen = set()
        for d, kind in deps:
            if d is o or id(d) in seen:
                continue
            need = True
            if d.eng == eng and not d.dma and not dma:
                if eng == "pe":
                    need = False
            if need:
                seen.add(id(d))
                o.deps.append(d)
                d.signal = True
        for k in reads:
            self.readers.setdefault(k, []).append(o)
        for k in writes:
            self.last_w[k] = o
            self.readers[k] = []
        self.ops[eng].append(o)
        if persist:
            for k in writes:
                self.persist_w[k] = o
        else:
            self.all_last[eng] = o
            if dma:
                self.pending_dma.append(o)
        return o

    def barrier(self):
        lasts = [self.all_last[e] for e in ENGS if self.all_last[e] is not None]
        pend = list(self.pending_dma)
        self.pending_dma = []
        for e in ENGS:
            self.op(e, None, extra_deps=lasts + pend)
        self.last_w = dict(self.persist_w)
        self.readers = {}

    def emit(self, nc):
        n_sems_needed = 0
        plan = {}
        for e in ENGS:
            cnt = 0
            epoch = 0
            for o in self.ops[e]:
                if o.dma:
                    continue
                if o.signal:
                    if cnt >= EPOCH:
                        epoch += 1
                        cnt = 0
                    cnt += 1
                    o.sem = (e, epoch)
                    o.val = cnt
            plan[e] = epoch + 1
        for e in ENGS:
            k = 0
            kc_ = 0
            for o in self.ops[e]:
                if o.dma:
                    if o.inc == 16:
                        o.dslot = k % NDMASEM
                        k += 1
                    else:
                        o.dslot = NDMASEM + (kc_ % NCCSEM)
                        kc_ += 1
        with contextlib.ExitStack() as st:
            sems = {}
            for e in ENGS:
                for ep in range(plan[e]):
                    sems[(e, ep)] = st.enter_context(nc.semaphore(f"s_{e}_{ep}"))
            dsems = {}
            for e in ENGS:
                if any(o.dma for o in self.ops[e]):
                    for k in range(NDMASEM + NCCSEM):
                        dsems[(e, k)] = st.enter_context(nc.semaphore(f"d_{e}_{k}"))
            block = st.enter_context(nc.Block())
            engmap = {"pe": block.tensor, "act": block.scalar, "dve": block.vector,
                      "pool": block.gpsimd, "sp": block.sync}

            def make(e):
                def body(eng):
                    waited = {}
                    dcount = {k: 0 for k in range(NDMASEM + NCCSEM)}
                    dlast = {}
                    for o in self.ops[e]:
                        for d in o.deps:
                            if d.dma:
                                key = ("d", d.eng, d.dslot)
                                s = dsems[(d.eng, d.dslot)]
                                v = d.val
                            else:
                                key = d.sem
                                s = sems[d.sem]
                                v = d.val
                            if waited.get(key, 0) >= v:
                                continue
                            waited[key] = v
                            eng.wait_ge(s, v)
                        if o.dma:
                            key = ("d", e, o.dslot)
                            prev = dcount[o.dslot]
                            if prev > 0 and waited.get(key, 0) < prev:
                                eng.wait_ge(dsems[(e, o.dslot)], prev)
                                waited[key] = prev
                            ins = o.fn(eng)
                            dcount[o.dslot] = prev + o.inc
                            o.val = prev + o.inc
                            ins.then_inc(dsems[(e, o.dslot)], o.inc)
                        else:
                            if o.fn is None:
                                if o.signal:
                                    eng.nop().then_inc(sems[o.sem], 1)
                                continue
                            ins = o.fn(eng)
                            if o.signal:
                                ins.then_inc(sems[o.sem], 1)
                return body

            for e in ENGS:
                dc = {k: 0 for k in range(NDMASEM + NCCSEM)}
                for o in self.ops[e]:
                    if o.dma:
                        dc[o.dslot] += o.inc
                        o.val = dc[o.dslot]
            for e in ENGS:
                if self.ops[e]:
                    engmap[e](make(e))


D = 1024
KC = 8
DIN = 1184
DFF = 4096
NH = 8
CT = 256
EPS = 1e-6
ATTN_SCALE = 96.0 ** -0.5
SB_BASE = 16512
SB_TOP = 229376 - 64


def _dtsize(dt):
    return 4 if dt == F32 else 2


class Ctx:
    def __init__(self, nc):
        self.nc = nc
        self.P = Prog()
        self.off = SB_BASE
        self.limit = SB_TOP
        self.uid = 0
        self.ps = [nc.alloc_psum_tensor(f"ps{i}", [128, 512], F32) for i in range(8)]
        self.psi = 0

    def alloc(self, name, shape, dt):
        n = 1
        for s in shape[1:]:
            n *= s
        nbytes = n * _dtsize(dt)
        off = (self.off + 63) // 64 * 64
        assert off + nbytes <= self.limit, (name, off, nbytes, self.limit)
        self.off = off + nbytes
        self.uid += 1
        return self.nc.alloc_sbuf_tensor_at(f"{name}_{self.uid}", list(shape), dt, offset=off)

    def alloc_at(self, name, shape, dt, off):
        self.uid += 1
        return self.nc.alloc_sbuf_tensor_at(f"{name}_{self.uid}", list(shape), dt, offset=off)

    def mark(self):
        return self.off

    def reset(self, m):
        self.P.barrier()
        self.off = m

    def bank(self, lo=0, hi=8):
        i = lo + (self.psi % (hi - lo))
        self.psi += 1
        return i

    def mm(self, out, lhsT, rhs, start, stop, r, w):
        return self.P.op("pe", lambda e: e.matmul(out, lhsT=lhsT, rhs=rhs, start=start, stop=stop),
                         reads=r, writes=w)

    def tr(self, out, in_, idn, r, w):
        return self.P.op("pe", lambda e: e.transpose(out, in_, idn), reads=r, writes=w)

    def act(self, out, in_, func, r, w, bias=None, scale=None):
        kw = {}
        if bias is not None:
            kw["bias"] = bias
        if scale is not None:
            kw["scale"] = scale
        return self.P.op("act", lambda e: e.activation(out, in_, func, **kw), reads=r, writes=w)

    def tt(self, eng, out, in0, in1, op, r, w):
        return self.P.op(eng, lambda e: e.tensor_tensor(out, in0, in1, op), reads=r, writes=w)

    def ts(self, eng, out, in0, s1, s2, op0, op1, r, w):
        if s2 is None:
            return self.P.op(eng, lambda e: e.tensor_scalar(out, in0, s1, None, op0), reads=r, writes=w)
        return self.P.op(eng, lambda e: e.tensor_scalar(out, in0, s1, s2, op0, op1), reads=r, writes=w)

    def stt(self, eng, out, in0, scalar, in1, op0, op1, r, w):
        return self.P.op(eng, lambda e: e.scalar_tensor_tensor(out, in0, scalar, in1, op0, op1),
                         reads=r, writes=w)

    def cp(self, eng, out, in_, r, w):
        if eng == "act":
            return self.P.op("act", lambda e: e.copy(out, in_), reads=r, writes=w)
        return self.P.op(eng, lambda e: e.tensor_copy(out, in_), reads=r, writes=w)

    def recip(self, out, in_, r, w):
        return self.P.op("dve", lambda e: e.reciprocal(out, in_), reads=r, writes=w)

    def memset(self, eng, ap, val, w):
        return self.P.op(eng, lambda e: e.memset(ap, val), writes=w)

    def dma(self, q, out, in_, r, w, persist=False):
        return self.P.op(q, lambda e: e.dma_start(out=out, in_=in_), reads=r, writes=w, dma=True, persist=persist)

    def allgather(self, src, dst, r, w, persist=False):
        return self.P.op("pool", lambda e: e.collective_compute(
            "AllGather", ALU.bypass, replica_groups=[[0, 1, 2, 3], [4, 5, 6, 7]],
            ins=[src], outs=[dst]), reads=r, writes=w, dma=True, inc=1, persist=persist)


def build_program(NT, L, dbg=None, dump=None):
    S = 128 * NT
    TC = S // 4
    NB = TC // 512
    TCt = TC // 128
    NKT = 2 + NT
    NK = NKT * 128
    assert TC % 512 == 0

    nc = bass.Bass("TRN2", target_bir_lowering=False)
    C = Ctx(nc)
    P = C.P

    def din(name, shape, dt=F32):
        return nc.dram_tensor(name, list(shape), dt, kind="ExternalInput")

    def dtmp(name, shape, dt):
        if dump and name in dump:
            return nc.dram_tensor(name, list(shape), dt, kind="ExternalOutput")
        return nc.dram_tensor(name, list(shape), dt)

    x_in = din("x_in", [TC, D])
    ctx_in = din("ctx_in", [CT, D])
    cvec = din("cvec", [2, D])
    w_mod = din("w_mod", [L, D, 1536]); b_mod = din("b_mod", [L, 6 * D])
    norm1_g = din("norm1_g", [L, D]); w_in = din("w_in", [L, D, DIN])
    q_norm_g = din("q_norm_g", [L, 256]); w_uq = din("w_uq", [L, 256, 768])
    kv_norm_g = din("kv_norm_g", [L, 128]); w_ukv = din("w_ukv", [L, 128, 1024])
    w_fourier = din("w_fourier", [L, 256, 256]); w_dw = din("w_dw", [L, 31, 256])
    b_dw = din("b_dw", [L, 256]); conv_ln_g = din("conv_ln_g", [L, 256]); conv_ln_b = din("conv_ln_b", [L, 256])
    w_pw2 = din("w_pw2", [L, 256, 256]); w_o = din("w_o", [L, D, D]); norm2_g = din("norm2_g", [L, D])
    w_mlp1 = din("w_mlp1", [L, D, DFF]); w_mlp2 = din("w_mlp2", [L, DFF, D])
    final_norm_g = din("final_norm_g", [D])
    rope1 = din("rope1", [32, TC]); rope2 = din("rope2", [32, TC])
    fft_cs = din("fft_cs", [NT, 2 * NT], BF16)
    fft_r1 = din("fft_r1", [128, NT * 64], BF16); fft_r2 = din("fft_r2", [128, NT * 64], BF16)
    ccb_d = din("ccb", [128, 128], BF16); mscb_d = din("mscb", [128, 128], BF16)
    w256_d = din("w256", [256, 512], BF16)
    ccbc_d = din("ccbc", [128, 128], BF16); mscbc_d = din("mscbc", [128, 128], BF16)
    ident_d = din("ident", [128, 128]); masks_d = din("masks", [128, 8])
    out_d = nc.dram_tensor("out", [TC, D], F32, kind="ExternalOutput")

    xT = dtmp("xT", [KC, 128, TC], F32); cT = dtmp("cT", [KC, 128, CT], F32)
    h2x = dtmp("h2x", [KC, 128, TC], BF16); h2c = dtmp("h2c", [KC, 128, CT], BF16)
    qx = dtmp("qx", [NH, 96, TC], BF16); qc = dtmp("qc", [NH, 96, CT], BF16)
    TCq = min(TC, 1024); NKV = TC // TCq
    FR = min(2 * TC, 2048); NF = 2 * TC // FR
    kv_own = [dtmp(f"kv_own{i}", [160, TCq], BF16) for i in range(NKV)]
    kv_all = [dtmp(f"kv_all{i}", [640, TCq], BF16) for i in range(NKV)]
    kvc = dtmp("kvc", [160, CT], BF16)
    f_own = [dtmp(f"f_own{i}", [FR, 128], BF16) for i in range(NF)]
    f_all = [dtmp(f"f_all{i}", [4 * FR, 128], BF16) for i in range(NF)]
    fc = dtmp("fc", [CT, 256], BF16)
    upx = dtmp("upx", [2, 128, TC + 32], BF16); upc = dtmp("upc", [2, 128, CT + 32], BF16)
    e_own = dtmp("e_own", [256, 32], BF16); e_all = dtmp("e_all", [1024, 32], BF16)
    mixx = dtmp("mixx", [KC, 128, TC], BF16); mixc = dtmp("mixc", [KC, 128, CT], BF16)

    kinds = {
        "x": dict(j=0, T=TC, xT=xT, h2=h2x, q=qx, kv=None, up=upx, mix=mixx),
        "c": dict(j=1, T=CT, xT=cT, h2=h2c, q=qc, kv=kvc, up=upc, mix=mixc),
    }

    def blocks(with_ctx=True):
        bl = []
        if with_ctx:
            bl.append(("c", 0, CT))
        for b in range(NB):
            bl.append(("x", b * 512, 512))
        return bl

    def tview(dt_, t0, n):
        return dt_.ap().rearrange("kc p n -> p kc n")[:, :, t0:t0 + n]

    ident = C.alloc("ident", [128, 128], F32)
    ones_bf = C.alloc("ones_bf", [128, 128], BF16)
    ones_f = C.alloc("ones_f", [128, 128], F32)
    eps_t = C.alloc("eps", [128, 1], F32)
    masks = C.alloc("masks", [128, 8], F32)
    modT = C.alloc("modT", [128, L, 2, 48], F32)
    bmodT = C.alloc("bmodT", [128, L, 48], F32)
    n1gT = C.alloc("n1gT", [128, L, 8], F32); n2gT = C.alloc("n2gT", [128, L, 8], F32)
    gm1 = C.alloc("gm1", [128, L, 2, 8], F32); gm2 = C.alloc("gm2", [128, L, 2, 8], F32)
    qgT = C.alloc("qgT", [128, L, 2], F32); kvgT = C.alloc("kvgT", [128, L, 1], F32)
    bdwT = C.alloc("bdwT", [128, L, 2], F32); lngT = C.alloc("lngT", [128, L, 2], F32)
    lnbT = C.alloc("lnbT", [128, L, 2], F32); wdwT = C.alloc("wdwT", [128, L, 62], F32)
    fngT = C.alloc("fngT", [128, 8], F32); cvT = C.alloc("cvT", [128, 16], F32)
    scT = C.alloc("scT", [128, 16], BF16)
    wukv_t = C.alloc("wukv", [128, 1024], BF16)
    PERSIST = C.mark()

    C.dma("sp", ident[:, :], ident_d.ap(), [], ["ident"])
    C.dma("sp", masks[:, :], masks_d.ap(), [], ["masks"])
    C.memset("dve", ones_bf[:, :], 1.0, ["ones_bf"])
    C.memset("dve", ones_f[:, :], 1.0, ["ones_f"])
    C.memset("dve", eps_t[:, :], EPS, ["eps"])

    stg = C.alloc("stg", [128, 128], F32)
    stg2 = C.alloc("stg2", [128, 128], F32)

    def rows128(ap1d, n):
        return ap1d.rearrange("(r p) -> r p", p=128)

    def vec_transpose(stage, key, nrows, dsts):
        b = C.bank()
        C.tr(C.ps[b][:, 0:nrows], stage[0:nrows, :], ident[0:nrows, 0:nrows], [key, "ident"], [f"ps{b}"])
        for dst, c0, c1, wk in dsts:
            C.cp("dve", dst, C.ps[b][:, c0:c1], [f"ps{b}"], [wk])

    for l in range(L):
        specs = [(rows128(b_mod.ap()[l], 48), 48), (rows128(norm1_g.ap()[l], 8), 8),
                 (rows128(norm2_g.ap()[l], 8), 8), (rows128(q_norm_g.ap()[l], 2), 2),
                 (rows128(kv_norm_g.ap()[l], 1), 1), (rows128(b_dw.ap()[l], 2), 2),
                 (rows128(conv_ln_g.ap()[l], 2), 2), (rows128(conv_ln_b.ap()[l], 2), 2)]
        r0 = 0
        for src, n in specs:
            C.dma("sp", stg[r0:r0 + n, :], src, [], ["stg"])
            r0 += n
        vec_transpose(stg, "stg", 73, [
            (bmodT[:, l, :], 0, 48, "bmodT"), (n1gT[:, l, :], 48, 56, "n1gT"), (n2gT[:, l, :], 56, 64, "n2gT"),
            (qgT[:, l, :], 64, 66, "qgT"), (kvgT[:, l, :], 66, 67, "kvgT"), (bdwT[:, l, :], 67, 69, "bdwT"),
            (lngT[:, l, :], 69, 71, "lngT"), (lnbT[:, l, :], 71, 73, "lnbT")])
        C.dma("sp", stg2[0:62, :], w_dw.ap()[l].rearrange("k (j p) -> (k j) p", p=128), [], ["stg2"])
        vec_transpose(stg2, "stg2", 62, [(wdwT[:, l, :], 0, 62, "wdwT")])
    C.dma("sp", stg[0:16, :], cvec.ap().rearrange("j (kc p) -> (j kc) p", p=128), [], ["stg"])
    C.dma("sp", stg[16:24, :], rows128(final_norm_g.ap(), 8), [], ["stg"])
    vec_transpose(stg, "stg", 24, [(cvT[:, :], 0, 16, "cvT"), (fngT[:, :], 16, 24, "fngT")])
    C.act(scT[:, :], cvT[:, :], AF.Silu, ["cvT"], ["scT"])

    m0 = C.mark()
    wm = [C.alloc(f"wm{i}", [128, 8, 1536], BF16) for i in range(2)]
    mpart = C.alloc("mpart", [128, L, 2, 12], F32)
    mp_own = dtmp("mp_own", [128, L * 24], F32)
    mp_all = dtmp("mp_all", [512, L * 24], F32)
    for l in range(L):
        wmt = wm[l % 2]
        for kc in range(8):
            C.dma("pool", wmt[:, kc, :], w_mod.ap()[l, kc * 128:(kc + 1) * 128, :], [], [("wm", l % 2, kc)])
        bm = C.bank()
        for oc in range(12):
            for kc in range(8):
                C.mm(C.ps[bm][:, oc * 2:oc * 2 + 2], wmt[:, kc, oc * 128:(oc + 1) * 128],
                     bass.AP(scT, kc, [[16, 128], [8, 2]]), kc == 0, kc == 7,
                     [("wm", l % 2, kc), "scT"], [f"ps{bm}"])
        for j in range(2):
            C.cp("dve", mpart[:, l, j, :], bass.AP(C.ps[bm], j, [[512, 128], [2, 12]]), [f"ps{bm}"], ["mpart"])
    C.dma("sp", mp_own.ap(), mpart[:, :, :, :].rearrange("p l j o -> p (l j o)"), ["mpart"], ["mp_own"])
    C.allgather(mp_own.ap(), mp_all.ap(), ["mp_own"], ["mp_all"])
    for r in range(4):
        C.dma("sp", modT[:, :, :, r * 12:(r + 1) * 12],
              mp_all.ap()[r * 128:(r + 1) * 128, :].rearrange("p (l j o) -> p l j o", l=L, j=2), ["mp_all"], [("modraw", r)])
    mrk = [("modraw", r) for r in range(4)]
    for l in range(L):
        for j in range(2):
            C.tt("dve", modT[:, l, j, :], modT[:, l, j, :], bmodT[:, l, :], ALU.add, mrk + ["bmodT"], ["modT"])
        for j in range(2):
            C.stt("dve", gm1[:, l, j, :], modT[:, l, j, 8:16], 1.0, n1gT[:, l, :], ALU.add, ALU.mult,
                  ["modT", "n1gT"], ["gm1"])
            C.stt("dve", gm2[:, l, j, :], modT[:, l, j, 32:40], 1.0, n2gT[:, l, :], ALU.add, ALU.mult,
                  ["modT", "n2gT"], ["gm2"])
    C.reset(m0)

    m0 = C.mark()
    xtok = [C.alloc(f"xtok{i}", [128, 4, D], F32) for i in range(2)]
    xbt = [C.alloc(f"xbt{i}", [128, 8, 512], F32) for i in range(2)]
    for bi, (kind, t0, n) in enumerate(blocks()):
        pb = bi % 2
        src = ctx_in if kind == "c" else x_in
        ntt = n // 128
        C.dma("sp", xtok[pb][:, 0:ntt, :], src.ap()[t0:t0 + n, :].rearrange("(tt p) d -> p tt d", p=128),
              [], [f"xtok{pb}"])
        for kc in range(8):
            b = C.bank()
            for tt_ in range(ntt):
                C.tr(C.ps[b][:, tt_ * 128:(tt_ + 1) * 128], xtok[pb][:, tt_, kc * 128:(kc + 1) * 128], ident[:, :],
                     [f"xtok{pb}", "ident"], [f"ps{b}"])
            C.cp("act" if kc % 2 else "dve", xbt[pb][:, kc, 0:n], C.ps[b][:, 0:n], [f"ps{b}"], [f"xbt{pb}"])
        C.dma("sp", tview(kinds[kind]["xT"], t0, n), xbt[pb][:, :, 0:n], [f"xbt{pb}"], [("xT", kind, t0)])
    zt = C.alloc("zt", [128, 2, 16], BF16)
    C.memset("dve", zt[:, :, :], 0.0, ["zt"])
    C.dma("sp", upc.ap().rearrange("j p n -> p j n")[:, :, 0:16], zt[:, :, :], ["zt"], ["upc_h"])
    C.dma("sp", upc.ap().rearrange("j p n -> p j n")[:, :, CT + 16:CT + 32], zt[:, :, :], ["zt"], ["upc_h"])
    C.reset(m0)

    def rms_stats(src3, n, nk, inv_n, sqt, sskey_r, tag):
        C.act(sqt[:, 0:nk, 0:n], src3, AF.Square, [sskey_r], ["sq" + tag])
        b = C.bank()
        for kc in range(nk):
            C.mm(C.ps[b][:, 0:n], ones_bf[:, :], sqt[:, kc, 0:n], kc == 0, kc == nk - 1,
                 ["sq" + tag, "ones_bf"], [f"ps{b}"])
        return b

    for l in range(L):
        last = (l == L - 1)
        if dbg == "P":
            break
        mA = C.mark()
        w_in_t = C.alloc("w_in_t", [128, 8, DIN], BF16)
        krsw = C.alloc("krsw", [128, 8, 32], BF16)
        w_uq_t = C.alloc("w_uq_t", [128, 2, 768], BF16)
        w_uq_s = C.alloc("w_uq_s", [128, 2, 768], BF16)
        for kc in range(8):
            C.dma("pool", w_in_t[:, kc, :], w_in.ap()[l, kc * 128:(kc + 1) * 128, :], [], [("w_in", kc)])
        C.dma("pool", wukv_t[:, :], w_ukv.ap()[l], [], ["wukv"], persist=True)
        C.dma("pool", krsw[:, :, 0:16], w_in.ap()[l].rearrange("(kc p) n -> p kc n", p=128)[:, :, 656:672], [], ["krsw0"])
        C.dma("pool", krsw[:, :, 16:32], w_in.ap()[l].rearrange("(kc p) n -> p kc n", p=128)[:, :, 640:656], [], ["krsw1"])
        uq_v = w_uq.ap()[l].rearrange("(j p) n -> p j n", p=128)
        C.dma("pool", w_uq_t[:, :, :], uq_v, [], ["w_uq"])
        C.dma("pool", w_uq_s[:, :, :], uq_v, [], ["w_uq_s"])
        uq_v4 = w_uq.ap()[l].rearrange("(j p) (h e) -> p j h e", p=128, e=96)
        s4 = w_uq_s[:, :, :].rearrange("p j (h e) -> p j h e", e=96)
        for j in range(2):
            C.dma("pool", s4[:, j, :, 64:80], uq_v4[:, j, :, 80:96], [], ["w_uq_s"])
            C.dma("pool", s4[:, j, :, 80:96], uq_v4[:, j, :, 64:80], [], ["w_uq_s"])

        xb = [C.alloc(f"xbA{i}", [128, 8, 512], F32) for i in range(2)]
        sq = C.alloc("sqA", [128, 8, 512], BF16)
        rs = C.alloc("rsA", [128, 512], F32)
        rstd = C.alloc("rstdA", [128, 512], F32)
        tmpA = [C.alloc(f"tmpA{i}", [128, 512], F32) for i in range(2)]
        hA = [C.alloc(f"hA{i}", [128, 8, 512], BF16) for i in range(2)]
        ftok = C.alloc("ftok", [128, 4, 256], BF16)
        cq = C.alloc("cq", [128, 2, 512], F32)
        sqq = C.alloc("sqq", [128, 2, 512], BF16)
        rq = C.alloc("rq", [128, 512], F32)
        cqn = C.alloc("cqn", [128, 2, 512], BF16)
        qst = C.alloc("qst", [128, NH, 512], BF16)
        rt1 = C.alloc("rt1", [128, 512], F32)
        rt2 = C.alloc("rt2", [128, 512], F32)
        rp1 = C.alloc("rp1", [128, 512], F32)
        rp2 = C.alloc("rp2", [128, 512], F32)
        ckv = C.alloc("ckv", [128, 512], F32)
        ckvn = C.alloc("ckvn", [128, 512], BF16)
        krr = C.alloc("krr", [128, 512], BF16)
        sg = C.alloc("sg", [128, 512], F32)
        ut = C.alloc("ut", [128, 2, 512], BF16)

        bl = blocks()
        def loadA(bi):
            kind, t0, n = bl[bi]
            C.dma("sp", xb[bi % 2][:, :, 0:n], tview(kinds[kind]["xT"], t0, n), [("xT", kind, t0)], [f"xbA{bi % 2}"])
        rsN = C.alloc("rsN", [128, 512], F32)

        def normA(bi):
            kind, t0, n = bl[bi]
            pb = bi % 2
            j = kinds[kind]["j"]
            b = rms_stats(xb[pb][:, :, 0:n], n, 8, None, sq, f"xbA{pb}", "A")
            C.act(rsN[:, 0:n], C.ps[b][:, 0:n], AF.Sqrt, [f"ps{b}", "eps"], ["rsN"], bias=eps_t[:, 0:1], scale=1.0 / D)
            C.recip(rstd[:, 0:n], rsN[:, 0:n], ["rsN"], ["rstdA"])
            for kc in range(8):
                C.tt("dve", tmpA[kc % 2][:, 0:n], xb[pb][:, kc, 0:n], rstd[:, 0:n], ALU.mult,
                     [f"xbA{pb}", "rstdA"], [f"tmpA{kc % 2}"])
                C.act(hA[pb][:, kc, 0:n], tmpA[kc % 2][:, 0:n], AF.Identity, [f"tmpA{kc % 2}", "gm1", "modT"], [f"hA{pb}"],
                      bias=modT[:, l, j, kc:kc + 1], scale=gm1[:, l, j, kc:kc + 1])

        loadA(0)
        if len(bl) > 1:
            loadA(1)
        normA(0)
        pend_ag = []
        defer_ag = []
        for bi, (kind, t0, n) in enumerate(bl):
            pb = bi % 2
            kd = kinds[kind]
            j = kd["j"]
            if bi + 1 < len(bl):
                normA(bi + 1)
            if bi + 2 < len(bl):
                loadA(bi + 2)
            for ag in pend_ag:
                C.allgather(*ag)
            pend_ag = []
            if kind == "x":
                C.dma("sp", rp1[64:96, 0:n], rope1.ap()[:, t0:t0 + n], [], ["rp1q"])
                C.dma("sp", rp2[64:96, 0:n], rope2.ap()[:, t0:t0 + n], [], ["rp2q"])
                C.dma("sp", rp1[0:32, 0:n], rope1.ap()[:, t0:t0 + n], [], ["rp1k"])
                C.dma("sp", rp2[0:32, 0:n], rope2.ap()[:, t0:t0 + n], [], ["rp2k"])
            hk = f"hA{pb}"
            h = hA[pb]
            do_rest = not (kind == "c" and last)
            if do_rest and 'f' not in SKIP:
                ntt = n // 128
                for t2 in range(0, ntt, 2):
                    b = C.bank()
                    for tt_ in range(t2, min(t2 + 2, ntt)):
                        for kc in range(8):
                            C.mm(C.ps[b][:, (tt_ - t2) * 256:(tt_ - t2 + 1) * 256], h[:, kc, tt_ * 128:(tt_ + 1) * 128],
                                 w_in_t[:, kc, 0:256], kc == 0, kc == 7, [hk, ("w_in", kc)], [f"ps{b}"])
                    nn = min(2, ntt - t2)
                    C.cp("act", ftok[:, t2:t2 + nn, :], C.ps[b][:, 0:nn * 256].rearrange("p (t c) -> p t c", c=256),
                         [f"ps{b}"], ["ftok"])
                if kind == "x":
                    for gp in range(2):
                        g0 = gp * TC + t0
                        C.dma("sp", f_own[g0 // FR].ap()[g0 % FR:g0 % FR + n, :].rearrange("(tt p) c -> p tt c", p=128),
                              ftok[:, 0:ntt, gp * 128:(gp + 1) * 128], ["ftok"], [("fo", g0 // FR, g0 % FR)])
                else:
                    C.dma("sp", fc.ap().rearrange("(tt p) c -> p tt c", p=128), ftok[:, 0:ntt, :], ["ftok"], ["fc"])
            if do_rest and 'q' not in SKIP:
                for jq in range(2):
                    b = C.bank()
                    for kc in range(8):
                        C.mm(C.ps[b][:, 0:n], w_in_t[:, kc, 256 + jq * 128:256 + (jq + 1) * 128], h[:, kc, 0:n],
                             kc == 0, kc == 7, [hk, ("w_in", kc)], [f"ps{b}"])
                    C.cp("dve", cq[:, jq, 0:n], C.ps[b][:, 0:n], [f"ps{b}"], ["cq"])
                    C.act(sqq[:, jq, 0:n], C.ps[b][:, 0:n], AF.Square, [f"ps{b}"], ["sqq"])
                b = C.bank()
                for jq in range(2):
                    C.mm(C.ps[b][:, 0:n], ones_bf[:, :], sqq[:, jq, 0:n], jq == 0, jq == 1, ["sqq", "ones_bf"], [f"ps{b}"])
                C.act(rs[:, 0:n], C.ps[b][:, 0:n], AF.Sqrt, [f"ps{b}", "eps"], ["rsA"], bias=eps_t[:, 0:1], scale=1.0 / 256)
                C.recip(rq[:, 0:n], rs[:, 0:n], ["rsA"], ["rq"])
                for jq in range(2):
                    C.stt("dve", cqn[:, jq, 0:n], cq[:, jq, 0:n], qgT[:, l, jq:jq + 1], rq[:, 0:n], ALU.mult, ALU.mult,
                          ["cq", "rq", "qgT"], ["cqn"])
                for hh in range(NH):
                    ba = C.bank()
                    for jq in range(2):
                        C.mm(C.ps[ba][0:96, 0:n], w_uq_t[:, jq, hh * 96:(hh + 1) * 96], cqn[:, jq, 0:n], jq == 0, jq == 1,
                             ["cqn", "w_uq"], [f"ps{ba}"])
                    C.cp("act", qst[0:64, hh, 0:n], C.ps[ba][0:64, 0:n], [f"ps{ba}"], [("qst", "n")])
                    if kind == "x":
                        bb = C.bank()
                        for jq in range(2):
                            C.mm(C.ps[bb][0:96, 0:n], w_uq_s[:, jq, hh * 96:(hh + 1) * 96], cqn[:, jq, 0:n], jq == 0, jq == 1,
                                 ["cqn", "w_uq_s"], [f"ps{bb}"])
                        C.tt("dve", rt1[64:96, 0:n], C.ps[ba][64:96, 0:n], rp1[64:96, 0:n], ALU.mult,
                             [f"ps{ba}", "rp1q"], ["rt1"])
                        C.tt("dve", rt2[64:96, 0:n], C.ps[bb][64:96, 0:n], rp2[64:96, 0:n], ALU.mult,
                             [f"ps{bb}", "rp2q"], ["rt2"])
                        C.tt("pool", qst[64:96, hh, 0:n], rt1[64:96, 0:n], rt2[64:96, 0:n], ALU.add,
                             ["rt1", "rt2"], [("qst", "r")])
                    else:
                        C.cp("act", qst[64:96, hh, 0:n], C.ps[ba][64:96, 0:n], [f"ps{ba}"], [("qst", "r")])
                C.dma("sp", kd["q"].ap().rearrange("h e n -> e h n")[:, :, t0:t0 + n], qst[0:96, :, 0:n],
                      [("qst", "n"), ("qst", "r")], [("q", kind)])
            if 'kv' in SKIP:
                continue
            b = C.bank()
            for kc in range(8):
                C.mm(C.ps[b][:, 0:n], w_in_t[:, kc, 512:640], h[:, kc, 0:n], kc == 0, kc == 7, [hk, ("w_in", kc)], [f"ps{b}"])
            C.cp("dve", ckv[:, 0:n], C.ps[b][:, 0:n], [f"ps{b}"], ["ckv"])
            C.act(sqq[:, 0, 0:n], C.ps[b][:, 0:n], AF.Square, [f"ps{b}"], ["sqq"])
            b = C.bank()
            C.mm(C.ps[b][:, 0:n], ones_bf[:, :], sqq[:, 0, 0:n], True, True, ["sqq", "ones_bf"], [f"ps{b}"])
            C.act(rs[:, 0:n], C.ps[b][:, 0:n], AF.Sqrt, [f"ps{b}", "eps"], ["rsA"], bias=eps_t[:, 0:1], scale=1.0 / 128)
            C.recip(rq[:, 0:n], rs[:, 0:n], ["rsA"], ["rq"])
            C.stt("dve", ckvn[:, 0:n], ckv[:, 0:n], kvgT[:, l, 0:1], rq[:, 0:n], ALU.mult, ALU.mult,
                  ["ckv", "rq", "kvgT"], ["ckvn"])
            kvdst = kvc.ap()[:, 0:n] if kind == "c" else kv_own[t0 // TCq].ap()[:, t0 % TCq:t0 % TCq + n]
            C.dma("sp", kvdst[0:128, :], ckvn[:, 0:n], ["ckvn"], [("kvo", kind, t0, 0)])
            if 'kr' in SKIP:
                continue
            ba = C.bank()
            for kc in range(8):
                C.mm(C.ps[ba][0:32, 0:n], w_in_t[:, kc, 640:672], h[:, kc, 0:n], kc == 0, kc == 7, [hk, ("w_in", kc)], [f"ps{ba}"])
            if kind == "x":
                bb = C.bank()
                for kc in range(8):
                    C.mm(C.ps[bb][0:32, 0:n], krsw[:, kc, :], h[:, kc, 0:n], kc == 0, kc == 7, [hk, "krsw0", "krsw1"], [f"ps{bb}"])
                C.tt("dve", rt1[0:32, 0:n], C.ps[ba][0:32, 0:n], rp1[0:32, 0:n], ALU.mult, [f"ps{ba}", "rp1k"], ["rt1"])
                C.tt("dve", rt2[0:32, 0:n], C.ps[bb][0:32, 0:n], rp2[0:32, 0:n], ALU.mult, [f"ps{bb}", "rp2k"], ["rt2"])
                C.tt("pool", krr[0:32, 0:n], rt1[0:32, 0:n], rt2[0:32, 0:n], ALU.add, ["rt1", "rt2"], ["krr"])
            else:
                C.cp("act", krr[0:32, 0:n], C.ps[ba][0:32, 0:n], [f"ps{ba}"], ["krr"])
            C.dma("sp", kvdst[128:160, :], krr[0:32, 0:n], ["krr"], [("kvo", kind, t0, 1)])
            if kind == "x" and 'ag' not in SKIP:
                if (t0 + n) % TCq == 0:
                    ci = t0 // TCq
                    rk = [("kvo", "x", tt0, pp) for tt0 in range(ci * TCq, (ci + 1) * TCq, 512) for pp in range(2)]
                    pend_ag.append((kv_own[ci].ap(), kv_all[ci].ap(), rk, []))
                if do_rest and 'f' not in SKIP:
                    for gp in range(2):
                        g1 = gp * TC + t0 + n
                        if g1 % FR == 0:
                            ci = g1 // FR - 1
                            rk = [("fo", ci, off) for off in range(0, FR, 512)]
                            defer_ag.append((f_own[ci].ap(), f_all[ci].ap(), rk, [("f_all", ci)]))

            if do_rest and 'glu' not in SKIP:
                for jg in range(2):
                    ba = C.bank()
                    bb = C.bank()
                    for kc in range(8):
                        C.mm(C.ps[ba][:, 0:n], w_in_t[:, kc, 672 + jg * 128:672 + (jg + 1) * 128], h[:, kc, 0:n],
                             kc == 0, kc == 7, [hk, ("w_in", kc)], [f"ps{ba}"])
                    for kc in range(8):
                        C.mm(C.ps[bb][:, 0:n], w_in_t[:, kc, 928 + jg * 128:928 + (jg + 1) * 128], h[:, kc, 0:n],
                             kc == 0, kc == 7, [hk, ("w_in", kc)], [f"ps{bb}"])
                    C.act(sg[:, 0:n], C.ps[bb][:, 0:n], AF.Sigmoid, [f"ps{bb}"], ["sg"])
                    C.tt("dve", ut[:, jg, 0:n], C.ps[ba][:, 0:n], sg[:, 0:n], ALU.mult, [f"ps{ba}", "sg"], ["ut"])
                C.dma("sp", kd["up"].ap().rearrange("j p n -> p j n")[:, :, 16 + t0:16 + t0 + n], ut[:, :, 0:n],
                      ["ut"], [("up", kind)])
        for ag in pend_ag:
            C.allgather(*ag)
        pend_ag = []
        upv = upx.ap().rearrange("j p n -> (j p) n")
        if 'edge' not in SKIP:
            C.dma("sp", e_own.ap()[:, 0:16], upv[:, 16:32], [("up", "x")], ["e_own"])
            C.dma("sp", e_own.ap()[:, 16:32], upv[:, TC:TC + 16], [("up", "x")], ["e_own"])
        if 'ag' not in SKIP:
            for ag in defer_ag:
                C.allgather(*ag, persist=True)
            C.allgather(e_own.ap(), e_all.ap(), ["e_own"], ["e_all"], persist=True)
        C.reset(mA)
        if dbg == "A":
            break
        V = SimpleNamespace(**locals())
        phase_attention(C, V)
        if dbg == "C":
            break
        phase_fourier(C, V)
        W2_OFF = SB_TOP - 65536
        W1_OFF = W2_OFF - 65536
        WO_OFF = W1_OFF - 16384
        V.wo = C.alloc_at("wo", [128, 8, D], BF16, WO_OFF)
        V.w1 = C.alloc_at("w1", [128, 8, DFF], BF16, W1_OFF)
        V.w2 = C.alloc_at("w2", [128, 32, D], BF16, W2_OFF)
        for kc in range(8):
            C.dma("pool", V.wo[:, kc, :], w_o.ap()[l, kc * 128:(kc + 1) * 128, :], [], [("wo", kc)], persist=True)
        for kc in range(8):
            C.dma("pool", V.w1[:, kc, :], w_mlp1.ap()[l, kc * 128:(kc + 1) * 128, :], [], [("w1", kc)], persist=True)
        for jc in range(32):
            C.dma("pool", V.w2[:, jc, :], w_mlp2.ap()[l, jc * 128:(jc + 1) * 128, :], [], [("w2", jc)], persist=True)
        C.limit = WO_OFF
        phase_conv(C, V)
        if dbg == "E":
            break
        phase_out(C, V)
        C.limit = W1_OFF
        phase_mlp(C, V)
        C.limit = SB_TOP
        C.P.persist_w = {}

    if dbg is None:
        phase_final(C, SimpleNamespace(**locals()))
    P.barrier()
    P.emit(nc)
    return nc


def phase_attention(C, V):
    l, last, TC, NB, NK, NKT = V.l, V.last, V.TC, V.NB, V.NK, V.NKT
    m = C.mark()
    wukv = V.wukv_t
    KVn = C.alloc("KVn", [128, NK], BF16)
    Kt = [C.alloc(f"Kt{i}", [128, NK], BF16) for i in range(2)]
    Vt = [C.alloc(f"Vt{i}", [128, NKT, 128], BF16) for i in range(2)]
    Qt = [C.alloc(f"Qt{i}", [128, TC], BF16) for i in range(2)]
    Qc = C.alloc("Qc", [128, CT], BF16)
    Pt = [C.alloc(f"Pt{i}", [128, 512], BF16) for i in range(4)]
    rinv = C.alloc("rinv", [128, 512], F32)
    aost = [C.alloc(f"aost{i}", [128, 512], BF16) for i in range(2)]
    TCq, NKV = V.TCq, V.NKV
    kvkeys = [("KVn", i) for i in range(1 + 4 * NKV)]
    C.dma("sp", KVn[:, 0:CT], V.kvc.ap()[0:128, :], [], [kvkeys[0]])
    for r in range(4):
        for cj in range(NKV):
            c0 = CT + r * TC + cj * TCq
            C.dma("sp", KVn[:, c0:c0 + TCq], V.kv_all[cj].ap()[r * 160:r * 160 + 128, :], [], [kvkeys[1 + r * NKV + cj]])
    krkeys = {}
    for i in range(2):
        krkeys[i] = [("Kr", i, k) for k in range(1 + 4 * NKV)]
        C.dma("sp", Kt[i][64:96, 0:CT], V.kvc.ap()[128:160, :], [], [krkeys[i][0]])
        for r in range(4):
            for cj in range(NKV):
                c0 = CT + r * TC + cj * TCq
                C.dma("sp", Kt[i][64:96, c0:c0 + TCq], V.kv_all[cj].ap()[r * 160 + 128:r * 160 + 160, :],
                      [], [krkeys[i][1 + r * NKV + cj]])
        C.memset("dve", Vt[i][:, :, 64:128], 1.0, [("Vone", i)])
    state = {"ob": 0, "ao": 0}

    def attn_block(hp, qap, qkey, n, nkt, dst, hook=None):
        ob = 4 + state["ob"] % 2
        state["ob"] += 1
        LOOK = 2
        for i in range(nkt + LOOK):
            if i < nkt:
                sbk = i % 4
                C.mm(C.ps[sbk][:, 0:n], Kt[hp][0:96, i * 128:(i + 1) * 128], qap, True, True,
                     [("Kn", hp), qkey] + krkeys[hp], [f"ps{sbk}"])
                C.act(Pt[sbk][:, 0:n], C.ps[sbk][:, 0:n], AF.Exp, [f"ps{sbk}"], [f"Pt{sbk}"], scale=ATTN_SCALE)
            ii = i - LOOK
            if ii >= 0:
                C.mm(C.ps[ob][:, 0:n], Vt[hp][:, ii, :], Pt[ii % 4][:, 0:n], ii == 0, ii == nkt - 1,
                     [f"Pt{ii % 4}", ("Vv", hp), ("Vone", hp)], [f"ps{ob}"])
                if hook is not None:
                    hook()
        a = state["ao"] % 2
        state["ao"] += 1
        C.recip(rinv[64:128, 0:n], C.ps[ob][64:128, 0:n], [f"ps{ob}"], ["rinv"])
        C.tt("dve", aost[a][0:64, 0:n], C.ps[ob][0:64, 0:n], rinv[64:128, 0:n], ALU.mult, [f"ps{ob}", "rinv"], [f"aost{a}"])
        C.dma("sp", dst, aost[a][0:64, 0:n], [f"aost{a}"], [])

    def build_steps(h):
        hp = h % 2
        steps = []
        steps.append(lambda: C.dma("sp", Qt[hp][0:96, :], V.qx.ap()[h], [], [("Q", hp)]))
        for c0 in range(0, NK, 512):
            nb = min(512, NK - c0)

            def kstep(c0=c0, nb=nb):
                b = C.bank(6, 8)
                C.mm(C.ps[b][0:64, 0:nb], wukv[:, h * 128:h * 128 + 64], KVn[:, c0:c0 + nb], True, True,
                     kvkeys + ["wukv"], [f"ps{b}"])
                C.cp("dve", Kt[hp][0:64, c0:c0 + nb], C.ps[b][0:64, 0:nb], [f"ps{b}"], [("Kn", hp)])
            steps.append(kstep)
        for g in range(0, NKT, 8):
            ng = min(8, NKT - g)

            def vstep(g=g, ng=ng):
                b = C.bank(6, 8)
                for kt in range(g, g + ng):
                    C.mm(C.ps[b][:, (kt - g) * 64:(kt - g + 1) * 64], KVn[:, kt * 128:(kt + 1) * 128],
                         wukv[:, h * 128 + 64:h * 128 + 128], True, True, kvkeys + ["wukv"], [f"ps{b}"])
                C.cp("dve", Vt[hp][:, g:g + ng, 0:64], C.ps[b][:, 0:ng * 64].rearrange("p (t c) -> p t c", c=64),
                     [f"ps{b}"], [("Vv", hp)])
            steps.append(vstep)
        return steps

    for st in build_steps(0):
        st()
    for h in range(NH):
        hp = h % 2
        nxt = build_steps(h + 1) if h + 1 < NH else []
        total_tiles = NB * NKT
        every = max(1, total_tiles // (len(nxt) + 2)) if nxt else 0
        cnt = {"n": 0}

        def hook():
            cnt["n"] += 1
            if nxt and every and cnt["n"] % every == 0:
                nxt.pop(0)()
        for qb in range(NB):
            dst = V.mixx.ap()[2 + h // 2, (h % 2) * 64:(h % 2) * 64 + 64, qb * 512:(qb + 1) * 512]
            attn_block(hp, Qt[hp][0:96, qb * 512:(qb + 1) * 512], ("Q", hp), 512, NKT, dst, hook=hook)
        if not last:
            C.dma("sp", Qc[0:96, :], V.qc.ap()[h], [], ["Qc"])
            dst = V.mixc.ap()[2 + h // 2, (h % 2) * 64:(h % 2) * 64 + 64, :]
            attn_block(hp, Qc[0:96, :], "Qc", CT, 2, dst)
        while nxt:
            nxt.pop(0)()
    C.reset(m)


def phase_fourier(C, V):
    l, last, TC, NB, NT, TCt = V.l, V.last, V.TC, V.NB, V.NT, V.TCt
    m = C.mark()
    cs = C.alloc("cs", [128, 2 * NT], BF16)
    r1 = C.alloc("r1", [128, NT, 64], BF16)
    r2 = C.alloc("r2", [128, NT, 64], BF16)
    ccb = C.alloc("ccb", [128, 128], BF16); mscb = C.alloc("mscb", [128, 128], BF16)
    wf = C.alloc("wf", [128, 2, 256], BF16)
    Yt = C.alloc("Yt", [128, 2, TC], BF16)
    Z = C.alloc("Z", [128, 128, 128], BF16)
    T = C.alloc("T", [128, 128, 2 * NT], BF16)
    Ure = C.alloc("Ure", [128, TC], BF16)
    Vv = C.alloc("Vv", [128, TC], BF16)
    yst = [C.alloc(f"yst{i}", [128, 2, 512], BF16) for i in range(2)]
    C.dma("sp", cs[0:NT, :], V.fft_cs.ap(), [], ["cs"])
    C.dma("sp", r1[:, :, :], V.fft_r1.ap().rearrange("p (k c) -> p k c", c=64), [], ["r1"])
    C.dma("sp", r2[:, :, :], V.fft_r2.ap().rearrange("p (k c) -> p k c", c=64), [], ["r2"])
    C.dma("sp", ccb[:, :], V.ccb_d.ap(), [], ["ccb"])
    C.dma("sp", mscb[:, :], V.mscb_d.ap(), [], ["mscb"])
    C.dma("pool", wf[:, :, :], V.w_fourier.ap()[l].rearrange("(j p) n -> p j n", p=128), [], ["wf"])
    cnt = 0
    for gp in range(2):
        FR, NF = V.FR, V.NF
        zkeys = []
        for r in range(4):
            rows_per = min(FR, TC)
            for cj in range(TC // rows_per):
                g0 = gp * TC + cj * rows_per
                ch, off = g0 // FR, g0 % FR
                p0 = r * TCt + cj * (rows_per // 128)
                zk = ("Z", r, cj)
                zkeys.append(zk)
                C.dma("sp", Z[p0:p0 + rows_per // 128, :, :],
                      V.f_all[ch].ap()[r * FR + off:r * FR + off + rows_per, :].rearrange("(a b) c -> a b c", b=128),
                      [("f_all", ch)], [zk])
        cpb = 512 // (2 * NT)
        cpb = min(cpb, 128)
        for c0 in range(0, 128, cpb):
            b = C.bank()
            for c in range(c0, c0 + cpb):
                C.mm(C.ps[b][:, (c - c0) * 2 * NT:(c - c0 + 1) * 2 * NT], Z[0:NT, :, c], cs[0:NT, :], True, True,
                     zkeys + ["cs"], [f"ps{b}"])
            C.cp("act" if cnt % 2 else "dve", T[:, c0:c0 + cpb, :],
                 C.ps[b][:, 0:cpb * 2 * NT].rearrange("p (c k) -> p c k", k=2 * NT), [f"ps{b}"], ["T"])
            cnt += 1
        for k0 in range(0, NT, 8):
            b = C.bank()
            for k1 in range(k0, k0 + 8):
                o = C.ps[b][:, (k1 - k0) * 64:(k1 - k0 + 1) * 64]
                C.mm(o, T[:, :, k1], r1[:, k1, :], True, False, ["T", "r1"], [f"ps{b}"])
                C.mm(o, T[:, :, NT + k1], r2[:, k1, :], False, True, ["T", "r2"], [f"ps{b}"])
            psv = C.ps[b][:, :].rearrange("p (k1 two k2) -> p two k2 k1", k1=8, two=2, k2=32)
            C.cp("dve", Ure[:, :].rearrange("p (k2 k1) -> p k2 k1", k1=NT)[:, :, k0:k0 + 8], psv[:, 0], [f"ps{b}"], ["Ure"])
            C.cp("act", Vv[:, :].rearrange("p (k2 k1) -> p k2 k1", k1=NT)[:, :, k0:k0 + 8], psv[:, 1], [f"ps{b}"], ["Vv"])
        for tb in range(NB):
            b = C.bank()
            C.mm(C.ps[b][:, :], ccb[:, :], Ure[:, tb * 512:(tb + 1) * 512], True, False, ["Ure", "ccb"], [f"ps{b}"])
            C.mm(C.ps[b][:, :], mscb[:, :], Vv[:, tb * 512:(tb + 1) * 512], False, True, ["Vv", "mscb"], [f"ps{b}"])
            C.cp("act" if tb % 2 else "dve", Yt[:, gp, tb * 512:(tb + 1) * 512], C.ps[b][:, :], [f"ps{b}"], [("Yt", gp)])
    mixv = V.mixx.ap().rearrange("kc p n -> p kc n")
    for tb in range(NB):
        y = yst[tb % 2]
        for oc in range(2):
            b = C.bank()
            for gp in range(2):
                C.mm(C.ps[b][:, :], wf[:, gp, oc * 128:(oc + 1) * 128], Yt[:, gp, tb * 512:(tb + 1) * 512], gp == 0, gp == 1,
                     [("Yt", 0), ("Yt", 1), "wf"], [f"ps{b}"])
            C.cp("act" if oc else "dve", y[:, oc, :], C.ps[b][:, :], [f"ps{b}"], [f"yst{tb % 2}"])
        C.dma("sp", mixv[:, 0:2, tb * 512:(tb + 1) * 512], y[:, :, :], [f"yst{tb % 2}"], [])
    if not last:
        Zc = C.alloc("Zc", [128, 2, 256], BF16)
        w256 = C.alloc("w256", [128, 2, 512], BF16)
        ccbc = C.alloc("ccbc", [128, 128], BF16); mscbc = C.alloc("mscbc", [128, 128], BF16)
        UVc = C.alloc("UVc", [128, 512], BF16)
        Ytc = C.alloc("Ytc", [128, 2, 256], BF16)
        ystc = C.alloc("ystc", [128, 2, 256], BF16)
        C.dma("sp", Zc[:, :, :], V.fc.ap().rearrange("(tt p) c -> p tt c", p=128), [], ["Zc"])
        C.dma("sp", w256[:, :, :], V.w256_d.ap().rearrange("(tt p) k -> p tt k", p=128), [], ["w256"])
        C.dma("sp", ccbc[:, :], V.ccbc_d.ap(), [], ["ccbc"])
        C.dma("sp", mscbc[:, :], V.mscbc_d.ap(), [], ["mscbc"])
        for cp_ in range(2):
            b = C.bank()
            for nt in range(2):
                C.mm(C.ps[b][:, :], Zc[:, nt, cp_ * 128:(cp_ + 1) * 128], w256[:, nt, :], nt == 0, nt == 1,
                     ["Zc", "w256"], [f"ps{b}"])
            C.cp("dve", UVc[:, :], C.ps[b][:, :], [f"ps{b}"], ["UVc"])
            b = C.bank()
            C.mm(C.ps[b][:, 0:256], ccbc[:, :], UVc[:, 0:256], True, False, ["UVc", "ccbc"], [f"ps{b}"])
            C.mm(C.ps[b][:, 0:256], mscbc[:, :], UVc[:, 256:512], False, True, ["UVc", "mscbc"], [f"ps{b}"])
            C.cp("act", Ytc[:, cp_, :], C.ps[b][:, 0:256], [f"ps{b}"], [("Ytc", cp_)])
        for oc in range(2):
            b = C.bank()
            for gp in range(2):
                C.mm(C.ps[b][:, 0:256], wf[:, gp, oc * 128:(oc + 1) * 128], Ytc[:, gp, :], gp == 0, gp == 1,
                     [("Ytc", 0), ("Ytc", 1), "wf"], [f"ps{b}"])
            C.cp("act" if oc else "dve", ystc[:, oc, :], C.ps[b][:, 0:256], [f"ps{b}"], ["ystc"])
        C.dma("sp", V.mixc.ap().rearrange("kc p n -> p kc n")[:, 0:2, :], ystc[:, :, :], ["ystc"], [])
    C.reset(m)


def phase_conv(C, V):
    l, last, TC = V.l, V.last, V.TC
    modT, wdwT, bdwT, lngT, lnbT = V.modT, V.wdwT, V.bdwT, V.lngT, V.lnbT
    m = C.mark()
    wpw = C.alloc("wpw", [128, 2, 256], BF16)
    C.dma("pool", wpw[:, :, :], V.w_pw2.ap()[l].rearrange("(j p) n -> p j n", p=128), [], ["wpw"])
    E = C.alloc("E", [128, 4, 2, 32], BF16)
    hl = C.alloc("hl", [128, 2, 16], F32); hr = C.alloc("hr", [128, 2, 16], F32)
    hb = C.alloc("hb", [128, 2, 32], BF16)
    C.dma("sp", E[:, :, :, :], V.e_all.ap().rearrange("(r j p) n -> p r j n", r=4, j=2), ["e_all"], ["E"])
    masks = V.masks
    C.ts("dve", hl[:, :, :], E[:, 0, :, 16:32], masks[:, 0:1], None, ALU.mult, None, ["E", "masks"], ["hl"])
    C.ts("dve", hr[:, :, :], E[:, 0, :, 0:16], masks[:, 4:5], None, ALU.mult, None, ["E", "masks"], ["hr"])
    for r in range(1, 4):
        C.stt("dve", hl[:, :, :], E[:, r, :, 16:32], masks[:, r:r + 1], hl[:, :, :], ALU.mult, ALU.add, ["E", "masks", "hl"], ["hl"])
        C.stt("dve", hr[:, :, :], E[:, r, :, 0:16], masks[:, 4 + r:5 + r], hr[:, :, :], ALU.mult, ALU.add, ["E", "masks", "hr"], ["hr"])
    C.cp("dve", hb[:, :, 0:16], hl[:, :, :], ["hl"], ["hb"])
    C.cp("dve", hb[:, :, 16:32], hr[:, :, :], ["hr"], ["hb"])
    upv = V.upx.ap().rearrange("j p n -> p j n")
    C.dma("sp", upv[:, :, 0:16], hb[:, :, 0:16], ["hb"], ["uph"])
    C.dma("sp", upv[:, :, TC + 16:TC + 32], hb[:, :, 16:32], ["hb"], ["uph"])

    U = [C.alloc(f"U{i}", [128, 2, 544], BF16) for i in range(2)]
    cv = C.alloc("cv", [128, 2, 512], F32); xc = C.alloc("xc", [128, 2, 512], F32)
    sqc = C.alloc("sqc", [128, 2, 512], F32)
    mean = C.alloc("mean", [128, 512], F32); rsc = C.alloc("rsc", [128, 512], F32)
    rstdc = C.alloc("rstdc", [128, 512], F32)
    tmpc = [C.alloc(f"tmpc{i}", [128, 512], F32) for i in range(2)]
    sl = C.alloc("sl", [128, 2, 512], BF16)
    ycst = [C.alloc(f"ycst{i}", [128, 2, 512], BF16) for i in range(2)]
    Dg = C.alloc("Dg", [128, 62, 128], BF16)
    for kj in range(62):
        C.ts("dve", Dg[:, kj, :], V.ident[:, :], wdwT[:, l, kj:kj + 1], None, ALU.mult, None,
             ["ident", "wdwT"], [("Dg", kj % 2)])
    bl = V.blocks(with_ctx=not last)

    def loadU(bi):
        kind, t0, n = bl[bi]
        up = V.kinds[kind]["up"].ap().rearrange("j p n -> p j n")
        C.dma("sp", U[bi % 2][:, :, 0:n + 32], up[:, :, t0:t0 + n + 32], ["uph"], [f"U{bi % 2}"])
    loadU(0)
    for bi, (kind, t0, n) in enumerate(bl):
        pb = bi % 2
        kd = V.kinds[kind]
        if bi + 1 < len(bl):
            loadU(bi + 1)
        Ub = U[pb]
        uk = f"U{pb}"
        for j in range(2):
            b = C.bank()
            for k in range(31):
                C.mm(C.ps[b][:, 0:n], Dg[:, k * 2 + j, :], Ub[:, j, k + 1:k + 1 + n], k == 0, k == 30,
                     [uk, ("Dg", 0), ("Dg", 1)], [f"ps{b}"])
            C.act(cv[:, j, 0:n], C.ps[b][:, 0:n], AF.Identity, [f"ps{b}", "bdwT"], [("cv", j)], bias=bdwT[:, l, j:j + 1])
        b = C.bank()
        for j in range(2):
            C.mm(C.ps[b][:, 0:n], V.ones_f[:, :], cv[:, j, 0:n], j == 0, j == 1, [("cv", j), "ones_f"], [f"ps{b}"])
        C.act(mean[:, 0:n], C.ps[b][:, 0:n], AF.Identity, [f"ps{b}"], ["mean"], scale=1.0 / 256)
        for j in range(2):
            C.tt("dve", xc[:, j, 0:n], cv[:, j, 0:n], mean[:, 0:n], ALU.subtract, [("cv", j), "mean"], ["xc"])
        C.act(sqc[:, :, 0:n], xc[:, :, 0:n], AF.Square, ["xc"], ["sqc"])
        b = C.bank()
        for j in range(2):
            C.mm(C.ps[b][:, 0:n], V.ones_f[:, :], sqc[:, j, 0:n], j == 0, j == 1, ["sqc", "ones_f"], [f"ps{b}"])
        C.act(rsc[:, 0:n], C.ps[b][:, 0:n], AF.Sqrt, [f"ps{b}", "eps"], ["rsc"], bias=V.eps_t[:, 0:1], scale=1.0 / 256)
        C.recip(rstdc[:, 0:n], rsc[:, 0:n], ["rsc"], ["rstdc"])
        for j in range(2):
            C.tt("dve", tmpc[j][:, 0:n], xc[:, j, 0:n], rstdc[:, 0:n], ALU.mult, ["xc", "rstdc"], [f"tmpc{j}"])
            C.act(sl[:, j, 0:n], tmpc[j][:, 0:n], AF.Silu, [f"tmpc{j}", "lngT", "lnbT"], ["sl"],
                  bias=lnbT[:, l, j:j + 1], scale=lngT[:, l, j:j + 1])
        for oc in range(2):
            b = C.bank()
            for j in range(2):
                C.mm(C.ps[b][:, 0:n], wpw[:, j, oc * 128:(oc + 1) * 128], sl[:, j, 0:n], j == 0, j == 1, ["sl", "wpw"], [f"ps{b}"])
            C.cp("act" if oc else "dve", ycst[pb][:, oc, 0:n], C.ps[b][:, 0:n], [f"ps{b}"], [f"ycst{pb}"])
        C.dma("sp", kd["mix"].ap().rearrange("kc p n -> p kc n")[:, 6:8, t0:t0 + n], ycst[pb][:, :, 0:n], [f"ycst{pb}"], [])
    C.reset(m)


def _norm_mod(C, V, xbt, xkey, n, l, j, gm, sh_off, sq, rs, rstd, tmp, hout, hkey):
    C.act(sq[:, :, 0:n], xbt[:, :, 0:n], AF.Square, [xkey], ["sqN"])
    b = C.bank()
    for kc in range(8):
        C.mm(C.ps[b][:, 0:n], V.ones_bf[:, :], sq[:, kc, 0:n], kc == 0, kc == 7, ["sqN", "ones_bf"], [f"ps{b}"])
    C.act(rs[:, 0:n], C.ps[b][:, 0:n], AF.Sqrt, [f"ps{b}", "eps"], ["rsN"], bias=V.eps_t[:, 0:1], scale=1.0 / D)
    C.recip(rstd[:, 0:n], rs[:, 0:n], ["rsN"], ["rstdN"])
    for kc in range(8):
        C.tt("dve", tmp[kc % 2][:, 0:n], xbt[:, kc, 0:n], rstd[:, 0:n], ALU.mult, [xkey, "rstdN"], [f"tmpN{kc % 2}"])
        C.act(hout[:, kc, 0:n], tmp[kc % 2][:, 0:n], AF.Identity, [f"tmpN{kc % 2}", "gm", "modT"], [hkey],
              bias=V.modT[:, l, j, sh_off + kc:sh_off + kc + 1], scale=gm[:, l, j, kc:kc + 1])


def phase_out(C, V):
    l, last = V.l, V.last
    m = C.mark()
    wo = V.wo
    M0 = C.alloc("M0", [128, 8, 512], BF16)
    M = [M0, M0]
    xb0 = C.alloc("xbO0", [128, 8, 512], F32)
    xb = [xb0, xb0]
    sq = C.alloc("sqO", [128, 8, 512], BF16)
    rs = C.alloc("rsO", [128, 512], F32); rstd = C.alloc("rstdO", [128, 512], F32)
    tmp = [C.alloc(f"tmpO{i}", [128, 512], F32) for i in range(2)]
    h20 = C.alloc("h2O0", [128, 8, 512], BF16)
    h2 = [h20, h20]
    bl = V.blocks(with_ctx=not last)

    def load(bi):
        kind, t0, n = bl[bi]
        kd = V.kinds[kind]
        C.dma("sp", M[bi % 2][:, :, 0:n], V.tview(kd["mix"], t0, n), [], ["M0"])
    for bi, (kind, t0, n) in enumerate(bl):
        pb = bi % 2
        kd = V.kinds[kind]
        j = kd["j"]
        load(bi)
        C.dma("sp", xb[pb][:, :, 0:n], V.tview(kd["xT"], t0, n), [("xT", kind, t0)], ["xbO0"])
        for oc in range(8):
            b = C.bank()
            for kc in range(8):
                C.mm(C.ps[b][:, 0:n], wo[:, kc, oc * 128:(oc + 1) * 128], M[pb][:, kc, 0:n], kc == 0, kc == 7,
                     ["M0", ("wo", kc)], [f"ps{b}"])
            C.stt("dve", xb[pb][:, oc, 0:n], C.ps[b][:, 0:n], V.modT[:, l, j, 16 + oc:17 + oc], xb[pb][:, oc, 0:n],
                  ALU.mult, ALU.add, [f"ps{b}", "modT", "xbO0"], ["xbO0"])
        C.dma("sp", V.tview(kd["xT"], t0, n), xb[pb][:, :, 0:n], ["xbO0"], [("xT", kind, t0)])
        _norm_mod(C, V, xb[pb], "xbO0", n, l, j, V.gm2, 24, sq, rs, rstd, tmp, h2[pb], "h2O0")
        C.dma("sp", V.tview(kd["h2"], t0, n), h2[pb][:, :, 0:n], ["h2O0"], [])
    C.reset(m)


def phase_mlp(C, V):
    l, last = V.l, V.last
    m = C.mark()
    w1 = V.w1
    w2 = V.w2
    h2 = [C.alloc(f"h2M{i}", [128, 8, 512], BF16) for i in range(2)]
    x1 = C.alloc("x1M", [128, 8, 512], F32)
    hid = C.alloc("hid", [128, 32, 512], BF16)
    rt = [C.alloc(f"rt{i}", [128, 512], F32) for i in range(2)]
    bl = V.blocks(with_ctx=not last)

    def load(bi):
        kind, t0, n = bl[bi]
        C.dma("sp", h2[bi % 2][:, :, 0:n], V.tview(V.kinds[kind]["h2"], t0, n), [], [f"h2M{bi % 2}"])
    load(0)
    for bi, (kind, t0, n) in enumerate(bl):
        pb = bi % 2
        kd = V.kinds[kind]
        j = kd["j"]
        if bi + 1 < len(bl):
            load(bi + 1)
        C.dma("sp", x1[:, :, 0:n], V.tview(kd["xT"], t0, n), [], ["x1M"])
        for jc in range(32):
            b = C.bank()
            for kc in range(8):
                C.mm(C.ps[b][:, 0:n], w1[:, kc, jc * 128:(jc + 1) * 128], h2[pb][:, kc, 0:n], kc == 0, kc == 7,
                     [f"h2M{pb}", ("w1", kc)], [f"ps{b}"])
            C.act(rt[jc % 2][:, 0:n], C.ps[b][:, 0:n], AF.Relu, [f"ps{b}"], [f"rt{jc % 2}"])
            C.tt("pool" if jc % 2 else "dve", hid[:, jc, 0:n], rt[jc % 2][:, 0:n], rt[jc % 2][:, 0:n], ALU.mult,
                 [f"rt{jc % 2}"], [("hid", jc % 2)])
        for oc in range(8):
            b = C.bank()
            for jc in range(32):
                C.mm(C.ps[b][:, 0:n], w2[:, jc, oc * 128:(oc + 1) * 128], hid[:, jc, 0:n], jc == 0, jc == 31,
                     [("hid", 0), ("hid", 1), ("w2", jc)], [f"ps{b}"])
            C.stt("dve", x1[:, oc, 0:n], C.ps[b][:, 0:n], V.modT[:, l, j, 40 + oc:41 + oc], x1[:, oc, 0:n],
                  ALU.mult, ALU.add, [f"ps{b}", "modT", "x1M"], ["x1M"])
        C.dma("sp", V.tview(kd["xT"], t0, n), x1[:, :, 0:n], ["x1M"], [])
    C.reset(m)


def phase_final(C, V):
    m = C.mark()
    xb = [C.alloc(f"xbF{i}", [128, 8, 512], F32) for i in range(2)]
    sq = C.alloc("sqF", [128, 8, 512], BF16)
    rs = C.alloc("rsF", [128, 512], F32); rstd = C.alloc("rstdF", [128, 512], F32)
    y = C.alloc("yF", [128, 8, 512], F32)
    otok = [C.alloc(f"otok{i}", [128, 4, D], F32) for i in range(2)]
    bl = V.blocks(with_ctx=False)

    def load(bi):
        kind, t0, n = bl[bi]
        C.dma("sp", xb[bi % 2][:, :, 0:n], V.tview(V.xT, t0, n), [], [f"xbF{bi % 2}"])
    load(0)
    cnt = 0
    for bi, (kind, t0, n) in enumerate(bl):
        pb = bi % 2
        if bi + 1 < len(bl):
            load(bi + 1)
        C.act(sq[:, :, 0:n], xb[pb][:, :, 0:n], AF.Square, [f"xbF{pb}"], ["sqF"])
        b = C.bank()
        for kc in range(8):
            C.mm(C.ps[b][:, 0:n], V.ones_bf[:, :], sq[:, kc, 0:n], kc == 0, kc == 7, ["sqF", "ones_bf"], [f"ps{b}"])
        C.act(rs[:, 0:n], C.ps[b][:, 0:n], AF.Sqrt, [f"ps{b}", "eps"], ["rsF"], bias=V.eps_t[:, 0:1], scale=1.0 / D)
        C.recip(rstd[:, 0:n], rs[:, 0:n], ["rsF"], ["rstdF"])
        for kc in range(8):
            C.stt("dve", y[:, kc, 0:n], xb[pb][:, kc, 0:n], V.fngT[:, kc:kc + 1], rstd[:, 0:n], ALU.mult, ALU.mult,
                  [f"xbF{pb}", "rstdF", "fngT"], ["yF"])
        for tt_ in range(n // 128):
            for kc2 in range(0, 8, 4):
                b = C.bank()
                for kc in range(kc2, kc2 + 4):
                    C.tr(C.ps[b][:, (kc - kc2) * 128:(kc - kc2 + 1) * 128], y[:, kc, tt_ * 128:(tt_ + 1) * 128], V.ident[:, :],
                         ["yF", "ident"], [f"ps{b}"])
                C.cp("act" if cnt % 2 else "dve", otok[pb][:, tt_, kc2 * 128:kc2 * 128 + 512], C.ps[b][:, :], [f"ps{b}"], [f"otok{pb}"])
                cnt += 1
        C.dma("sp", V.out_d.ap()[t0:t0 + n, :].rearrange("(tt p) d -> p tt d", p=128), otok[pb][:, 0:n // 128, :], [f"otok{pb}"], [])
    C.reset(m)


_BF = ml_dtypes.bfloat16


def _tables(NT, r):
    S = 128 * NT
    TC = S // 4
    N = S
    tb = {}
    t = np.arange(S)
    row = (t // 64).astype(np.float32)
    col = (t % 64).astype(np.float32)
    inv = (10000.0 ** (-np.arange(0, 16, 2, dtype=np.float32) / 16)).astype(np.float32)
    ang = np.concatenate([row[:, None] * inv, col[:, None] * inv], axis=-1).astype(np.float32)
    cos = np.cos(ang).astype(np.float32)[r * TC:(r + 1) * TC].T
    sin = np.sin(ang).astype(np.float32)[r * TC:(r + 1) * TC].T
    tb["rope1"] = np.ascontiguousarray(np.concatenate([cos, cos], 0))
    tb["rope2"] = np.ascontiguousarray(np.concatenate([-sin, sin], 0))
    n1 = np.arange(NT)[:, None]; k1 = np.arange(NT)[None, :]
    a = 2 * np.pi * ((n1 * k1) % NT) / NT
    tb["fft_cs"] = np.concatenate([np.cos(a), np.sin(a)], 1).astype(_BF)
    n2 = np.arange(128)[:, None, None]
    k1 = np.arange(NT)[None, :, None]
    k2 = np.arange(32)[None, None, :]
    k = k1 + NT * (32 * r + k2)
    a = 2 * np.pi * ((n2 * k) % N) / N
    wc, ws = np.cos(a), np.sin(a)
    tb["fft_r1"] = np.concatenate([wc, ws], 2).reshape(128, NT * 64).astype(_BF)
    tb["fft_r2"] = np.concatenate([-ws, wc], 2).reshape(128, NT * 64).astype(_BF)
    c = np.arange(128)[:, None]; mm_ = np.arange(128)[None, :]
    same = (c // 64) == (mm_ // 64)
    a = 2 * np.pi * (((c % 64) * (mm_ % 64)) % 64) / 64
    for nm, nn in (("", N), ("c", CT)):
        sc = 1.0 / np.sqrt(nn * 64.0)
        tb["ccb" + nm] = (np.where(same, np.cos(a), 0.0) * sc).astype(_BF)
        tb["mscb" + nm] = (np.where(same, -np.sin(a), 0.0) * sc).astype(_BF)
    n = np.arange(CT)[:, None]; kk = np.arange(CT)[None, :]
    a = 2 * np.pi * ((n * kk) % CT) / CT
    tb["w256"] = np.concatenate([np.cos(a), np.sin(a)], 1).astype(_BF)
    tb["ident"] = np.eye(128, dtype=np.float32)
    mk = np.zeros((128, 8), np.float32)
    if r > 0:
        mk[:, r - 1] = 1.0
    if r < 3:
        mk[:, 4 + r + 1] = 1.0
    tb["masks"] = mk
    return tb


_CACHE = {}


def run_model(inputs, NT, L, dbg=None, dump=None):
    key = (NT, L, dbg, tuple(sorted(dump)) if dump else None)
    if key not in _CACHE:
        _CACHE[key] = build_program(NT, L, dbg=dbg, dump=dump)
    nc = _CACHE[key]
    S = 128 * NT
    TC = S // 4
    f32 = lambda a: np.ascontiguousarray(np.asarray(a, dtype=np.float32))
    wnames = ["w_mod", "b_mod", "norm1_g", "w_in", "q_norm_g", "w_uq", "kv_norm_g", "w_ukv", "w_fourier", "w_dw",
              "b_dw", "conv_ln_g", "conv_ln_b", "w_pw2", "w_o", "norm2_g", "w_mlp1", "w_mlp2"]
    shared = {n: f32(inputs[n])[:L] for n in wnames if n != "w_mod"}
    wmod_full = f32(inputs["w_mod"])[:L]
    wmod_sh = [np.ascontiguousarray(wmod_full[:, :, r * 1536:(r + 1) * 1536]) for r in range(4)]
    shared["final_norm_g"] = f32(inputs["final_norm_g"])
    x = f32(inputs["x"]); ctx = f32(inputs["ctx"]); c = f32(inputs["c"]); cc = f32(inputs["c_ctx"])
    in_maps = []
    for core in range(8):
        b, r = core // 4, core % 4
        m = dict(shared)
        m["x_in"] = np.ascontiguousarray(x[b, r * TC:(r + 1) * TC])
        m["ctx_in"] = np.ascontiguousarray(ctx[b])
        m["cvec"] = np.ascontiguousarray(np.stack([c[b], cc], 0))
        m["w_mod"] = wmod_sh[r]
        m.update(_tables(NT, r))
        in_maps.append(m)
    res = run_bass_kernel_spmd(nc, in_maps, core_ids=list(range(8)))
    return res


def kernel(**inputs):
    res = run_model(inputs, 128, 4)
    S = 128 * 128
    TC = S // 4
    out = np.empty((2, S, D), np.float32)
    for core in range(8):
        b, r = core // 4, core % 4
        out[b, r * TC:(r + 1) * TC] = np.asarray(res.results[core]["out"], dtype=np.float32)
    return out
```

```python
import contextlib
import os
SKIP = set(os.environ.get('KSKIP', '').split(','))
from types import SimpleNamespace
import numpy as np
import ml_dtypes
import concourse.bass as bass
import concourse.mybir as mybir
from concourse.bass_utils import run_bass_kernel_spmd

F32 = mybir.dt.float32
BF16 = mybir.dt.bfloat16
AF = mybir.ActivationFunctionType
ALU = mybir.AluOpType

ENGS = ("pe", "act", "dve", "pool", "sp")
EPOCH = 20000
NDMASEM = 12
NCCSEM = 3


class Op:
    __slots__ = ("eng", "fn", "deps", "signal", "dma", "sem", "val", "idx", "dslot", "inc")

    def __init__(self, eng, fn, dma):
        self.eng = eng
        self.fn = fn
        self.deps = []
        self.signal = False
        self.dma = dma
        self.sem = None
        self.val = 0
        self.idx = 0
        self.dslot = None
        self.inc = 16


class Prog:
    def __init__(self):
        self.ops = {e: [] for e in ENGS}
        self.last_w = {}
        self.readers = {}
        self.pending_dma = []
        self.all_last = {e: None for e in ENGS}
        self.persist_w = {}

    def op(self, eng, fn, reads=(), writes=(), dma=False, extra_deps=(), inc=16, persist=False):
        o = Op(eng, fn, dma)
        o.inc = inc
        pr = [k for k in reads if isinstance(k, str) and k.startswith("ps") and k[2:].isdigit()]
        if pr:
            writes = list(writes) + pr
        o.idx = len(self.ops[eng])
        deps = []
        for k in reads:
            w = self.last_w.get(k)
            if w is not None:
                deps.append((w, "raw"))
        for k in writes:
            w = self.last_w.get(k)
            if w is not None:
                deps.append((w, "waw"))
            for r in self.readers.get(k, ()):
                deps.append((r, "war"))
        for d in extra_deps:
            if d is not None:
                deps.append((d, "raw"))
        seen = set()
        for d, kind in deps:
            if d is o or id(d) in seen:
                continue
            need = True
            if d.eng == eng and not d.dma and not dma:
                if eng == "pe":
                    need = False
            if need:
                seen.add(id(d))
                o.deps.append(d)
                d.signal = True
        for k in reads:
            self.readers.setdefault(k, []).append(o)
        for k in writes:
            self.last_w[k] = o
            self.readers[k] = []
        self.ops[eng].append(o)
        if persist:
            for k in writes:
                self.persist_w[k] = o
        else:
            self.all_last[eng] = o
            if dma:
                self.pending_dma.append(o)
        return o

    def barrier(self):
        lasts = [self.all_last[e] for e in ENGS if self.all_last[e] is not None]
        pend = list(self.pending_dma)
        self.pending_dma = []
        for e in ENGS:
            self.op(e, None, extra_deps=lasts + pend)
        self.last_w = dict(self.persist_w)
        self.readers = {}

    def emit(self, nc):
        n_sems_needed = 0
        plan = {}
        for e in ENGS:
            cnt = 0
            epoch = 0
            for o in self.ops[e]:
                if o.dma:
                    continue
                if o.signal:
                    if cnt >= EPOCH:
                        epoch += 1
                        cnt = 0
                    cnt += 1
                    o.sem = (e, epoch)
                    o.val = cnt
            plan[e] = epoch + 1
        for e in ENGS:
            k = 0
            kc_ = 0
            for o in self.ops[e]:
                if o.dma:
                    if o.inc == 16:
                        o.dslot = k % NDMASEM
                        k += 1
                    else:
                        o.dslot = NDMASEM + (kc_ % NCCSEM)
                        kc_ += 1
        with contextlib.ExitStack() as st:
            sems = {}
            for e in ENGS:
                for ep in range(plan[e]):
                    sems[(e, ep)] = st.enter_context(nc.semaphore(f"s_{e}_{ep}"))
            dsems = {}
            for e in ENGS:
                if any(o.dma for o in self.ops[e]):
                    for k in range(NDMASEM + NCCSEM):
                        dsems[(e, k)] = st.enter_context(nc.semaphore(f"d_{e}_{k}"))
            block = st.enter_context(nc.Block())
            engmap = {"pe": block.tensor, "act": block.scalar, "dve": block.vector,
                      "pool": block.gpsimd, "sp": block.sync}

            def make(e):
                def body(eng):
                    waited = {}
                    dcount = {k: 0 for k in range(NDMASEM + NCCSEM)}
                    dlast = {}
                    for o in self.ops[e]:
                        for d in o.deps:
                            if d.dma:
                                key = ("d", d.eng, d.dslot)
                                s = dsems[(d.eng, d.dslot)]
                                v = d.val
                            else:
                                key = d.sem
                                s = sems[d.sem]
                                v = d.val
                            if waited.get(key, 0) >= v:
                                continue
                            waited[key] = v
                            eng.wait_ge(s, v)
                        if o.dma:
                            key = ("d", e, o.dslot)
                            prev = dcount[o.dslot]
                            if prev > 0 and waited.get(key, 0) < prev:
                                eng.wait_ge(dsems[(e, o.dslot)], prev)
                                waited[key] = prev
                            ins = o.fn(eng)
                            dcount[o.dslot] = prev + o.inc
                            o.val = prev + o.inc
                            ins.then_inc(dsems[(e, o.dslot)], o.inc)
                        else:
                            if o.fn is None:
                                if o.signal:
                                    eng.nop().then_inc(sems[o.sem], 1)
                                continue
                            ins = o.fn(eng)
                            if o.signal:
                                ins.then_inc(sems[o.sem], 1)
                return body

            for e in ENGS:
                dc = {k: 0 for k in range(NDMASEM + NCCSEM)}
                for o in self.ops[e]:
                    if o.dma:
                        dc[o.dslot] += o.inc
                        o.val = dc[o.dslot]
            for e in ENGS:
                if self.ops[e]:
                    engmap[e](make(e))


D = 1024
KC = 8
DIN = 1184
DFF = 4096
NH = 8
CT = 256
EPS = 1e-6
ATTN_SCALE = 96.0 ** -0.5
SB_BASE = 16512
SB_TOP = 229376 - 64


def _dtsize(dt):
    return 4 if dt == F32 else 2


class Ctx:
    def __init__(self, nc):
        self.nc = nc
        self.P = Prog()
        self.off = SB_BASE
        self.limit = SB_TOP
        self.uid = 0
        self.ps = [nc.alloc_psum_tensor(f"ps{i}", [128, 512], F32) for i in range(8)]
        self.psi = 0

    def alloc(self, name, shape, dt):
        n = 1
        for s in shape[1:]:
            n *= s
        nbytes = n * _dtsize(dt)
        off = (self.off + 63) // 64 * 64
        assert off + nbytes <= self.limit, (name, off, nbytes, self.limit)
        self.off = off + nbytes
        self.uid += 1
        return self.nc.alloc_sbuf_tensor_at(f"{name}_{self.uid}", list(shape), dt, offset=off)

    def alloc_at(self, name, shape, dt, off):
        self.uid += 1
        return self.nc.alloc_sbuf_tensor_at(f"{name}_{self.uid}", list(shape), dt, offset=off)

    def mark(self):
        return self.off

    def reset(self, m):
        self.P.barrier()
        self.off = m

    def bank(self, lo=0, hi=8):
        i = lo + (self.psi % (hi - lo))
        self.psi += 1
        return i

    def mm(self, out, lhsT, rhs, start, stop, r, w):
        return self.P.op("pe", lambda e: e.matmul(out, lhsT=lhsT, rhs=rhs, start=start, stop=stop),
                         reads=r, writes=w)

    def tr(self, out, in_, idn, r, w):
        return self.P.op("pe", lambda e: e.transpose(out, in_, idn), reads=r, writes=w)

    def act(self, out, in_, func, r, w, bias=None, scale=None):
        kw = {}
        if bias is not None:
            kw["bias"] = bias
        if scale is not None:
            kw["scale"] = scale
        return self.P.op("act", lambda e: e.activation(out, in_, func, **kw), reads=r, writes=w)

    def tt(self, eng, out, in0, in1, op, r, w):
        return self.P.op(eng, lambda e: e.tensor_tensor(out, in0, in1, op), reads=r, writes=w)

    def ts(self, eng, out, in0, s1, s2, op0, op1, r, w):
        if s2 is None:
            return self.P.op(eng, lambda e: e.tensor_scalar(out, in0, s1, None, op0), reads=r, writes=w)
        return self.P.op(eng, lambda e: e.tensor_scalar(out, in0, s1, s2, op0, op1), reads=r, writes=w)

    def stt(self, eng, out, in0, scalar, in1, op0, op1, r, w):
        return self.P.op(eng, lambda e: e.scalar_tensor_tensor(out, in0, scalar, in1, op0, op1),
                         reads=r, writes=w)

    def cp(self, eng, out, in_, r, w):
        if eng == "act":
            return self.P.op("act", lambda e: e.copy(out, in_), reads=r, writes=w)
        return self.P.op(eng, lambda e: e.tensor_copy(out, in_), reads=r, writes=w)

    def recip(self, out, in_, r, w):
        return self.P.op("dve", lambda e: e.reciprocal(out, in_), reads=r, writes=w)

    def memset(self, eng, ap, val, w):
        return self.P.op(eng, lambda e: e.memset(ap, val), writes=w)

    def dma(self, q, out, in_, r, w, persist=False):
        return self.P.op(q, lambda e: e.dma_start(out=out, in_=in_), reads=r, writes=w, dma=True, persist=persist)

    def allgather(self, src, dst, r, w, persist=False):
        return self.P.op("pool", lambda e: e.collective_compute(
            "AllGather", ALU.bypass, replica_groups=[[0, 1, 2, 3], [4, 5, 6, 7]],
            ins=[src], outs=[dst]), reads=r, writes=w, dma=True, inc=1, persist=persist)


def build_program(NT, L, dbg=None, dump=None):
    S = 128 * NT
    TC = S // 4
    NB = TC // 512
    TCt = TC // 128
    NKT = 2 + NT
    NK = NKT * 128
    assert TC % 512 == 0

    nc = bass.Bass("TRN2", target_bir_lowering=False)
    C = Ctx(nc)
    P = C.P

    def din(name, shape, dt=F32):
        return nc.dram_tensor(name, list(shape), dt, kind="ExternalInput")

    def dtmp(name, shape, dt):
        if dump and name in dump:
            return nc.dram_tensor(name, list(shape), dt, kind="ExternalOutput")
        return nc.dram_tensor(name, list(shape), dt)

    x_in = din("x_in", [TC, D])
    ctx_in = din("ctx_in", [CT, D])
    cvec = din("cvec", [2, D])
    w_mod = din("w_mod", [L, D, 1536]); b_mod = din("b_mod", [L, 6 * D])
    norm1_g = din("norm1_g", [L, D]); w_in = din("w_in", [L, D, DIN])
    q_norm_g = din("q_norm_g", [L, 256]); w_uq = din("w_uq", [L, 256, 768])
    kv_norm_g = din("kv_norm_g", [L, 128]); w_ukv = din("w_ukv", [L, 128, 1024])
    w_fourier = din("w_fourier", [L, 256, 256]); w_dw = din("w_dw", [L, 31, 256])
    b_dw = din("b_dw", [L, 256]); conv_ln_g = din("conv_ln_g", [L, 256]); conv_ln_b = din("conv_ln_b", [L, 256])
    w_pw2 = din("w_pw2", [L, 256, 256]); w_o = din("w_o", [L, D, D]); norm2_g = din("norm2_g", [L, D])
    w_mlp1 = din("w_mlp1", [L, D, DFF]); w_mlp2 = din("w_mlp2", [L, DFF, D])
    final_norm_g = din("final_norm_g", [D])
    rope1 = din("rope1", [32, TC]); rope2 = din("rope2", [32, TC])
    fft_cs = din("fft_cs", [NT, 2 * NT], BF16)
    fft_r1 = din("fft_r1", [128, NT * 64], BF16); fft_r2 = din("fft_r2", [128, NT * 64], BF16)
    ccb_d = din("ccb", [128, 128], BF16); mscb_d = din("mscb", [128, 128], BF16)
    w256_d = din("w256", [256, 512], BF16)
    ccbc_d = din("ccbc", [128, 128], BF16); mscbc_d = din("mscbc", [128, 128], BF16)
    ident_d = din("ident", [128, 128]); masks_d = din("masks", [128, 8])
    out_d = nc.dram_tensor("out", [TC, D], F32, kind="ExternalOutput")

    xT = dtmp("xT", [KC, 128, TC], F32); cT = dtmp("cT", [KC, 128, CT], F32)
    h2x = dtmp("h2x", [KC, 128, TC], BF16); h2c = dtmp("h2c", [KC, 128, CT], BF16)
    qx = dtmp("qx", [NH, 96, TC], BF16); qc = dtmp("qc", [NH, 96, CT], BF16)
    TCq = min(TC, 1024); NKV = TC // TCq
    FR = min(2 * TC, 2048); NF = 2 * TC // FR
    kv_own = [dtmp(f"kv_own{i}", [160, TCq], BF16) for i in range(NKV)]
    kv_all = [dtmp(f"kv_all{i}", [640, TCq], BF16) for i in range(NKV)]
    kvc = dtmp("kvc", [160, CT], BF16)
    f_own = [dtmp(f"f_own{i}", [FR, 128], BF16) for i in range(NF)]
    f_all = [dtmp(f"f_all{i}", [4 * FR, 128], BF16) for i in range(NF)]
    fc = dtmp("fc", [CT, 256], BF16)
    upx = dtmp("upx", [2, 128, TC + 32], BF16); upc = dtmp("upc", [2, 128, CT + 32], BF16)
    e_own = dtmp("e_own", [256, 32], BF16); e_all = dtmp("e_all", [1024, 32], BF16)
    mixx = dtmp("mixx", [KC, 128, TC], BF16); mixc = dtmp("mixc", [KC, 128, CT], BF16)

    kinds = {
        "x": dict(j=0, T=TC, xT=xT, h2=h2x, q=qx, kv=None, up=upx, mix=mixx),
        "c": dict(j=1, T=CT, xT=cT, h2=h2c, q=qc, kv=kvc, up=upc, mix=mixc),
    }

    def blocks(with_ctx=True):
        bl = []
        if with_ctx:
            bl.append(("c", 0, CT))
        for b in range(NB):
            bl.append(("x", b * 512, 512))
        return bl

    def tview(dt_, t0, n):
        return dt_.ap().rearrange("kc p n -> p kc n")[:, :, t0:t0 + n]

    ident = C.alloc("ident", [128, 128], F32)
    ones_bf = C.alloc("ones_bf", [128, 128], BF16)
    ones_f = C.alloc("ones_f", [128, 128], F32)
    eps_t = C.alloc("eps", [128, 1], F32)
    masks = C.alloc("masks", [128, 8], F32)
    modT = C.alloc("modT", [128, L, 2, 48], F32)
    bmodT = C.alloc("bmodT", [128, L, 48], F32)
    n1gT = C.alloc("n1gT", [128, L, 8], F32); n2gT = C.alloc("n2gT", [128, L, 8], F32)
    gm1 = C.alloc("gm1", [128, L, 2, 8], F32); gm2 = C.alloc("gm2", [128, L, 2, 8], F32)
    qgT = C.alloc("qgT", [128, L, 2], F32); kvgT = C.alloc("kvgT", [128, L, 1], F32)
    bdwT = C.alloc("bdwT", [128, L, 2], F32); lngT = C.alloc("lngT", [128, L, 2], F32)
    lnbT = C.alloc("lnbT", [128, L, 2], F32); wdwT = C.alloc("wdwT", [128, L, 62], F32)
    fngT = C.alloc("fngT", [128, 8], F32); cvT = C.alloc("cvT", [128, 16], F32)
    scT = C.alloc("scT", [128, 16], BF16)
    wukv_t = C.alloc("wukv", [128, 1024], BF16)
    PERSIST = C.mark()

    C.dma("sp", ident[:, :], ident_d.ap(), [], ["ident"])
    C.dma("sp", masks[:, :], masks_d.ap(), [], ["masks"])
    C.memset("dve", ones_bf[:, :], 1.0, ["ones_bf"])
    C.memset("dve", ones_f[:, :], 1.0, ["ones_f"])
    C.memset("dve", eps_t[:, :], EPS, ["eps"])

    stg = C.alloc("stg", [128, 128], F32)
    stg2 = C.alloc("stg2", [128, 128], F32)

    def rows128(ap1d, n):
        return ap1d.rearrange("(r p) -> r p", p=128)

    def vec_transpose(stage, key, nrows, dsts):
        b = C.bank()
        C.tr(C.ps[b][:, 0:nrows], stage[0:nrows, :], ident[0:nrows, 0:nrows], [key, "ident"], [f"ps{b}"])
        for dst, c0, c1, wk in dsts:
            C.cp("dve", dst, C.ps[b][:, c0:c1], [f"ps{b}"], [wk])

    for l in range(L):
        specs = [(rows128(b_mod.ap()[l], 48), 48), (rows128(norm1_g.ap()[l], 8), 8),
                 (rows128(norm2_g.ap()[l], 8), 8), (rows128(q_norm_g.ap()[l], 2), 2),
                 (rows128(kv_norm_g.ap()[l], 1), 1), (rows128(b_dw.ap()[l], 2), 2),
                 (rows128(conv_ln_g.ap()[l], 2), 2), (rows128(conv_ln_b.ap()[l], 2), 2)]
        r0 = 0
        for src, n in specs:
            C.dma("sp", stg[r0:r0 + n, :], src, [], ["stg"])
            r0 += n
        vec_transpose(stg, "stg", 73, [
            (bmodT[:, l, :], 0, 48, "bmodT"), (n1gT[:, l, :], 48, 56, "n1gT"), (n2gT[:, l, :], 56, 64, "n2gT"),
            (qgT[:, l, :], 64, 66, "qgT"), (kvgT[:, l, :], 66, 67, "kvgT"), (bdwT[:, l, :], 67, 69, "bdwT"),
            (lngT[:, l, :], 69, 71, "lngT"), (lnbT[:, l, :], 71, 73, "lnbT")])
        C.dma("sp", stg2[0:62, :], w_dw.ap()[l].rearrange("k (j p) -> (k j) p", p=128), [], ["stg2"])
        vec_transpose(stg2, "stg2", 62, [(wdwT[:, l, :], 0, 62, "wdwT")])
    C.dma("sp", stg[0:16, :], cvec.ap().rearrange("j (kc p) -> (j kc) p", p=128), [], ["stg"])
    C.dma("sp", stg[16:24, :], rows128(final_norm_g.ap(), 8), [], ["stg"])
    vec_transpose(stg, "stg", 24, [(cvT[:, :], 0, 16, "cvT"), (fngT[:, :], 16, 24, "fngT")])
    C.act(scT[:, :], cvT[:, :], AF.Silu, ["cvT"], ["scT"])

    m0 = C.mark()
    wm = [C.alloc(f"wm{i}", [128, 8, 1536], BF16) for i in range(2)]
    mpart = C.alloc("mpart", [128, L, 2, 12], F32)
    mp_own = dtmp("mp_own", [128, L * 24], F32)
    mp_all = dtmp("mp_all", [512, L * 24], F32)
    for l in range(L):
        wmt = wm[l % 2]
        for kc in range(8):
            C.dma("pool", wmt[:, kc, :], w_mod.ap()[l, kc * 128:(kc + 1) * 128, :], [], [("wm", l % 2, kc)])
        bm = C.bank()
        for oc in range(12):
            for kc in range(8):
                C.mm(C.ps[bm][:, oc * 2:oc * 2 + 2], wmt[:, kc, oc * 128:(oc + 1) * 128],
                     bass.AP(scT, kc, [[16, 128], [8, 2]]), kc == 0, kc == 7,
                     [("wm", l % 2, kc), "scT"], [f"ps{bm}"])
        for j in range(2):
            C.cp("dve", mpart[:, l, j, :], bass.AP(C.ps[bm], j, [[512, 128], [2, 12]]), [f"ps{bm}"], ["mpart"])
    C.dma("sp", mp_own.ap(), mpart[:, :, :, :].rearrange("p l j o -> p (l j o)"), ["mpart"], ["mp_own"])
    C.allgather(mp_own.ap(), mp_all.ap(), ["mp_own"], ["mp_all"])
    for r in range(4):
        C.dma("sp", modT[:, :, :, r * 12:(r + 1) * 12],
              mp_all.ap()[r * 128:(r + 1) * 128, :].rearrange("p (l j o) -> p l j o", l=L, j=2), ["mp_all"], [("modraw", r)])
    mrk = [("modraw", r) for r in range(4)]
    for l in range(L):
        for j in range(2):
            C.tt("dve", modT[:, l, j, :], modT[:, l, j, :], bmodT[:, l, :], ALU.add, mrk + ["bmodT"], ["modT"])
        for j in range(2):
            C.stt("dve", gm1[:, l, j, :], modT[:, l, j, 8:16], 1.0, n1gT[:, l, :], ALU.add, ALU.mult,
                  ["modT", "n1gT"], ["gm1"])
            C.stt("dve", gm2[:, l, j, :], modT[:, l, j, 32:40], 1.0, n2gT[:, l, :], ALU.add, ALU.mult,
                  ["modT", "n2gT"], ["gm2"])
    C.reset(m0)

    m0 = C.mark()
    xtok = [C.alloc(f"xtok{i}", [128, 4, D], F32) for i in range(2)]
    xbt = [C.alloc(f"xbt{i}", [128, 8, 512], F32) for i in range(2)]
    for bi, (kind, t0, n) in enumerate(blocks()):
        pb = bi % 2
        src = ctx_in if kind == "c" else x_in
        ntt = n // 128
        C.dma("sp", xtok[pb][:, 0:ntt, :], src.ap()[t0:t0 + n, :].rearrange("(tt p) d -> p tt d", p=128),
              [], [f"xtok{pb}"])
        for kc in range(8):
            b = C.bank()
            for tt_ in range(ntt):
                C.tr(C.ps[b][:, tt_ * 128:(tt_ + 1) * 128], xtok[pb][:, tt_, kc * 128:(kc + 1) * 128], ident[:, :],
                     [f"xtok{pb}", "ident"], [f"ps{b}"])
            C.cp("act" if kc % 2 else "dve", xbt[pb][:, kc, 0:n], C.ps[b][:, 0:n], [f"ps{b}"], [f"xbt{pb}"])
        C.dma("sp", tview(kinds[kind]["xT"], t0, n), xbt[pb][:, :, 0:n], [f"xbt{pb}"], [("xT", kind, t0)])
    zt = C.alloc("zt", [128, 2, 16], BF16)
    C.memset("dve", zt[:, :, :], 0.0, ["zt"])
    C.dma("sp", upc.ap().rearrange("j p n -> p j n")[:, :, 0:16], zt[:, :, :], ["zt"], ["upc_h"])
    C.dma("sp", upc.ap().rearrange("j p n -> p j n")[:, :, CT + 16:CT + 32], zt[:, :, :], ["zt"], ["upc_h"])
    C.reset(m0)

    def rms_stats(src3, n, nk, inv_n, sqt, sskey_r, tag):
        C.act(sqt[:, 0:nk, 0:n], src3, AF.Square, [sskey_r], ["sq" + tag])
        b = C.bank()
        for kc in range(nk):
            C.mm(C.ps[b][:, 0:n], ones_bf[:, :], sqt[:, kc, 0:n], kc == 0, kc == nk - 1,
                 ["sq" + tag, "ones_bf"], [f"ps{b}"])
        return b

    for l in range(L):
        last = (l == L - 1)
        if dbg == "P":
            break
        mA = C.mark()
        w_in_t = C.alloc("w_in_t", [128, 8, DIN], BF16)
        krsw = C.alloc("krsw", [128, 8, 32], BF16)
        w_uq_t = C.alloc("w_uq_t", [128, 2, 768], BF16)
        w_uq_s = C.alloc("w_uq_s", [128, 2, 768], BF16)
        for kc in range(8):
            C.dma("pool", w_in_t[:, kc, :], w_in.ap()[l, kc * 128:(kc + 1) * 128, :], [], [("w_in", kc)])
        C.dma("pool", wukv_t[:, :], w_ukv.ap()[l], [], ["wukv"], persist=True)
        C.dma("pool", krsw[:, :, 0:16], w_in.ap()[l].rearrange("(kc p) n -> p kc n", p=128)[:, :, 656:672], [], ["krsw0"])
        C.dma("pool", krsw[:, :, 16:32], w_in.ap()[l].rearrange("(kc p) n -> p kc n", p=128)[:, :, 640:656], [], ["krsw1"])
        uq_v = w_uq.ap()[l].rearrange("(j p) n -> p j n", p=128)
        C.dma("pool", w_uq_t[:, :, :], uq_v, [], ["w_uq"])
        C.dma("pool", w_uq_s[:, :, :], uq_v, [], ["w_uq_s"])
        uq_v4 = w_uq.ap()[l].rearrange("(j p) (h e) -> p j h e", p=128, e=96)
        s4 = w_uq_s[:, :, :].rearrange("p j (h e) -> p j h e", e=96)
        for j in range(2):
            C.dma("pool", s4[:, j, :, 64:80], uq_v4[:, j, :, 80:96], [], ["w_uq_s"])
            C.dma("pool", s4[:, j, :, 80:96], uq_v4[:, j, :, 64:80], [], ["w_uq_s"])

        xb = [C.alloc(f"xbA{i}", [128, 8, 512], F32) for i in range(2)]
        sq = C.alloc("sqA", [128, 8, 512], BF16)
        rs = C.alloc("rsA", [128, 512], F32)
        rstd = C.alloc("rstdA", [128, 512], F32)
        tmpA = [C.alloc(f"tmpA{i}", [128, 512], F32) for i in range(2)]
        hA = [C.alloc(f"hA{i}", [128, 8, 512], BF16) for i in range(2)]
        ftok = C.alloc("ftok", [128, 4, 256], BF16)
        cq = C.alloc("cq", [128, 2, 512], F32)
        sqq = C.alloc("sqq", [128, 2, 512], BF16)
        rq = C.alloc("rq", [128, 512], F32)
        cqn = C.alloc("cqn", [128, 2, 512], BF16)
        qst = C.alloc("qst", [128, NH, 512], BF16)
        rt1 = C.alloc("rt1", [128, 512], F32)
        rt2 = C.alloc("rt2", [128, 512], F32)
        rp1 = C.alloc("rp1", [128, 512], F32)
        rp2 = C.alloc("rp2", [128, 512], F32)
        ckv = C.alloc("ckv", [128, 512], F32)
        ckvn = C.alloc("ckvn", [128, 512], BF16)
        krr = C.alloc("krr", [128, 512], BF16)
        sg = C.alloc("sg", [128, 512], F32)
        ut = C.alloc("ut", [128, 2, 512], BF16)

        bl = blocks()
        def loadA(bi):
            kind, t0, n = bl[bi]
            C.dma("sp", xb[bi % 2][:, :, 0:n], tview(kinds[kind]["xT"], t0, n), [("xT", kind, t0)], [f"xbA{bi % 2}"])
        rsN = C.alloc("rsN", [128, 512], F32)

        def normA(bi):
            kind, t0, n = bl[bi]
            pb = bi % 2
            j = kinds[kind]["j"]
            b = rms_stats(xb[pb][:, :, 0:n], n, 8, None, sq, f"xbA{pb}", "A")
            C.act(rsN[:, 0:n], C.ps[b][:, 0:n], AF.Sqrt, [f"ps{b}", "eps"], ["rsN"], bias=eps_t[:, 0:1], scale=1.0 / D)
            C.recip(rstd[:, 0:n], rsN[:, 0:n], ["rsN"], ["rstdA"])
            for kc in range(8):
                C.tt("dve", tmpA[kc % 2][:, 0:n], xb[pb][:, kc, 0:n], rstd[:, 0:n], ALU.mult,
                     [f"xbA{pb}", "rstdA"], [f"tmpA{kc % 2}"])
                C.act(hA[pb][:, kc, 0:n], tmpA[kc % 2][:, 0:n], AF.Identity, [f"tmpA{kc % 2}", "gm1", "modT"], [f"hA{pb}"],
                      bias=modT[:, l, j, kc:kc + 1], scale=gm1[:, l, j, kc:kc + 1])

        loadA(0)
        if len(bl) > 1:
            loadA(1)
        normA(0)
        pend_ag = []
        defer_ag = []
        for bi, (kind, t0, n) in enumerate(bl):
            pb = bi % 2
            kd = kinds[kind]
            j = kd["j"]
            if bi + 1 < len(bl):
                normA(bi + 1)
            if bi + 2 < len(bl):
                loadA(bi + 2)
            for ag in pend_ag:
                C.allgather(*ag)
            pend_ag = []
            if kind == "x":
                C.dma("sp", rp1[64:96, 0:n], rope1.ap()[:, t0:t0 + n], [], ["rp1q"])
                C.dma("sp", rp2[64:96, 0:n], rope2.ap()[:, t0:t0 + n], [], ["rp2q"])
                C.dma("sp", rp1[0:32, 0:n], rope1.ap()[:, t0:t0 + n], [], ["rp1k"])
                C.dma("sp", rp2[0:32, 0:n], rope2.ap()[:, t0:t0 + n], [], ["rp2k"])
            hk = f"hA{pb}"
            h = hA[pb]
            do_rest = not (kind == "c" and last)
            if do_rest and 'f' not in SKIP:
                ntt = n // 128
                for t2 in range(0, ntt, 2):
                    b = C.bank()
                    for tt_ in range(t2, min(t2 + 2, ntt)):
                        for kc in range(8):
                            C.mm(C.ps[b][:, (tt_ - t2) * 256:(tt_ - t2 + 1) * 256], h[:, kc, tt_ * 128:(tt_ + 1) * 128],
                                 w_in_t[:, kc, 0:256], kc == 0, kc == 7, [hk, ("w_in", kc)], [f"ps{b}"])
                    nn = min(2, ntt - t2)
                    C.cp("act", ftok[:, t2:t2 + nn, :], C.ps[b][:, 0:nn * 256].rearrange("p (t c) -> p t c", c=256),
                         [f"ps{b}"], ["ftok"])
                if kind == "x":
                    for gp in range(2):
                        g0 = gp * TC + t0
                        C.dma("sp", f_own[g0 // FR].ap()[g0 % FR:g0 % FR + n, :].rearrange("(tt p) c -> p tt c", p=128),
                              ftok[:, 0:ntt, gp * 128:(gp + 1) * 128], ["ftok"], [("fo", g0 // FR, g0 % FR)])
                else:
                    C.dma("sp", fc.ap().rearrange("(tt p) c -> p tt c", p=128), ftok[:, 0:ntt, :], ["ftok"], ["fc"])
            if do_rest and 'q' not in SKIP:
                for jq in range(2):
                    b = C.bank()
                    for kc in range(8):
                        C.mm(C.ps[b][:, 0:n], w_in_t[:, kc, 256 + jq * 128:256 + (jq + 1) * 128], h[:, kc, 0:n],
                             kc == 0, kc == 7, [hk, ("w_in", kc)], [f"ps{b}"])
                    C.cp("dve", cq[:, jq, 0:n], C.ps[b][:, 0:n], [f"ps{b}"], ["cq"])
                    C.act(sqq[:, jq, 0:n], C.ps[b][:, 0:n], AF.Square, [f"ps{b}"], ["sqq"])
                b = C.bank()
                for jq in range(2):
                    C.mm(C.ps[b][:, 0:n], ones_bf[:, :], sqq[:, jq, 0:n], jq == 0, jq == 1, ["sqq", "ones_bf"], [f"ps{b}"])
                C.act(rs[:, 0:n], C.ps[b][:, 0:n], AF.Sqrt, [f"ps{b}", "eps"], ["rsA"], bias=eps_t[:, 0:1], scale=1.0 / 256)
                C.recip(rq[:, 0:n], rs[:, 0:n], ["rsA"], ["rq"])
                for jq in range(2):
                    C.stt("dve", cqn[:, jq, 0:n], cq[:, jq, 0:n], qgT[:, l, jq:jq + 1], rq[:, 0:n], ALU.mult, ALU.mult,
                          ["cq", "rq", "qgT"], ["cqn"])
                for hh in range(NH):
                    ba = C.bank()
                    for jq in range(2):
                        C.mm(C.ps[ba][0:96, 0:n], w_uq_t[:, jq, hh * 96:(hh + 1) * 96], cqn[:, jq, 0:n], jq == 0, jq == 1,
                             ["cqn", "w_uq"], [f"ps{ba}"])
                    C.cp("act", qst[0:64, hh, 0:n], C.ps[ba][0:64, 0:n], [f"ps{ba}"], [("qst", "n")])
                    if kind == "x":
                        bb = C.bank()
                        for jq in range(2):
                            C.mm(C.ps[bb][0:96, 0:n], w_uq_s[:, jq, hh * 96:(hh + 1) * 96], cqn[:, jq, 0:n], jq == 0, jq == 1,
                                 ["cqn", "w_uq_s"], [f"ps{bb}"])
                        C.tt("dve", rt1[64:96, 0:n], C.ps[ba][64:96, 0:n], rp1[64:96, 0:n], ALU.mult,
                             [f"ps{ba}", "rp1q"], ["rt1"])
                        C.tt("dve", rt2[64:96, 0:n], C.ps[bb][64:96, 0:n], rp2[64:96, 0:n], ALU.mult,
                             [f"ps{bb}", "rp2q"], ["rt2"])
                        C.tt("pool", qst[64:96, hh, 0:n], rt1[64:96, 0:n], rt2[64:96, 0:n], ALU.add,
                             ["rt1", "rt2"], [("qst", "r")])
                    else:
                        C.cp("act", qst[64:96, hh, 0:n], C.ps[ba][64:96, 0:n], [f"ps{ba}"], [("qst", "r")])
                C.dma("sp", kd["q"].ap().rearrange("h e n -> e h n")[:, :, t0:t0 + n], qst[0:96, :, 0:n],
                      [("qst", "n"), ("qst", "r")], [("q", kind)])
            if 'kv' in SKIP:
                continue
            b = C.bank()
            for kc in range(8):
                C.mm(C.ps[b][:, 0:n], w_in_t[:, kc, 512:640], h[:, kc, 0:n], kc == 0, kc == 7, [hk, ("w_in", kc)], [f"ps{b}"])
            C.cp("dve", ckv[:, 0:n], C.ps[b][:, 0:n], [f"ps{b}"], ["ckv"])
            C.act(sqq[:, 0, 0:n], C.ps[b][:, 0:n], AF.Square, [f"ps{b}"], ["sqq"])
            b = C.bank()
            C.mm(C.ps[b][:, 0:n], ones_bf[:, :], sqq[:, 0, 0:n], True, True, ["sqq", "ones_bf"], [f"ps{b}"])
            C.act(rs[:, 0:n], C.ps[b][:, 0:n], AF.Sqrt, [f"ps{b}", "eps"], ["rsA"], bias=eps_t[:, 0:1], scale=1.0 / 128)
            C.recip(rq[:, 0:n], rs[:, 0:n], ["rsA"], ["rq"])
            C.stt("dve", ckvn[:, 0:n], ckv[:, 0:n], kvgT[:, l, 0:1], rq[:, 0:n], ALU.mult, ALU.mult,
                  ["ckv", "rq", "kvgT"], ["ckvn"])
            kvdst = kvc.ap()[:, 0:n] if kind == "c" else kv_own[t0 // TCq].ap()[:, t0 % TCq:t0 % TCq + n]
            C.dma("sp", kvdst[0:128, :], ckvn[:, 0:n], ["ckvn"], [("kvo", kind, t0, 0)])
            if 'kr' in SKIP:
                continue
            ba = C.bank()
            for kc in range(8):
                C.mm(C.ps[ba][0:32, 0:n], w_in_t[:, kc, 640:672], h[:, kc, 0:n], kc == 0, kc == 7, [hk, ("w_in", kc)], [f"ps{ba}"])
            if kind == "x":
                bb = C.bank()
                for kc in range(8):
                    C.mm(C.ps[bb][0:32, 0:n], krsw[:, kc, :], h[:, kc, 0:n], kc == 0, kc == 7, [hk, "krsw0", "krsw1"], [f"ps{bb}"])
                C.tt("dve", rt1[0:32, 0:n], C.ps[ba][0:32, 0:n], rp1[0:32, 0:n], ALU.mult, [f"ps{ba}", "rp1k"], ["rt1"])
                C.tt("dve", rt2[0:32, 0:n], C.ps[bb][0:32, 0:n], rp2[0:32, 0:n], ALU.mult, [f"ps{bb}", "rp2k"], ["rt2"])
                C.tt("pool", krr[0:32, 0:n], rt1[0:32, 0:n], rt2[0:32, 0:n], ALU.add, ["rt1", "rt2"], ["krr"])
            else:
                C.cp("act", krr[0:32, 0:n], C.ps[ba][0:32, 0:n], [f"ps{ba}"], ["krr"])
            C.dma("sp", kvdst[128:160, :], krr[0:32, 0:n], ["krr"], [("kvo", kind, t0, 1)])
            if kind == "x" and 'ag' not in SKIP:
                if (t0 + n) % TCq == 0:
                    ci = t0 // TCq
                    rk = [("kvo", "x", tt0, pp) for tt0 in range(ci * TCq, (ci + 1) * TCq, 512) for pp in range(2)]
                    pend_ag.append((kv_own[ci].ap(), kv_all[ci].ap(), rk, []))
                if do_rest and 'f' not in SKIP:
                    for gp in range(2):
                        g1 = gp * TC + t0 + n
                        if g1 % FR == 0:
                            ci = g1 // FR - 1
                            rk = [("fo", ci, off) for off in range(0, FR, 512)]
                            defer_ag.append((f_own[ci].ap(), f_all[ci].ap(), rk, [("f_all", ci)]))

            if do_rest and 'glu' not in SKIP:
                for jg in range(2):
                    ba = C.bank()
                    bb = C.bank()
                    for kc in range(8):
                        C.mm(C.ps[ba][:, 0:n], w_in_t[:, kc, 672 + jg * 128:672 + (jg + 1) * 128], h[:, kc, 0:n],
                             kc == 0, kc == 7, [hk, ("w_in", kc)], [f"ps{ba}"])
                    for kc in range(8):
                        C.mm(C.ps[bb][:, 0:n], w_in_t[:, kc, 928 + jg * 128:928 + (jg + 1) * 128], h[:, kc, 0:n],
                             kc == 0, kc == 7, [hk, ("w_in", kc)], [f"ps{bb}"])
                    C.act(sg[:, 0:n], C.ps[bb][:, 0:n], AF.Sigmoid, [f"ps{bb}"], ["sg"])
                    C.tt("dve", ut[:, jg, 0:n], C.ps[ba][:, 0:n], sg[:, 0:n], ALU.mult, [f"ps{ba}", "sg"], ["ut"])
                C.dma("sp", kd["up"].ap().rearrange("j p n -> p j n")[:, :, 16 + t0:16 + t0 + n], ut[:, :, 0:n],
                      ["ut"], [("up", kind)])
        for ag in pend_ag:
            C.allgather(*ag)
        pend_ag = []
        upv = upx.ap().rearrange("j p n -> (j p) n")
        if 'edge' not in SKIP:
            C.dma("sp", e_own.ap()[:, 0:16], upv[:, 16:32], [("up", "x")], ["e_own"])
            C.dma("sp", e_own.ap()[:, 16:32], upv[:, TC:TC + 16], [("up", "x")], ["e_own"])
        if 'ag' not in SKIP:
            for ag in defer_ag:
                C.allgather(*ag, persist=True)
            C.allgather(e_own.ap(), e_all.ap(), ["e_own"], ["e_all"], persist=True)
        C.reset(mA)
        if dbg == "A":
            break
        V = SimpleNamespace(**locals())
        phase_attention(C, V)
        if dbg == "C":
            break
        phase_fourier(C, V)
        W2_OFF = SB_TOP - 65536
        W1_OFF = W2_OFF - 65536
        WO_OFF = W1_OFF - 16384
        V.wo = C.alloc_at("wo", [128, 8, D], BF16, WO_OFF)
        V.w1 = C.alloc_at("w1", [128, 8, DFF], BF16, W1_OFF)
        V.w2 = C.alloc_at("w2", [128, 32, D], BF16, W2_OFF)
        def prefetch_weights(after_keys, l=l, V=V):
            for kc in range(8):
                C.dma("pool", V.wo[:, kc, :], w_o.ap()[l, kc * 128:(kc + 1) * 128, :], after_keys if kc == 0 else [],
                      [("wo", kc)], persist=True)
            for kc in range(8):
                C.dma("pool", V.w1[:, kc, :], w_mlp1.ap()[l, kc * 128:(kc + 1) * 128, :], [], [("w1", kc)], persist=True)
            for jc in range(32):
                C.dma("pool", V.w2[:, jc, :], w_mlp2.ap()[l, jc * 128:(jc + 1) * 128, :], [], [("w2", jc)], persist=True)
        V.after_first_loads = prefetch_weights
        C.limit = WO_OFF
        phase_conv(C, V)
        if dbg == "E":
            break
        phase_out(C, V)
        C.limit = W1_OFF
        phase_mlp(C, V)
        C.limit = SB_TOP
        C.P.persist_w = {}

    if dbg is None:
        phase_final(C, SimpleNamespace(**locals()))
    P.barrier()
    P.emit(nc)
    return nc


def phase_attention(C, V):
    l, last, TC, NB, NK, NKT = V.l, V.last, V.TC, V.NB, V.NK, V.NKT
    m = C.mark()
    wukv = V.wukv_t
    KVn = C.alloc("KVn", [128, NK], BF16)
    Kt = [C.alloc(f"Kt{i}", [128, NK], BF16) for i in range(2)]
    Vt = [C.alloc(f"Vt{i}", [128, NKT, 128], BF16) for i in range(2)]
    Qt = [C.alloc(f"Qt{i}", [128, TC], BF16) for i in range(2)]
    Qc = C.alloc("Qc", [128, CT], BF16)
    Pt = [C.alloc(f"Pt{i}", [128, 512], BF16) for i in range(4)]
    rinv = C.alloc("rinv", [128, 512], F32)
    aost = [C.alloc(f"aost{i}", [128, 512], BF16) for i in range(2)]
    TCq, NKV = V.TCq, V.NKV
    kvkeys = [("KVn", i) for i in range(1 + 4 * NKV)]
    C.dma("sp", KVn[:, 0:CT], V.kvc.ap()[0:128, :], [], [kvkeys[0]])
    for r in range(4):
        for cj in range(NKV):
            c0 = CT + r * TC + cj * TCq
            C.dma("sp", KVn[:, c0:c0 + TCq], V.kv_all[cj].ap()[r * 160:r * 160 + 128, :], [], [kvkeys[1 + r * NKV + cj]])
    krkeys = {}
    for i in range(2):
        krkeys[i] = [("Kr", i, k) for k in range(1 + 4 * NKV)]
        C.dma("sp", Kt[i][64:96, 0:CT], V.kvc.ap()[128:160, :], [], [krkeys[i][0]])
        for r in range(4):
            for cj in range(NKV):
                c0 = CT + r * TC + cj * TCq
                C.dma("sp", Kt[i][64:96, c0:c0 + TCq], V.kv_all[cj].ap()[r * 160 + 128:r * 160 + 160, :],
                      [], [krkeys[i][1 + r * NKV + cj]])
        C.memset("dve", Vt[i][:, :, 64:128], 1.0, [("Vone", i)])
    state = {"ob": 0, "ao": 0}

    def attn_block(hp, qap, qkey, n, nkt, dst, hook=None):
        ob = 4 + state["ob"] % 2
        state["ob"] += 1
        LOOK = 2
        for i in range(nkt + LOOK):
            if i < nkt:
                sbk = i % 4
                C.mm(C.ps[sbk][:, 0:n], Kt[hp][0:96, i * 128:(i + 1) * 128], qap, True, True,
                     [("Kn", hp), qkey] + krkeys[hp], [f"ps{sbk}"])
                C.act(Pt[sbk][:, 0:n], C.ps[sbk][:, 0:n], AF.Exp, [f"ps{sbk}"], [f"Pt{sbk}"], scale=ATTN_SCALE)
            ii = i - LOOK
            if ii >= 0:
                C.mm(C.ps[ob][:, 0:n], Vt[hp][:, ii, :], Pt[ii % 4][:, 0:n], ii == 0, ii == nkt - 1,
                     [f"Pt{ii % 4}", ("Vv", hp), ("Vone", hp)], [f"ps{ob}"])
                if hook is not None:
                    hook()
        a = state["ao"] % 2
        state["ao"] += 1
        C.recip(rinv[64:128, 0:n], C.ps[ob][64:128, 0:n], [f"ps{ob}"], ["rinv"])
        C.tt("dve", aost[a][0:64, 0:n], C.ps[ob][0:64, 0:n], rinv[64:128, 0:n], ALU.mult, [f"ps{ob}", "rinv"], [f"aost{a}"])
        C.dma("sp", dst, aost[a][0:64, 0:n], [f"aost{a}"], [])

    def build_steps(h):
        hp = h % 2
        steps = []
        steps.append(lambda: C.dma("sp", Qt[hp][0:96, :], V.qx.ap()[h], [], [("Q", hp)]))
        for c0 in range(0, NK, 512):
            nb = min(512, NK - c0)

            def kstep(c0=c0, nb=nb):
                b = C.bank(6, 8)
                C.mm(C.ps[b][0:64, 0:nb], wukv[:, h * 128:h * 128 + 64], KVn[:, c0:c0 + nb], True, True,
                     kvkeys + ["wukv"], [f"ps{b}"])
                C.cp("dve", Kt[hp][0:64, c0:c0 + nb], C.ps[b][0:64, 0:nb], [f"ps{b}"], [("Kn", hp)])
            steps.append(kstep)
        for g in range(0, NKT, 8):
            ng = min(8, NKT - g)

            def vstep(g=g, ng=ng):
                b = C.bank(6, 8)
                for kt in range(g, g + ng):
                    C.mm(C.ps[b][:, (kt - g) * 64:(kt - g + 1) * 64], KVn[:, kt * 128:(kt + 1) * 128],
                         wukv[:, h * 128 + 64:h * 128 + 128], True, True, kvkeys + ["wukv"], [f"ps{b}"])
                C.cp("dve", Vt[hp][:, g:g + ng, 0:64], C.ps[b][:, 0:ng * 64].rearrange("p (t c) -> p t c", c=64),
                     [f"ps{b}"], [("Vv", hp)])
            steps.append(vstep)
        return steps

    for st in build_steps(0):
        st()
    for h in range(NH):
        hp = h % 2
        nxt = build_steps(h + 1) if h + 1 < NH else []
        total_tiles = NB * NKT
        every = max(1, total_tiles // (len(nxt) + 2)) if nxt else 0
        cnt = {"n": 0}

        def hook():
            cnt["n"] += 1
            if nxt and every and cnt["n"] % every == 0:
                nxt.pop(0)()
        for qb in range(NB):
            dst = V.mixx.ap()[2 + h // 2, (h % 2) * 64:(h % 2) * 64 + 64, qb * 512:(qb + 1) * 512]
            attn_block(hp, Qt[hp][0:96, qb * 512:(qb + 1) * 512], ("Q", hp), 512, NKT, dst, hook=hook)
        if not last:
            C.dma("sp", Qc[0:96, :], V.qc.ap()[h], [], ["Qc"])
            dst = V.mixc.ap()[2 + h // 2, (h % 2) * 64:(h % 2) * 64 + 64, :]
            attn_block(hp, Qc[0:96, :], "Qc", CT, 2, dst)
        while nxt:
            nxt.pop(0)()
    C.reset(m)


def phase_fourier(C, V):
    l, last, TC, NB, NT, TCt = V.l, V.last, V.TC, V.NB, V.NT, V.TCt
    m = C.mark()
    cs = C.alloc("cs", [128, 2 * NT], BF16)
    r1 = C.alloc("r1", [128, NT, 64], BF16)
    r2 = C.alloc("r2", [128, NT, 64], BF16)
    ccb = C.alloc("ccb", [128, 128], BF16); mscb = C.alloc("mscb", [128, 128], BF16)
    wf = C.alloc("wf", [128, 2, 256], BF16)
    Yt = C.alloc("Yt", [128, 2, TC], BF16)
    Z = C.alloc("Z", [128, 128, 128], BF16)
    T = C.alloc("T", [128, 128, 2 * NT], BF16)
    Ure = C.alloc("Ure", [128, TC], BF16)
    Vv = C.alloc("Vv", [128, TC], BF16)
    yst = [C.alloc(f"yst{i}", [128, 2, 512], BF16) for i in range(2)]
    C.dma("sp", cs[0:NT, :], V.fft_cs.ap(), [], ["cs"])
    C.dma("sp", r1[:, :, :], V.fft_r1.ap().rearrange("p (k c) -> p k c", c=64), [], ["r1"])
    C.dma("sp", r2[:, :, :], V.fft_r2.ap().rearrange("p (k c) -> p k c", c=64), [], ["r2"])
    C.dma("sp", ccb[:, :], V.ccb_d.ap(), [], ["ccb"])
    C.dma("sp", mscb[:, :], V.mscb_d.ap(), [], ["mscb"])
    C.dma("pool", wf[:, :, :], V.w_fourier.ap()[l].rearrange("(j p) n -> p j n", p=128), [], ["wf"])
    cnt = 0
    for gp in range(2):
        FR, NF = V.FR, V.NF
        zkeys = []
        for r in range(4):
            rows_per = min(FR, TC)
            for cj in range(TC // rows_per):
                g0 = gp * TC + cj * rows_per
                ch, off = g0 // FR, g0 % FR
                p0 = r * TCt + cj * (rows_per // 128)
                zk = ("Z", r, cj)
                zkeys.append(zk)
                C.dma("sp", Z[p0:p0 + rows_per // 128, :, :],
                      V.f_all[ch].ap()[r * FR + off:r * FR + off + rows_per, :].rearrange("(a b) c -> a b c", b=128),
                      [("f_all", ch)], [zk])
        cpb = 512 // (2 * NT)
        cpb = min(cpb, 128)
        for c0 in range(0, 128, cpb):
            b = C.bank()
            for c in range(c0, c0 + cpb):
                C.mm(C.ps[b][:, (c - c0) * 2 * NT:(c - c0 + 1) * 2 * NT], Z[0:NT, :, c], cs[0:NT, :], True, True,
                     zkeys + ["cs"], [f"ps{b}"])
            C.cp("act" if cnt % 2 else "dve", T[:, c0:c0 + cpb, :],
                 C.ps[b][:, 0:cpb * 2 * NT].rearrange("p (c k) -> p c k", k=2 * NT), [f"ps{b}"], ["T"])
            cnt += 1
        for k0 in range(0, NT, 8):
            b = C.bank()
            for k1 in range(k0, k0 + 8):
                o = C.ps[b][:, (k1 - k0) * 64:(k1 - k0 + 1) * 64]
                C.mm(o, T[:, :, k1], r1[:, k1, :], True, False, ["T", "r1"], [f"ps{b}"])
                C.mm(o, T[:, :, NT + k1], r2[:, k1, :], False, True, ["T", "r2"], [f"ps{b}"])
            psv = C.ps[b][:, :].rearrange("p (k1 two k2) -> p two k2 k1", k1=8, two=2, k2=32)
            C.cp("dve", Ure[:, :].rearrange("p (k2 k1) -> p k2 k1", k1=NT)[:, :, k0:k0 + 8], psv[:, 0], [f"ps{b}"], ["Ure"])
            C.cp("act", Vv[:, :].rearrange("p (k2 k1) -> p k2 k1", k1=NT)[:, :, k0:k0 + 8], psv[:, 1], [f"ps{b}"], ["Vv"])
        for tb in range(NB):
            b = C.bank()
            C.mm(C.ps[b][:, :], ccb[:, :], Ure[:, tb * 512:(tb + 1) * 512], True, False, ["Ure", "ccb"], [f"ps{b}"])
            C.mm(C.ps[b][:, :], mscb[:, :], Vv[:, tb * 512:(tb + 1) * 512], False, True, ["Vv", "mscb"], [f"ps{b}"])
            C.cp("act" if tb % 2 else "dve", Yt[:, gp, tb * 512:(tb + 1) * 512], C.ps[b][:, :], [f"ps{b}"], [("Yt", gp)])
    mixv = V.mixx.ap().rearrange("kc p n -> p kc n")
    for tb in range(NB):
        y = yst[tb % 2]
        for oc in range(2):
            b = C.bank()
            for gp in range(2):
                C.mm(C.ps[b][:, :], wf[:, gp, oc * 128:(oc + 1) * 128], Yt[:, gp, tb * 512:(tb + 1) * 512], gp == 0, gp == 1,
                     [("Yt", 0), ("Yt", 1), "wf"], [f"ps{b}"])
            C.cp("act" if oc else "dve", y[:, oc, :], C.ps[b][:, :], [f"ps{b}"], [f"yst{tb % 2}"])
        C.dma("sp", mixv[:, 0:2, tb * 512:(tb + 1) * 512], y[:, :, :], [f"yst{tb % 2}"], [])
    if not last:
        Zc = C.alloc("Zc", [128, 2, 256], BF16)
        w256 = C.alloc("w256", [128, 2, 512], BF16)
        ccbc = C.alloc("ccbc", [128, 128], BF16); mscbc = C.alloc("mscbc", [128, 128], BF16)
        UVc = C.alloc("UVc", [128, 512], BF16)
        Ytc = C.alloc("Ytc", [128, 2, 256], BF16)
        ystc = C.alloc("ystc", [128, 2, 256], BF16)
        C.dma("sp", Zc[:, :, :], V.fc.ap().rearrange("(tt p) c -> p tt c", p=128), [], ["Zc"])
        C.dma("sp", w256[:, :, :], V.w256_d.ap().rearrange("(tt p) k -> p tt k", p=128), [], ["w256"])
        C.dma("sp", ccbc[:, :], V.ccbc_d.ap(), [], ["ccbc"])
        C.dma("sp", mscbc[:, :], V.mscbc_d.ap(), [], ["mscbc"])
        for cp_ in range(2):
            b = C.bank()
            for nt in range(2):
                C.mm(C.ps[b][:, :], Zc[:, nt, cp_ * 128:(cp_ + 1) * 128], w256[:, nt, :], nt == 0, nt == 1,
                     ["Zc", "w256"], [f"ps{b}"])
            C.cp("dve", UVc[:, :], C.ps[b][:, :], [f"ps{b}"], ["UVc"])
            b = C.bank()
            C.mm(C.ps[b][:, 0:256], ccbc[:, :], UVc[:, 0:256], True, False, ["UVc", "ccbc"], [f"ps{b}"])
            C.mm(C.ps[b][:, 0:256], mscbc[:, :], UVc[:, 256:512], False, True, ["UVc", "mscbc"], [f"ps{b}"])
            C.cp("act", Ytc[:, cp_, :], C.ps[b][:, 0:256], [f"ps{b}"], [("Ytc", cp_)])
        for oc in range(2):
            b = C.bank()
            for gp in range(2):
                C.mm(C.ps[b][:, 0:256], wf[:, gp, oc * 128:(oc + 1) * 128], Ytc[:, gp, :], gp == 0, gp == 1,
                     [("Ytc", 0), ("Ytc", 1), "wf"], [f"ps{b}"])
            C.cp("act" if oc else "dve", ystc[:, oc, :], C.ps[b][:, 0:256], [f"ps{b}"], ["ystc"])
        C.dma("sp", V.mixc.ap().rearrange("kc p n -> p kc n")[:, 0:2, :], ystc[:, :, :], ["ystc"], [])
    C.reset(m)


def phase_conv(C, V):
    l, last, TC = V.l, V.last, V.TC
    modT, wdwT, bdwT, lngT, lnbT = V.modT, V.wdwT, V.bdwT, V.lngT, V.lnbT
    m = C.mark()
    wpw = C.alloc("wpw", [128, 2, 256], BF16)
    C.dma("pool", wpw[:, :, :], V.w_pw2.ap()[l].rearrange("(j p) n -> p j n", p=128), [], ["wpw"])
    E = C.alloc("E", [128, 4, 2, 32], BF16)
    hl = C.alloc("hl", [128, 2, 16], F32); hr = C.alloc("hr", [128, 2, 16], F32)
    hb = C.alloc("hb", [128, 2, 32], BF16)
    C.dma("sp", E[:, :, :, :], V.e_all.ap().rearrange("(r j p) n -> p r j n", r=4, j=2), ["e_all"], ["E"])
    masks = V.masks
    C.ts("dve", hl[:, :, :], E[:, 0, :, 16:32], masks[:, 0:1], None, ALU.mult, None, ["E", "masks"], ["hl"])
    C.ts("dve", hr[:, :, :], E[:, 0, :, 0:16], masks[:, 4:5], None, ALU.mult, None, ["E", "masks"], ["hr"])
    for r in range(1, 4):
        C.stt("dve", hl[:, :, :], E[:, r, :, 16:32], masks[:, r:r + 1], hl[:, :, :], ALU.mult, ALU.add, ["E", "masks", "hl"], ["hl"])
        C.stt("dve", hr[:, :, :], E[:, r, :, 0:16], masks[:, 4 + r:5 + r], hr[:, :, :], ALU.mult, ALU.add, ["E", "masks", "hr"], ["hr"])
    C.cp("dve", hb[:, :, 0:16], hl[:, :, :], ["hl"], ["hb"])
    C.cp("dve", hb[:, :, 16:32], hr[:, :, :], ["hr"], ["hb"])
    upv = V.upx.ap().rearrange("j p n -> p j n")
    C.dma("sp", upv[:, :, 0:16], hb[:, :, 0:16], ["hb"], ["uph"])
    C.dma("sp", upv[:, :, TC + 16:TC + 32], hb[:, :, 16:32], ["hb"], ["uph"])

    U = [C.alloc(f"U{i}", [128, 2, 544], BF16) for i in range(2)]
    cvb = [C.alloc(f"cv{i}", [128, 2, 512], F32) for i in range(2)]
    xc = C.alloc("xc", [128, 2, 512], F32)
    sqc = C.alloc("sqc", [128, 2, 512], F32)
    mean = C.alloc("mean", [128, 512], F32); rsc = C.alloc("rsc", [128, 512], F32)
    rstdc = C.alloc("rstdc", [128, 512], F32)
    tmpc = [C.alloc(f"tmpc{i}", [128, 512], F32) for i in range(2)]
    sl = C.alloc("sl", [128, 2, 512], BF16)
    ycst = [C.alloc(f"ycst{i}", [128, 2, 512], BF16) for i in range(2)]
    Dg = C.alloc("Dg", [128, 62, 128], BF16)
    for kj in range(62):
        C.ts("dve", Dg[:, kj, :], V.ident[:, :], wdwT[:, l, kj:kj + 1], None, ALU.mult, None,
             ["ident", "wdwT"], [("Dg", kj % 2)])
    bl = V.blocks(with_ctx=not last)

    def loadU(bi):
        kind, t0, n = bl[bi]
        up = V.kinds[kind]["up"].ap().rearrange("j p n -> p j n")
        C.dma("sp", U[bi % 2][:, :, 0:n + 32], up[:, :, t0:t0 + n + 32], ["uph"], [f"U{bi % 2}"])
    def convmm(bi):
        kind, t0, n = bl[bi]
        pb = bi % 2
        Ub = U[pb]
        uk = f"U{pb}"
        for j in range(2):
            b = C.bank()
            for k in range(31):
                C.mm(C.ps[b][:, 0:n], Dg[:, k * 2 + j, :], Ub[:, j, k + 1:k + 1 + n], k == 0, k == 30,
                     [uk, ("Dg", 0), ("Dg", 1)], [f"ps{b}"])
            C.act(cvb[pb][:, j, 0:n], C.ps[b][:, 0:n], AF.Identity, [f"ps{b}", "bdwT"], [("cv", pb, j)], bias=bdwT[:, l, j:j + 1])

    loadU(0)
    if len(bl) > 1:
        loadU(1)
    if V.after_first_loads is not None:
        V.after_first_loads(["E", "U0", "U1", "wpw"])
    convmm(0)
    for bi, (kind, t0, n) in enumerate(bl):
        pb = bi % 2
        kd = V.kinds[kind]
        cv = cvb[pb]
        if bi + 2 < len(bl):
            loadU(bi + 2)
        if bi + 1 < len(bl):
            convmm(bi + 1)
        b = C.bank()
        for j in range(2):
            C.mm(C.ps[b][:, 0:n], V.ones_f[:, :], cv[:, j, 0:n], j == 0, j == 1, [("cv", pb, j), "ones_f"], [f"ps{b}"])
        C.act(mean[:, 0:n], C.ps[b][:, 0:n], AF.Identity, [f"ps{b}"], ["mean"], scale=1.0 / 256)
        for j in range(2):
            C.tt("dve", xc[:, j, 0:n], cv[:, j, 0:n], mean[:, 0:n], ALU.subtract, [("cv", pb, j), "mean"], ["xc"])
        C.act(sqc[:, :, 0:n], xc[:, :, 0:n], AF.Square, ["xc"], ["sqc"])
        b = C.bank()
        for j in range(2):
            C.mm(C.ps[b][:, 0:n], V.ones_f[:, :], sqc[:, j, 0:n], j == 0, j == 1, ["sqc", "ones_f"], [f"ps{b}"])
        C.act(rsc[:, 0:n], C.ps[b][:, 0:n], AF.Sqrt, [f"ps{b}", "eps"], ["rsc"], bias=V.eps_t[:, 0:1], scale=1.0 / 256)
        C.recip(rstdc[:, 0:n], rsc[:, 0:n], ["rsc"], ["rstdc"])
        for j in range(2):
            C.tt("dve", tmpc[j][:, 0:n], xc[:, j, 0:n], rstdc[:, 0:n], ALU.mult, ["xc", "rstdc"], [f"tmpc{j}"])
            C.act(sl[:, j, 0:n], tmpc[j][:, 0:n], AF.Silu, [f"tmpc{j}", "lngT", "lnbT"], ["sl"],
                  bias=lnbT[:, l, j:j + 1], scale=lngT[:, l, j:j + 1])
        for oc in range(2):
            b = C.bank()
            for j in range(2):
                C.mm(C.ps[b][:, 0:n], wpw[:, j, oc * 128:(oc + 1) * 128], sl[:, j, 0:n], j == 0, j == 1, ["sl", "wpw"], [f"ps{b}"])
            C.cp("act" if oc else "dve", ycst[pb][:, oc, 0:n], C.ps[b][:, 0:n], [f"ps{b}"], [f"ycst{pb}"])
        C.dma("sp", kd["mix"].ap().rearrange("kc p n -> p kc n")[:, 6:8, t0:t0 + n], ycst[pb][:, :, 0:n], [f"ycst{pb}"], [])
    C.reset(m)


def _norm_mod(C, V, xbt, xkey, n, l, j, gm, sh_off, sq, rs, rstd, tmp, hout, hkey, sqkey="sqN"):
    C.act(sq[:, :, 0:n], xbt[:, :, 0:n], AF.Square, [xkey], [sqkey])
    b = C.bank()
    for kc in range(8):
        C.mm(C.ps[b][:, 0:n], V.ones_bf[:, :], sq[:, kc, 0:n], kc == 0, kc == 7, [sqkey, "ones_bf"], [f"ps{b}"])
    C.act(rs[:, 0:n], C.ps[b][:, 0:n], AF.Sqrt, [f"ps{b}", "eps"], ["rsN"], bias=V.eps_t[:, 0:1], scale=1.0 / D)
    C.recip(rstd[:, 0:n], rs[:, 0:n], ["rsN"], ["rstdN"])
    for kc in range(8):
        C.tt("dve", tmp[kc % 2][:, 0:n], xbt[:, kc, 0:n], rstd[:, 0:n], ALU.mult, [xkey, "rstdN"], [f"tmpN{kc % 2}"])
        C.act(hout[:, kc, 0:n], tmp[kc % 2][:, 0:n], AF.Identity, [f"tmpN{kc % 2}", "gm", "modT"], [hkey],
              bias=V.modT[:, l, j, sh_off + kc:sh_off + kc + 1], scale=gm[:, l, j, kc:kc + 1])


def phase_out(C, V):
    l, last = V.l, V.last
    m = C.mark()
    wo = V.wo
    M = [C.alloc(f"M{i}", [128, 8, 512], BF16) for i in range(2)]
    xb0 = C.alloc("xbO0", [128, 8, 512], F32)
    xb = [xb0, xb0]
    rs = C.alloc("rsO", [128, 512], F32); rstd = C.alloc("rstdO", [128, 512], F32)
    tmp = [C.alloc(f"tmpO{i}", [128, 512], F32) for i in range(2)]
    h20 = C.alloc("h2O0", [128, 8, 512], BF16)
    h2 = [h20, h20]
    bl = V.blocks(with_ctx=not last)

    def load(bi):
        kind, t0, n = bl[bi]
        kd = V.kinds[kind]
        C.dma("sp", M[bi % 2][:, :, 0:n], V.tview(kd["mix"], t0, n), [], [f"M{bi % 2}"])
    load(0)
    for bi, (kind, t0, n) in enumerate(bl):
        pb = bi % 2
        kd = V.kinds[kind]
        j = kd["j"]
        if bi + 1 < len(bl):
            load(bi + 1)
        C.dma("sp", xb[pb][:, :, 0:n], V.tview(kd["xT"], t0, n), [("xT", kind, t0)], ["xbO0"])
        for oc in range(8):
            b = C.bank()
            for kc in range(8):
                C.mm(C.ps[b][:, 0:n], wo[:, kc, oc * 128:(oc + 1) * 128], M[pb][:, kc, 0:n], kc == 0, kc == 7,
                     [f"M{pb}", ("wo", kc)], [f"ps{b}"])
            C.stt("dve", xb[pb][:, oc, 0:n], C.ps[b][:, 0:n], V.modT[:, l, j, 16 + oc:17 + oc], xb[pb][:, oc, 0:n],
                  ALU.mult, ALU.add, [f"ps{b}", "modT", "xbO0"], ["xbO0"])
        C.dma("sp", V.tview(kd["xT"], t0, n), xb[pb][:, :, 0:n], ["xbO0"], [("xT", kind, t0)])
        _norm_mod(C, V, xb[pb], "xbO0", n, l, j, V.gm2, 24, h2[pb], rs, rstd, tmp, h2[pb], "h2O0", sqkey="h2O0")
        C.dma("sp", V.tview(kd["h2"], t0, n), h2[pb][:, :, 0:n], ["h2O0"], [])
    C.reset(m)


def phase_mlp(C, V):
    l, last = V.l, V.last
    m = C.mark()
    w1 = V.w1
    w2 = V.w2
    h2 = [C.alloc(f"h2M{i}", [128, 8, 512], BF16) for i in range(2)]
    x1 = C.alloc("x1M", [128, 8, 512], F32)
    hid = C.alloc("hid", [128, 32, 512], BF16)
    rt = [C.alloc(f"rt{i}", [128, 512], F32) for i in range(2)]
    bl = V.blocks(with_ctx=not last)

    def load(bi):
        kind, t0, n = bl[bi]
        C.dma("sp", h2[bi % 2][:, :, 0:n], V.tview(V.kinds[kind]["h2"], t0, n), [], [f"h2M{bi % 2}"])
    load(0)
    for bi, (kind, t0, n) in enumerate(bl):
        pb = bi % 2
        kd = V.kinds[kind]
        j = kd["j"]
        if bi + 1 < len(bl):
            load(bi + 1)
        C.dma("sp", x1[:, :, 0:n], V.tview(kd["xT"], t0, n), [], ["x1M"])
        for jc in range(32):
            b = C.bank()
            for kc in range(8):
                C.mm(C.ps[b][:, 0:n], w1[:, kc, jc * 128:(jc + 1) * 128], h2[pb][:, kc, 0:n], kc == 0, kc == 7,
                     [f"h2M{pb}", ("w1", kc)], [f"ps{b}"])
            C.act(rt[jc % 2][:, 0:n], C.ps[b][:, 0:n], AF.Relu, [f"ps{b}"], [f"rt{jc % 2}"])
            C.tt("pool" if jc % 2 else "dve", hid[:, jc, 0:n], rt[jc % 2][:, 0:n], rt[jc % 2][:, 0:n], ALU.mult,
                 [f"rt{jc % 2}"], [("hid", jc % 2)])
        for oc in range(8):
            b = C.bank()
            for jc in range(32):
                C.mm(C.ps[b][:, 0:n], w2[:, jc, oc * 128:(oc + 1) * 128], hid[:, jc, 0:n], jc == 0, jc == 31,
                     [("hid", 0), ("hid", 1), ("w2", jc)], [f"ps{b}"])
            C.stt("dve", x1[:, oc, 0:n], C.ps[b][:, 0:n], V.modT[:, l, j, 40 + oc:41 + oc], x1[:, oc, 0:n],
                  ALU.mult, ALU.add, [f"ps{b}", "modT", "x1M"], ["x1M"])
        C.dma("sp", V.tview(kd["xT"], t0, n), x1[:, :, 0:n], ["x1M"], [])
    C.reset(m)


def phase_final(C, V):
    m = C.mark()
    xb = [C.alloc(f"xbF{i}", [128, 8, 512], F32) for i in range(2)]
    sq = C.alloc("sqF", [128, 8, 512], BF16)
    rs = C.alloc("rsF", [128, 512], F32); rstd = C.alloc("rstdF", [128, 512], F32)
    y = C.alloc("yF", [128, 8, 512], F32)
    otok = [C.alloc(f"otok{i}", [128, 4, D], F32) for i in range(2)]
    bl = V.blocks(with_ctx=False)

    def load(bi):
        kind, t0, n = bl[bi]
        C.dma("sp", xb[bi % 2][:, :, 0:n], V.tview(V.xT, t0, n), [], [f"xbF{bi % 2}"])
    load(0)
    cnt = 0
    for bi, (kind, t0, n) in enumerate(bl):
        pb = bi % 2
        if bi + 1 < len(bl):
            load(bi + 1)
        C.act(sq[:, :, 0:n], xb[pb][:, :, 0:n], AF.Square, [f"xbF{pb}"], ["sqF"])
        b = C.bank()
        for kc in range(8):
            C.mm(C.ps[b][:, 0:n], V.ones_bf[:, :], sq[:, kc, 0:n], kc == 0, kc == 7, ["sqF", "ones_bf"], [f"ps{b}"])
        C.act(rs[:, 0:n], C.ps[b][:, 0:n], AF.Sqrt, [f"ps{b}", "eps"], ["rsF"], bias=V.eps_t[:, 0:1], scale=1.0 / D)
        C.recip(rstd[:, 0:n], rs[:, 0:n], ["rsF"], ["rstdF"])
        for kc in range(8):
            C.stt("dve", y[:, kc, 0:n], xb[pb][:, kc, 0:n], V.fngT[:, kc:kc + 1], rstd[:, 0:n], ALU.mult, ALU.mult,
                  [f"xbF{pb}", "rstdF", "fngT"], ["yF"])
        for tt_ in range(n // 128):
            for kc2 in range(0, 8, 4):
                b = C.bank()
                for kc in range(kc2, kc2 + 4):
                    C.tr(C.ps[b][:, (kc - kc2) * 128:(kc - kc2 + 1) * 128], y[:, kc, tt_ * 128:(tt_ + 1) * 128], V.ident[:, :],
                         ["yF", "ident"], [f"ps{b}"])
                C.cp("act" if cnt % 2 else "dve", otok[pb][:, tt_, kc2 * 128:kc2 * 128 + 512], C.ps[b][:, :], [f"ps{b}"], [f"otok{pb}"])
                cnt += 1
        C.dma("sp", V.out_d.ap()[t0:t0 + n, :].rearrange("(tt p) d -> p tt d", p=128), otok[pb][:, 0:n // 128, :], [f"otok{pb}"], [])
    C.reset(m)


_BF = ml_dtypes.bfloat16


def _tables(NT, r):
    S = 128 * NT
    TC = S // 4
    N = S
    tb = {}
    t = np.arange(S)
    row = (t // 64).astype(np.float32)
    col = (t % 64).astype(np.float32)
    inv = (10000.0 ** (-np.arange(0, 16, 2, dtype=np.float32) / 16)).astype(np.float32)
    ang = np.concatenate([row[:, None] * inv, col[:, None] * inv], axis=-1).astype(np.float32)
    cos = np.cos(ang).astype(np.float32)[r * TC:(r + 1) * TC].T
    sin = np.sin(ang).astype(np.float32)[r * TC:(r + 1) * TC].T
    tb["rope1"] = np.ascontiguousarray(np.concatenate([cos, cos], 0))
    tb["rope2"] = np.ascontiguousarray(np.concatenate([-sin, sin], 0))
    n1 = np.arange(NT)[:, None]; k1 = np.arange(NT)[None, :]
    a = 2 * np.pi * ((n1 * k1) % NT) / NT
    tb["fft_cs"] = np.concatenate([np.cos(a), np.sin(a)], 1).astype(_BF)
    n2 = np.arange(128)[:, None, None]
    k1 = np.arange(NT)[None, :, None]
    k2 = np.arange(32)[None, None, :]
    k = k1 + NT * (32 * r + k2)
    a = 2 * np.pi * ((n2 * k) % N) / N
    wc, ws = np.cos(a), np.sin(a)
    tb["fft_r1"] = np.concatenate([wc, ws], 2).reshape(128, NT * 64).astype(_BF)
    tb["fft_r2"] = np.concatenate([-ws, wc], 2).reshape(128, NT * 64).astype(_BF)
    c = np.arange(128)[:, None]; mm_ = np.arange(128)[None, :]
    same = (c // 64) == (mm_ // 64)
    a = 2 * np.pi * (((c % 64) * (mm_ % 64)) % 64) / 64
    for nm, nn in (("", N), ("c", CT)):
        sc = 1.0 / np.sqrt(nn * 64.0)
        tb["ccb" + nm] = (np.where(same, np.cos(a), 0.0) * sc).astype(_BF)
        tb["mscb" + nm] = (np.where(same, -np.sin(a), 0.0) * sc).astype(_BF)
    n = np.arange(CT)[:, None]; kk = np.arange(CT)[None, :]
    a = 2 * np.pi * ((n * kk) % CT) / CT
    tb["w256"] = np.concatenate([np.cos(a), np.sin(a)], 1).astype(_BF)
    tb["ident"] = np.eye(128, dtype=np.float32)
    mk = np.zeros((128, 8), np.float32)
    if r > 0:
        mk[:, r - 1] = 1.0
    if r < 3:
        mk[:, 4 + r + 1] = 1.0
    tb["masks"] = mk
    return tb


_CACHE = {}


def run_model(inputs, NT, L, dbg=None, dump=None):
    key = (NT, L, dbg, tuple(sorted(dump)) if dump else None)
    if key not in _CACHE:
        _CACHE[key] = build_program(NT, L, dbg=dbg, dump=dump)
    nc = _CACHE[key]
    S = 128 * NT
    TC = S // 4
    f32 = lambda a: np.ascontiguousarray(np.asarray(a, dtype=np.float32))
    wnames = ["w_mod", "b_mod", "norm1_g", "w_in", "q_norm_g", "w_uq", "kv_norm_g", "w_ukv", "w_fourier", "w_dw",
              "b_dw", "conv_ln_g", "conv_ln_b", "w_pw2", "w_o", "norm2_g", "w_mlp1", "w_mlp2"]
    shared = {n: f32(inputs[n])[:L] for n in wnames if n != "w_mod"}
    wmod_full = f32(inputs["w_mod"])[:L]
    wmod_sh = [np.ascontiguousarray(wmod_full[:, :, r * 1536:(r + 1) * 1536]) for r in range(4)]
    shared["final_norm_g"] = f32(inputs["final_norm_g"])
    x = f32(inputs["x"]); ctx = f32(inputs["ctx"]); c = f32(inputs["c"]); cc = f32(inputs["c_ctx"])
    in_maps = []
    for core in range(8):
        b, r = core // 4, core % 4
        m = dict(shared)
        m["x_in"] = np.ascontiguousarray(x[b, r * TC:(r + 1) * TC])
        m["ctx_in"] = np.ascontiguousarray(ctx[b])
        m["cvec"] = np.ascontiguousarray(np.stack([c[b], cc], 0))
        m["w_mod"] = wmod_sh[r]
        m.update(_tables(NT, r))
        in_maps.append(m)
    res = run_bass_kernel_spmd(nc, in_maps, core_ids=list(range(8)))
    return res


def kernel(**inputs):
    res = run_model(inputs, 128, 4)
    S = 128 * 128
    TC = S // 4
    out = np.empty((2, S, D), np.float32)
    for core in range(8):
        b, r = core // 4, core % 4
        out[b, r * TC:(r + 1) * TC] = np.asarray(res.results[core]["out"], dtype=np.float32)
    return out
```

```python
import contextlib
import os
SKIP = set(os.environ.get('KSKIP', '').split(','))
from types import SimpleNamespace
import numpy as np
import ml_dtypes
import concourse.bass as bass
import concourse.mybir as mybir
from concourse.bass_utils import run_bass_kernel_spmd

F32 = mybir.dt.float32
BF16 = mybir.dt.bfloat16
AF = mybir.ActivationFunctionType
ALU = mybir.AluOpType

ENGS = ("pe", "act", "dve", "pool", "sp")
EPOCH = 20000
NDMASEM = 12
NCCSEM = 3


class Op:
    __slots__ = ("eng", "fn", "deps", "signal", "dma", "sem", "val", "idx", "dslot", "inc")

    def __init__(self, eng, fn, dma):
        self.eng = eng
        self.fn = fn
        self.deps = []
        self.signal = False
        self.dma = dma
        self.sem = None
        self.val = 0
        self.idx = 0
        self.dslot = None
        self.inc = 16


class Prog:
    def __init__(self):
        self.ops = {e: [] for e in ENGS}
        self.last_w = {}
        self.readers = {}
        self.pending_dma = []
        self.all_last = {e: None for e in ENGS}
        self.persist_w = {}

    def op(self, eng, fn, reads=(), writes=(), dma=False, extra_deps=(), inc=16, persist=False):
        o = Op(eng, fn, dma)
        o.inc = inc
        pr = [k for k in reads if isinstance(k, str) and k.startswith("ps") and k[2:].isdigit()]
        if pr:
            writes = list(writes) + pr
        o.idx = len(self.ops[eng])
        deps = []
        for k in reads:
            w = self.last_w.get(k)
            if w is not None:
                deps.append((w, "raw"))
        for k in writes:
            w = self.last_w.get(k)
            if w is not None:
                deps.append((w, "waw"))
            for r in self.readers.get(k, ()):
                deps.append((r, "war"))
        for d in extra_deps:
            if d is not None:
                deps.append((d, "raw"))
        seen = set()
        for d, kind in deps:
            if d is o or id(d) in seen:
                continue
            need = True
            if d.eng == eng and not d.dma and not dma:
                if eng == "pe":
                    need = False
            if need:
                seen.add(id(d))
                o.deps.append(d)
                d.signal = True
        for k in reads:
            self.readers.setdefault(k, []).append(o)
        for k in writes:
            self.last_w[k] = o
            self.readers[k] = []
        self.ops[eng].append(o)
        if persist:
            for k in writes:
                self.persist_w[k] = o
        else:
            self.all_last[eng] = o
            if dma:
                self.pending_dma.append(o)
        return o

    def barrier(self):
        lasts = [self.all_last[e] for e in ENGS if self.all_last[e] is not None]
        pend = list(self.pending_dma)
        self.pending_dma = []
        for e in ENGS:
            self.op(e, None, extra_deps=lasts + pend)
        self.last_w = dict(self.persist_w)
        self.readers = {}

    def emit(self, nc):
        n_sems_needed = 0
        plan = {}
        for e in ENGS:
            cnt = 0
            epoch = 0
            for o in self.ops[e]:
                if o.dma:
                    continue
                if o.signal:
                    if cnt >= EPOCH:
                        epoch += 1
                        cnt = 0
                    cnt += 1
                    o.sem = (e, epoch)
                    o.val = cnt
            plan[e] = epoch + 1
        for e in ENGS:
            k = 0
            kc_ = 0
            for o in self.ops[e]:
                if o.dma:
                    if o.inc == 16:
                        o.dslot = k % NDMASEM
                        k += 1
                    else:
                        o.dslot = NDMASEM + (kc_ % NCCSEM)
                        kc_ += 1
        with contextlib.ExitStack() as st:
            sems = {}
            for e in ENGS:
                for ep in range(plan[e]):
                    sems[(e, ep)] = st.enter_context(nc.semaphore(f"s_{e}_{ep}"))
            dsems = {}
            for e in ENGS:
                if any(o.dma for o in self.ops[e]):
                    for k in range(NDMASEM + NCCSEM):
                        dsems[(e, k)] = st.enter_context(nc.semaphore(f"d_{e}_{k}"))
            block = st.enter_context(nc.Block())
            engmap = {"pe": block.tensor, "act": block.scalar, "dve": block.vector,
                      "pool": block.gpsimd, "sp": block.sync}

            def make(e):
                def body(eng):
                    waited = {}
                    dcount = {k: 0 for k in range(NDMASEM + NCCSEM)}
                    dlast = {}
                    for o in self.ops[e]:
                        for d in o.deps:
                            if d.dma:
                                key = ("d", d.eng, d.dslot)
                                s = dsems[(d.eng, d.dslot)]
                                v = d.val
                            else:
                                key = d.sem
                                s = sems[d.sem]
                                v = d.val
                            if waited.get(key, 0) >= v:
                                continue
                            waited[key] = v
                            eng.wait_ge(s, v)
                        if o.dma:
                            key = ("d", e, o.dslot)
                            prev = dcount[o.dslot]
                            if prev > 0 and waited.get(key, 0) < prev:
                                eng.wait_ge(dsems[(e, o.dslot)], prev)
                                waited[key] = prev
                            ins = o.fn(eng)
                            dcount[o.dslot] = prev + o.inc
                            o.val = prev + o.inc
                            ins.then_inc(dsems[(e, o.dslot)], o.inc)
                        else:
                            if o.fn is None:
                                if o.signal:
                                    eng.nop().then_inc(sems[o.sem], 1)
                                continue
                            ins = o.fn(eng)
                            if o.signal:
                                ins.then_inc(sems[o.sem], 1)
                return body

            for e in ENGS:
                dc = {k: 0 for k in range(NDMASEM + NCCSEM)}
                for o in self.ops[e]:
                    if o.dma:
                        dc[o.dslot] += o.inc
                        o.val = dc[o.dslot]
            for e in ENGS:
                if self.ops[e]:
                    engmap[e](make(e))


D = 1024
KC = 8
DIN = 1184
DFF = 4096
NH = 8
CT = 256
EPS = 1e-6
ATTN_SCALE = 96.0 ** -0.5
SB_BASE = 16512
SB_TOP = 229376 - 64


def _dtsize(dt):
    return 4 if dt == F32 else 2


class Ctx:
    def __init__(self, nc):
        self.nc = nc
        self.P = Prog()
        self.off = SB_BASE
        self.limit = SB_TOP
        self.uid = 0
        self.ps = [nc.alloc_psum_tensor(f"ps{i}", [128, 512], F32) for i in range(8)]
        self.psi = 0

    def alloc(self, name, shape, dt):
        n = 1
        for s in shape[1:]:
            n *= s
        nbytes = n * _dtsize(dt)
        off = (self.off + 63) // 64 * 64
        assert off + nbytes <= self.limit, (name, off, nbytes, self.limit)
        self.off = off + nbytes
        self.uid += 1
        return self.nc.alloc_sbuf_tensor_at(f"{name}_{self.uid}", list(shape), dt, offset=off)

    def alloc_at(self, name, shape, dt, off):
        self.uid += 1
        return self.nc.alloc_sbuf_tensor_at(f"{name}_{self.uid}", list(shape), dt, offset=off)

    def mark(self):
        return self.off

    def reset(self, m):
        self.P.barrier()
        self.off = m

    def bank(self, lo=0, hi=8):
        i = lo + (self.psi % (hi - lo))
        self.psi += 1
        return i

    def mm(self, out, lhsT, rhs, start, stop, r, w):
        return self.P.op("pe", lambda e: e.matmul(out, lhsT=lhsT, rhs=rhs, start=start, stop=stop),
                         reads=r, writes=w)

    def tr(self, out, in_, idn, r, w):
        return self.P.op("pe", lambda e: e.transpose(out, in_, idn), reads=r, writes=w)

    def act(self, out, in_, func, r, w, bias=None, scale=None):
        kw = {}
        if bias is not None:
            kw["bias"] = bias
        if scale is not None:
            kw["scale"] = scale
        return self.P.op("act", lambda e: e.activation(out, in_, func, **kw), reads=r, writes=w)

    def tt(self, eng, out, in0, in1, op, r, w):
        return self.P.op(eng, lambda e: e.tensor_tensor(out, in0, in1, op), reads=r, writes=w)

    def ts(self, eng, out, in0, s1, s2, op0, op1, r, w):
        if s2 is None:
            return self.P.op(eng, lambda e: e.tensor_scalar(out, in0, s1, None, op0), reads=r, writes=w)
        return self.P.op(eng, lambda e: e.tensor_scalar(out, in0, s1, s2, op0, op1), reads=r, writes=w)

    def stt(self, eng, out, in0, scalar, in1, op0, op1, r, w):
        return self.P.op(eng, lambda e: e.scalar_tensor_tensor(out, in0, scalar, in1, op0, op1),
                         reads=r, writes=w)

    def cp(self, eng, out, in_, r, w):
        if eng == "act":
            return self.P.op("act", lambda e: e.copy(out, in_), reads=r, writes=w)
        return self.P.op(eng, lambda e: e.tensor_copy(out, in_), reads=r, writes=w)

    def recip(self, out, in_, r, w):
        return self.P.op("dve", lambda e: e.reciprocal(out, in_), reads=r, writes=w)

    def memset(self, eng, ap, val, w):
        return self.P.op(eng, lambda e: e.memset(ap, val), writes=w)

    def dma(self, q, out, in_, r, w, persist=False):
        return self.P.op(q, lambda e: e.dma_start(out=out, in_=in_), reads=r, writes=w, dma=True, persist=persist)

    def allgather(self, src, dst, r, w, persist=False):
        return self.P.op("pool", lambda e: e.collective_compute(
            "AllGather", ALU.bypass, replica_groups=[[0, 1, 2, 3], [4, 5, 6, 7]],
            ins=[src], outs=[dst]), reads=r, writes=w, dma=True, inc=1, persist=persist)


def build_program(NT, L, dbg=None, dump=None):
    S = 128 * NT
    TC = S // 4
    NB = TC // 512
    TCt = TC // 128
    NKT = 2 + NT
    NK = NKT * 128
    assert TC % 512 == 0

    nc = bass.Bass("TRN2", target_bir_lowering=False)
    C = Ctx(nc)
    P = C.P

    def din(name, shape, dt=F32):
        return nc.dram_tensor(name, list(shape), dt, kind="ExternalInput")

    def dtmp(name, shape, dt):
        if dump and name in dump:
            return nc.dram_tensor(name, list(shape), dt, kind="ExternalOutput")
        return nc.dram_tensor(name, list(shape), dt)

    x_in = din("x_in", [TC, D])
    ctx_in = din("ctx_in", [CT, D])
    cvec = din("cvec", [2, D])
    w_mod = din("w_mod", [L, D, 1536]); b_mod = din("b_mod", [L, 6 * D])
    norm1_g = din("norm1_g", [L, D]); w_in = din("w_in", [L, D, DIN])
    q_norm_g = din("q_norm_g", [L, 256]); w_uq = din("w_uq", [L, 256, 768])
    kv_norm_g = din("kv_norm_g", [L, 128]); w_ukv = din("w_ukv", [L, 128, 1024])
    w_fourier = din("w_fourier", [L, 256, 256]); w_dw = din("w_dw", [L, 31, 256])
    b_dw = din("b_dw", [L, 256]); conv_ln_g = din("conv_ln_g", [L, 256]); conv_ln_b = din("conv_ln_b", [L, 256])
    w_pw2 = din("w_pw2", [L, 256, 256]); w_o = din("w_o", [L, D, D]); norm2_g = din("norm2_g", [L, D])
    w_mlp1 = din("w_mlp1", [L, D, DFF]); w_mlp2 = din("w_mlp2", [L, DFF, D])
    final_norm_g = din("final_norm_g", [D])
    rope1 = din("rope1", [32, TC]); rope2 = din("rope2", [32, TC])
    fft_cs = din("fft_cs", [NT, 2 * NT], BF16)
    fft_r1 = din("fft_r1", [128, NT * 64], BF16); fft_r2 = din("fft_r2", [128, NT * 64], BF16)
    ccb_d = din("ccb", [128, 128], BF16); mscb_d = din("mscb", [128, 128], BF16)
    w256_d = din("w256", [256, 512], BF16)
    ccbc_d = din("ccbc", [128, 128], BF16); mscbc_d = din("mscbc", [128, 128], BF16)
    ident_d = din("ident", [128, 128]); masks_d = din("masks", [128, 8])
    out_d = nc.dram_tensor("out", [TC, D], F32, kind="ExternalOutput")

    xT = dtmp("xT", [KC, 128, TC], F32); cT = dtmp("cT", [KC, 128, CT], F32)
    h2x = dtmp("h2x", [KC, 128, TC], BF16); h2c = dtmp("h2c", [KC, 128, CT], BF16)
    qx = dtmp("qx", [NH, 96, TC], BF16); qc = dtmp("qc", [NH, 96, CT], BF16)
    TCq = min(TC, 1024); NKV = TC // TCq
    FR = min(2 * TC, 2048); NF = 2 * TC // FR
    kv_own = [dtmp(f"kv_own{i}", [160, TCq], BF16) for i in range(NKV)]
    kv_all = [dtmp(f"kv_all{i}", [640, TCq], BF16) for i in range(NKV)]
    kvc = dtmp("kvc", [160, CT], BF16)
    f_own = [dtmp(f"f_own{i}", [FR, 128], BF16) for i in range(NF)]
    f_all = [dtmp(f"f_all{i}", [4 * FR, 128], BF16) for i in range(NF)]
    fc = dtmp("fc", [CT, 256], BF16)
    upx = dtmp("upx", [2, 128, TC + 32], BF16); upc = dtmp("upc", [2, 128, CT + 32], BF16)
    e_own = dtmp("e_own", [256, 32], BF16); e_all = dtmp("e_all", [1024, 32], BF16)
    mixx = dtmp("mixx", [KC, 128, TC], BF16); mixc = dtmp("mixc", [KC, 128, CT], BF16)

    kinds = {
        "x": dict(j=0, T=TC, xT=xT, h2=h2x, q=qx, kv=None, up=upx, mix=mixx),
        "c": dict(j=1, T=CT, xT=cT, h2=h2c, q=qc, kv=kvc, up=upc, mix=mixc),
    }

    def blocks(with_ctx=True):
        bl = []
        if with_ctx:
            bl.append(("c", 0, CT))
        for b in range(NB):
            bl.append(("x", b * 512, 512))
        return bl

    def tview(dt_, t0, n):
        return dt_.ap().rearrange("kc p n -> p kc n")[:, :, t0:t0 + n]

    ident = C.alloc("ident", [128, 128], F32)
    ones_bf = C.alloc("ones_bf", [128, 128], BF16)
    ones_f = C.alloc("ones_f", [128, 128], F32)
    eps_t = C.alloc("eps", [128, 1], F32)
    masks = C.alloc("masks", [128, 8], F32)
    modT = C.alloc("modT", [128, L, 2, 48], F32)
    bmodT = C.alloc("bmodT", [128, L, 48], F32)
    n1gT = C.alloc("n1gT", [128, L, 8], F32); n2gT = C.alloc("n2gT", [128, L, 8], F32)
    gm1 = C.alloc("gm1", [128, L, 2, 8], F32); gm2 = C.alloc("gm2", [128, L, 2, 8], F32)
    qgT = C.alloc("qgT", [128, L, 2], F32); kvgT = C.alloc("kvgT", [128, L, 1], F32)
    bdwT = C.alloc("bdwT", [128, L, 2], F32); lngT = C.alloc("lngT", [128, L, 2], F32)
    lnbT = C.alloc("lnbT", [128, L, 2], F32); wdwT = C.alloc("wdwT", [128, L, 62], F32)
    fngT = C.alloc("fngT", [128, 8], F32); cvT = C.alloc("cvT", [128, 16], F32)
    scT = C.alloc("scT", [128, 16], BF16)
    wukv_t = C.alloc("wukv", [128, 1024], BF16)
    PERSIST = C.mark()

    C.dma("sp", ident[:, :], ident_d.ap(), [], ["ident"])
    C.dma("sp", masks[:, :], masks_d.ap(), [], ["masks"])
    C.memset("dve", ones_bf[:, :], 1.0, ["ones_bf"])
    C.memset("dve", ones_f[:, :], 1.0, ["ones_f"])
    C.memset("dve", eps_t[:, :], EPS, ["eps"])

    stg = C.alloc("stg", [128, 128], F32)
    stg2 = C.alloc("stg2", [128, 128], F32)

    def rows128(ap1d, n):
        return ap1d.rearrange("(r p) -> r p", p=128)

    def vec_transpose(stage, key, nrows, dsts):
        b = C.bank()
        C.tr(C.ps[b][:, 0:nrows], stage[0:nrows, :], ident[0:nrows, 0:nrows], [key, "ident"], [f"ps{b}"])
        for dst, c0, c1, wk in dsts:
            C.cp("dve", dst, C.ps[b][:, c0:c1], [f"ps{b}"], [wk])

    for l in range(L):
        specs = [(rows128(b_mod.ap()[l], 48), 48), (rows128(norm1_g.ap()[l], 8), 8),
                 (rows128(norm2_g.ap()[l], 8), 8), (rows128(q_norm_g.ap()[l], 2), 2),
                 (rows128(kv_norm_g.ap()[l], 1), 1), (rows128(b_dw.ap()[l], 2), 2),
                 (rows128(conv_ln_g.ap()[l], 2), 2), (rows128(conv_ln_b.ap()[l], 2), 2)]
        r0 = 0
        for src, n in specs:
            C.dma("sp", stg[r0:r0 + n, :], src, [], ["stg"])
            r0 += n
        vec_transpose(stg, "stg", 73, [
            (bmodT[:, l, :], 0, 48, "bmodT"), (n1gT[:, l, :], 48, 56, "n1gT"), (n2gT[:, l, :], 56, 64, "n2gT"),
            (qgT[:, l, :], 64, 66, "qgT"), (kvgT[:, l, :], 66, 67, "kvgT"), (bdwT[:, l, :], 67, 69, "bdwT"),
            (lngT[:, l, :], 69, 71, "lngT"), (lnbT[:, l, :], 71, 73, "lnbT")])
        C.dma("sp", stg2[0:62, :], w_dw.ap()[l].rearrange("k (j p) -> (k j) p", p=128), [], ["stg2"])
        vec_transpose(stg2, "stg2", 62, [(wdwT[:, l, :], 0, 62, "wdwT")])
    C.dma("sp", stg[0:16, :], cvec.ap().rearrange("j (kc p) -> (j kc) p", p=128), [], ["stg"])
    C.dma("sp", stg[16:24, :], rows128(final_norm_g.ap(), 8), [], ["stg"])
    vec_transpose(stg, "stg", 24, [(cvT[:, :], 0, 16, "cvT"), (fngT[:, :], 16, 24, "fngT")])
    C.act(scT[:, :], cvT[:, :], AF.Silu, ["cvT"], ["scT"])

    m0 = C.mark()
    wm = [C.alloc(f"wm{i}", [128, 8, 1536], BF16) for i in range(2)]
    mpart = C.alloc("mpart", [128, L, 2, 12], F32)
    mp_own = dtmp("mp_own", [128, L * 24], F32)
    mp_all = dtmp("mp_all", [512, L * 24], F32)
    for l in range(L):
        wmt = wm[l % 2]
        for kc in range(8):
            C.dma("pool", wmt[:, kc, :], w_mod.ap()[l, kc * 128:(kc + 1) * 128, :], [], [("wm", l % 2, kc)])
        bm = C.bank()
        for oc in range(12):
            for kc in range(8):
                C.mm(C.ps[bm][:, oc * 2:oc * 2 + 2], wmt[:, kc, oc * 128:(oc + 1) * 128],
                     bass.AP(scT, kc, [[16, 128], [8, 2]]), kc == 0, kc == 7,
                     [("wm", l % 2, kc), "scT"], [f"ps{bm}"])
        for j in range(2):
            C.cp("dve", mpart[:, l, j, :], bass.AP(C.ps[bm], j, [[512, 128], [2, 12]]), [f"ps{bm}"], ["mpart"])
    C.dma("sp", mp_own.ap(), mpart[:, :, :, :].rearrange("p l j o -> p (l j o)"), ["mpart"], ["mp_own"])
    C.allgather(mp_own.ap(), mp_all.ap(), ["mp_own"], ["mp_all"])
    for r in range(4):
        C.dma("sp", modT[:, :, :, r * 12:(r + 1) * 12],
              mp_all.ap()[r * 128:(r + 1) * 128, :].rearrange("p (l j o) -> p l j o", l=L, j=2), ["mp_all"], [("modraw", r)])
    mrk = [("modraw", r) for r in range(4)]
    for l in range(L):
        for j in range(2):
            C.tt("dve", modT[:, l, j, :], modT[:, l, j, :], bmodT[:, l, :], ALU.add, mrk + ["bmodT"], ["modT"])
        for j in range(2):
            C.stt("dve", gm1[:, l, j, :], modT[:, l, j, 8:16], 1.0, n1gT[:, l, :], ALU.add, ALU.mult,
                  ["modT", "n1gT"], ["gm1"])
            C.stt("dve", gm2[:, l, j, :], modT[:, l, j, 32:40], 1.0, n2gT[:, l, :], ALU.add, ALU.mult,
                  ["modT", "n2gT"], ["gm2"])
    C.reset(m0)

    m0 = C.mark()
    xtok = [C.alloc(f"xtok{i}", [128, 4, D], F32) for i in range(2)]
    xbt = [C.alloc(f"xbt{i}", [128, 8, 512], F32) for i in range(2)]
    for bi, (kind, t0, n) in enumerate(blocks()):
        pb = bi % 2
        src = ctx_in if kind == "c" else x_in
        ntt = n // 128
        C.dma("sp", xtok[pb][:, 0:ntt, :], src.ap()[t0:t0 + n, :].rearrange("(tt p) d -> p tt d", p=128),
              [], [f"xtok{pb}"])
        for kc in range(8):
            b = C.bank()
            for tt_ in range(ntt):
                C.tr(C.ps[b][:, tt_ * 128:(tt_ + 1) * 128], xtok[pb][:, tt_, kc * 128:(kc + 1) * 128], ident[:, :],
                     [f"xtok{pb}", "ident"], [f"ps{b}"])
            C.cp("act" if kc % 2 else "dve", xbt[pb][:, kc, 0:n], C.ps[b][:, 0:n], [f"ps{b}"], [f"xbt{pb}"])
        C.dma("sp", tview(kinds[kind]["xT"], t0, n), xbt[pb][:, :, 0:n], [f"xbt{pb}"], [("xT", kind, t0)])
    zt = C.alloc("zt", [128, 2, 16], BF16)
    C.memset("dve", zt[:, :, :], 0.0, ["zt"])
    C.dma("sp", upc.ap().rearrange("j p n -> p j n")[:, :, 0:16], zt[:, :, :], ["zt"], ["upc_h"])
    C.dma("sp", upc.ap().rearrange("j p n -> p j n")[:, :, CT + 16:CT + 32], zt[:, :, :], ["zt"], ["upc_h"])
    C.reset(m0)

    def rms_stats(src3, n, nk, inv_n, sqt, sskey_r, tag):
        C.act(sqt[:, 0:nk, 0:n], src3, AF.Square, [sskey_r], ["sq" + tag])
        b = C.bank()
        for kc in range(nk):
            C.mm(C.ps[b][:, 0:n], ones_bf[:, :], sqt[:, kc, 0:n], kc == 0, kc == nk - 1,
                 ["sq" + tag, "ones_bf"], [f"ps{b}"])
        return b

    for l in range(L):
        last = (l == L - 1)
        if dbg == "P":
            break
        mA = C.mark()
        w_in_t = C.alloc("w_in_t", [128, 8, DIN], BF16)
        krsw = C.alloc("krsw", [128, 8, 32], BF16)
        w_uq_t = C.alloc("w_uq_t", [128, 2, 768], BF16)
        w_uq_s = C.alloc("w_uq_s", [128, 2, 768], BF16)
        for kc in range(8):
            C.dma("pool", w_in_t[:, kc, :], w_in.ap()[l, kc * 128:(kc + 1) * 128, :], [], [("w_in", kc)])
        C.dma("pool", wukv_t[:, :], w_ukv.ap()[l], [], ["wukv"], persist=True)
        C.dma("pool", krsw[:, :, 0:16], w_in.ap()[l].rearrange("(kc p) n -> p kc n", p=128)[:, :, 656:672], [], ["krsw0"])
        C.dma("pool", krsw[:, :, 16:32], w_in.ap()[l].rearrange("(kc p) n -> p kc n", p=128)[:, :, 640:656], [], ["krsw1"])
        uq_v = w_uq.ap()[l].rearrange("(j p) n -> p j n", p=128)
        C.dma("pool", w_uq_t[:, :, :], uq_v, [], ["w_uq"])
        C.dma("pool", w_uq_s[:, :, :], uq_v, [], ["w_uq_s"])
        uq_v4 = w_uq.ap()[l].rearrange("(j p) (h e) -> p j h e", p=128, e=96)
        s4 = w_uq_s[:, :, :].rearrange("p j (h e) -> p j h e", e=96)
        for j in range(2):
            C.dma("pool", s4[:, j, :, 64:80], uq_v4[:, j, :, 80:96], [], ["w_uq_s"])
            C.dma("pool", s4[:, j, :, 80:96], uq_v4[:, j, :, 64:80], [], ["w_uq_s"])

        xb = [C.alloc(f"xbA{i}", [128, 8, 512], F32) for i in range(2)]
        sq = C.alloc("sqA", [128, 8, 512], BF16)
        rs = C.alloc("rsA", [128, 512], F32)
        rstd = C.alloc("rstdA", [128, 512], F32)
        tmpA = [C.alloc(f"tmpA{i}", [128, 512], F32) for i in range(2)]
        hA = [C.alloc(f"hA{i}", [128, 8, 512], BF16) for i in range(2)]
        ftok = C.alloc("ftok", [128, 4, 256], BF16)
        cq = C.alloc("cq", [128, 2, 512], F32)
        sqq = C.alloc("sqq", [128, 2, 512], BF16)
        rq = C.alloc("rq", [128, 512], F32)
        cqn = C.alloc("cqn", [128, 2, 512], BF16)
        qst = C.alloc("qst", [128, NH, 512], BF16)
        rt1 = C.alloc("rt1", [128, 512], F32)
        rt2 = C.alloc("rt2", [128, 512], F32)
        rp1 = C.alloc("rp1", [128, 512], F32)
        rp2 = C.alloc("rp2", [128, 512], F32)
        ckv = C.alloc("ckv", [128, 512], F32)
        ckvn = C.alloc("ckvn", [128, 512], BF16)
        krr = C.alloc("krr", [128, 512], BF16)
        sg = C.alloc("sg", [128, 512], F32)
        ut = C.alloc("ut", [128, 2, 512], BF16)

        bl = blocks()
        def loadA(bi):
            kind, t0, n = bl[bi]
            C.dma("sp", xb[bi % 2][:, :, 0:n], tview(kinds[kind]["xT"], t0, n), [("xT", kind, t0)], [f"xbA{bi % 2}"])
        rsN = C.alloc("rsN", [128, 512], F32)

        def normA(bi):
            kind, t0, n = bl[bi]
            pb = bi % 2
            j = kinds[kind]["j"]
            b = rms_stats(xb[pb][:, :, 0:n], n, 8, None, sq, f"xbA{pb}", "A")
            C.act(rsN[:, 0:n], C.ps[b][:, 0:n], AF.Sqrt, [f"ps{b}", "eps"], ["rsN"], bias=eps_t[:, 0:1], scale=1.0 / D)
            C.recip(rstd[:, 0:n], rsN[:, 0:n], ["rsN"], ["rstdA"])
            for kc in range(8):
                C.tt("dve", tmpA[kc % 2][:, 0:n], xb[pb][:, kc, 0:n], rstd[:, 0:n], ALU.mult,
                     [f"xbA{pb}", "rstdA"], [f"tmpA{kc % 2}"])
                C.act(hA[pb][:, kc, 0:n], tmpA[kc % 2][:, 0:n], AF.Identity, [f"tmpA{kc % 2}", "gm1", "modT"], [f"hA{pb}"],
                      bias=modT[:, l, j, kc:kc + 1], scale=gm1[:, l, j, kc:kc + 1])

        loadA(0)
        if len(bl) > 1:
            loadA(1)
        normA(0)
        pend_ag = []
        defer_ag = []
        for bi, (kind, t0, n) in enumerate(bl):
            pb = bi % 2
            kd = kinds[kind]
            j = kd["j"]
            if bi + 1 < len(bl):
                normA(bi + 1)
            if bi + 2 < len(bl):
                loadA(bi + 2)
            for ag in pend_ag:
                C.allgather(*ag)
            pend_ag = []
            if kind == "x":
                C.dma("sp", rp1[64:96, 0:n], rope1.ap()[:, t0:t0 + n], [], ["rp1q"])
                C.dma("sp", rp2[64:96, 0:n], rope2.ap()[:, t0:t0 + n], [], ["rp2q"])
                C.dma("sp", rp1[0:32, 0:n], rope1.ap()[:, t0:t0 + n], [], ["rp1k"])
                C.dma("sp", rp2[0:32, 0:n], rope2.ap()[:, t0:t0 + n], [], ["rp2k"])
            hk = f"hA{pb}"
            h = hA[pb]
            do_rest = not (kind == "c" and last)
            if do_rest and 'f' not in SKIP:
                ntt = n // 128
                for t2 in range(0, ntt, 2):
                    b = C.bank()
                    for tt_ in range(t2, min(t2 + 2, ntt)):
                        for kc in range(8):
                            C.mm(C.ps[b][:, (tt_ - t2) * 256:(tt_ - t2 + 1) * 256], h[:, kc, tt_ * 128:(tt_ + 1) * 128],
                                 w_in_t[:, kc, 0:256], kc == 0, kc == 7, [hk, ("w_in", kc)], [f"ps{b}"])
                    nn = min(2, ntt - t2)
                    C.cp("act", ftok[:, t2:t2 + nn, :], C.ps[b][:, 0:nn * 256].rearrange("p (t c) -> p t c", c=256),
                         [f"ps{b}"], ["ftok"])
                if kind == "x":
                    for gp in range(2):
                        g0 = gp * TC + t0
                        C.dma("sp", f_own[g0 // FR].ap()[g0 % FR:g0 % FR + n, :].rearrange("(tt p) c -> p tt c", p=128),
                              ftok[:, 0:ntt, gp * 128:(gp + 1) * 128], ["ftok"], [("fo", g0 // FR, g0 % FR)])
                else:
                    C.dma("sp", fc.ap().rearrange("(tt p) c -> p tt c", p=128), ftok[:, 0:ntt, :], ["ftok"], ["fc"])
            if do_rest and 'q' not in SKIP:
                for jq in range(2):
                    b = C.bank()
                    for kc in range(8):
                        C.mm(C.ps[b][:, 0:n], w_in_t[:, kc, 256 + jq * 128:256 + (jq + 1) * 128], h[:, kc, 0:n],
                             kc == 0, kc == 7, [hk, ("w_in", kc)], [f"ps{b}"])
                    C.cp("dve", cq[:, jq, 0:n], C.ps[b][:, 0:n], [f"ps{b}"], ["cq"])
                    C.act(sqq[:, jq, 0:n], C.ps[b][:, 0:n], AF.Square, [f"ps{b}"], ["sqq"])
                b = C.bank()
                for jq in range(2):
                    C.mm(C.ps[b][:, 0:n], ones_bf[:, :], sqq[:, jq, 0:n], jq == 0, jq == 1, ["sqq", "ones_bf"], [f"ps{b}"])
                C.act(rs[:, 0:n], C.ps[b][:, 0:n], AF.Sqrt, [f"ps{b}", "eps"], ["rsA"], bias=eps_t[:, 0:1], scale=1.0 / 256)
                C.recip(rq[:, 0:n], rs[:, 0:n], ["rsA"], ["rq"])
                for jq in range(2):
                    C.stt("dve", cqn[:, jq, 0:n], cq[:, jq, 0:n], qgT[:, l, jq:jq + 1], rq[:, 0:n], ALU.mult, ALU.mult,
                          ["cq", "rq", "qgT"], ["cqn"])
            if 'kv' in SKIP:
                continue
            b = C.bank()
            for kc in range(8):
                C.mm(C.ps[b][:, 0:n], w_in_t[:, kc, 512:640], h[:, kc, 0:n], kc == 0, kc == 7, [hk, ("w_in", kc)], [f"ps{b}"])
            C.cp("dve", ckv[:, 0:n], C.ps[b][:, 0:n], [f"ps{b}"], ["ckv"])
            C.act(sqq[:, 0, 0:n], C.ps[b][:, 0:n], AF.Square, [f"ps{b}"], ["sqq"])
            b = C.bank()
            C.mm(C.ps[b][:, 0:n], ones_bf[:, :], sqq[:, 0, 0:n], True, True, ["sqq", "ones_bf"], [f"ps{b}"])
            C.act(rs[:, 0:n], C.ps[b][:, 0:n], AF.Sqrt, [f"ps{b}", "eps"], ["rsA"], bias=eps_t[:, 0:1], scale=1.0 / 128)
            C.recip(rq[:, 0:n], rs[:, 0:n], ["rsA"], ["rq"])
            C.stt("dve", ckvn[:, 0:n], ckv[:, 0:n], kvgT[:, l, 0:1], rq[:, 0:n], ALU.mult, ALU.mult,
                  ["ckv", "rq", "kvgT"], ["ckvn"])
            kvdst = kvc.ap()[:, 0:n] if kind == "c" else kv_own[t0 // TCq].ap()[:, t0 % TCq:t0 % TCq + n]
            C.dma("sp", kvdst[0:128, :], ckvn[:, 0:n], ["ckvn"], [("kvo", kind, t0, 0)])
            if 'kr' in SKIP:
                continue
            ba = C.bank()
            for kc in range(8):
                C.mm(C.ps[ba][0:32, 0:n], w_in_t[:, kc, 640:672], h[:, kc, 0:n], kc == 0, kc == 7, [hk, ("w_in", kc)], [f"ps{ba}"])
            if kind == "x":
                bb = C.bank()
                for kc in range(8):
                    C.mm(C.ps[bb][0:32, 0:n], krsw[:, kc, :], h[:, kc, 0:n], kc == 0, kc == 7, [hk, "krsw0", "krsw1"], [f"ps{bb}"])
                C.tt("dve", rt1[0:32, 0:n], C.ps[ba][0:32, 0:n], rp1[0:32, 0:n], ALU.mult, [f"ps{ba}", "rp1k"], ["rt1"])
                C.tt("dve", rt2[0:32, 0:n], C.ps[bb][0:32, 0:n], rp2[0:32, 0:n], ALU.mult, [f"ps{bb}", "rp2k"], ["rt2"])
                C.tt("pool", krr[0:32, 0:n], rt1[0:32, 0:n], rt2[0:32, 0:n], ALU.add, ["rt1", "rt2"], ["krr"])
            else:
                C.cp("act", krr[0:32, 0:n], C.ps[ba][0:32, 0:n], [f"ps{ba}"], ["krr"])
            C.dma("sp", kvdst[128:160, :], krr[0:32, 0:n], ["krr"], [("kvo", kind, t0, 1)])
            if kind == "x" and 'ag' not in SKIP:
                if (t0 + n) % TCq == 0:
                    ci = t0 // TCq
                    rk = [("kvo", "x", tt0, pp) for tt0 in range(ci * TCq, (ci + 1) * TCq, 512) for pp in range(2)]
                    pend_ag.append((kv_own[ci].ap(), kv_all[ci].ap(), rk, []))
                if do_rest and 'f' not in SKIP:
                    for gp in range(2):
                        g1 = gp * TC + t0 + n
                        if g1 % FR == 0:
                            ci = g1 // FR - 1
                            rk = [("fo", ci, off) for off in range(0, FR, 512)]
                            defer_ag.append((f_own[ci].ap(), f_all[ci].ap(), rk, [("f_all", ci)]))

            if do_rest and 'glu' not in SKIP:
                for jg in range(2):
                    ba = C.bank()
                    bb = C.bank()
                    for kc in range(8):
                        C.mm(C.ps[ba][:, 0:n], w_in_t[:, kc, 672 + jg * 128:672 + (jg + 1) * 128], h[:, kc, 0:n],
                             kc == 0, kc == 7, [hk, ("w_in", kc)], [f"ps{ba}"])
                    for kc in range(8):
                        C.mm(C.ps[bb][:, 0:n], w_in_t[:, kc, 928 + jg * 128:928 + (jg + 1) * 128], h[:, kc, 0:n],
                             kc == 0, kc == 7, [hk, ("w_in", kc)], [f"ps{bb}"])
                    C.act(sg[:, 0:n], C.ps[bb][:, 0:n], AF.Sigmoid, [f"ps{bb}"], ["sg"])
                    C.tt("dve", ut[:, jg, 0:n], C.ps[ba][:, 0:n], sg[:, 0:n], ALU.mult, [f"ps{ba}", "sg"], ["ut"])
                C.dma("sp", kd["up"].ap().rearrange("j p n -> p j n")[:, :, 16 + t0:16 + t0 + n], ut[:, :, 0:n],
                      ["ut"], [("up", kind)])
            if do_rest and 'q' not in SKIP:
                for hh in range(NH):
                    ba = C.bank()
                    for jq in range(2):
                        C.mm(C.ps[ba][0:96, 0:n], w_uq_t[:, jq, hh * 96:(hh + 1) * 96], cqn[:, jq, 0:n], jq == 0, jq == 1,
                             ["cqn", "w_uq"], [f"ps{ba}"])
                    C.cp("act", qst[0:64, hh, 0:n], C.ps[ba][0:64, 0:n], [f"ps{ba}"], [("qst", "n")])
                    if kind == "x":
                        bb = C.bank()
                        for jq in range(2):
                            C.mm(C.ps[bb][0:96, 0:n], w_uq_s[:, jq, hh * 96:(hh + 1) * 96], cqn[:, jq, 0:n], jq == 0, jq == 1,
                                 ["cqn", "w_uq_s"], [f"ps{bb}"])
                        C.tt("dve", rt1[64:96, 0:n], C.ps[ba][64:96, 0:n], rp1[64:96, 0:n], ALU.mult,
                             [f"ps{ba}", "rp1q"], ["rt1"])
                        C.tt("dve", rt2[64:96, 0:n], C.ps[bb][64:96, 0:n], rp2[64:96, 0:n], ALU.mult,
                             [f"ps{bb}", "rp2q"], ["rt2"])
                        C.tt("pool", qst[64:96, hh, 0:n], rt1[64:96, 0:n], rt2[64:96, 0:n], ALU.add,
                             ["rt1", "rt2"], [("qst", "r")])
                    else:
                        C.cp("act", qst[64:96, hh, 0:n], C.ps[ba][64:96, 0:n], [f"ps{ba}"], [("qst", "r")])
                C.dma("sp", kd["q"].ap().rearrange("h e n -> e h n")[:, :, t0:t0 + n], qst[0:96, :, 0:n],
                      [("qst", "n"), ("qst", "r")], [("q", kind)])
        for ag in pend_ag:
            C.allgather(*ag)
        pend_ag = []
        upv = upx.ap().rearrange("j p n -> (j p) n")
        if 'edge' not in SKIP:
            C.dma("sp", e_own.ap()[:, 0:16], upv[:, 16:32], [("up", "x")], ["e_own"])
            C.dma("sp", e_own.ap()[:, 16:32], upv[:, TC:TC + 16], [("up", "x")], ["e_own"])
        if 'ag' not in SKIP:
            for ag in defer_ag:
                C.allgather(*ag, persist=True)
            C.allgather(e_own.ap(), e_all.ap(), ["e_own"], ["e_all"], persist=True)
        C.reset(mA)
        if dbg == "A":
            break
        V = SimpleNamespace(**locals())
        phase_attention(C, V)
        if dbg == "C":
            break
        phase_fourier(C, V)
        W2_OFF = SB_TOP - 65536
        W1_OFF = W2_OFF - 65536
        WO_OFF = W1_OFF - 16384
        V.wo = C.alloc_at("wo", [128, 8, D], BF16, WO_OFF)
        V.w1 = C.alloc_at("w1", [128, 8, DFF], BF16, W1_OFF)
        V.w2 = C.alloc_at("w2", [128, 32, D], BF16, W2_OFF)
        def prefetch_weights(after_keys, l=l, V=V):
            for kc in range(8):
                C.dma("pool", V.wo[:, kc, :], w_o.ap()[l, kc * 128:(kc + 1) * 128, :], after_keys if kc == 0 else [],
                      [("wo", kc)], persist=True)
            for kc in range(8):
                C.dma("pool", V.w1[:, kc, :], w_mlp1.ap()[l, kc * 128:(kc + 1) * 128, :], [], [("w1", kc)], persist=True)
            for jc in range(32):
                C.dma("pool", V.w2[:, jc, :], w_mlp2.ap()[l, jc * 128:(jc + 1) * 128, :], [], [("w2", jc)], persist=True)
        V.after_first_loads = prefetch_weights
        C.limit = WO_OFF
        phase_conv(C, V)
        if dbg == "E":
            break
        phase_out(C, V)
        C.limit = W1_OFF
        phase_mlp(C, V)
        C.limit = SB_TOP
        C.P.persist_w = {}

    if dbg is None:
        phase_final(C, SimpleNamespace(**locals()))
    P.barrier()
    P.emit(nc)
    return nc


def phase_attention(C, V):
    l, last, TC, NB, NK, NKT = V.l, V.last, V.TC, V.NB, V.NK, V.NKT
    m = C.mark()
    wukv = V.wukv_t
    KVn = C.alloc("KVn", [128, NK], BF16)
    Kt = [C.alloc(f"Kt{i}", [128, NK], BF16) for i in range(2)]
    Vt = [C.alloc(f"Vt{i}", [128, NKT, 128], BF16) for i in range(2)]
    Qt = [C.alloc(f"Qt{i}", [128, TC], BF16) for i in range(2)]
    Qc = C.alloc("Qc", [128, CT], BF16)
    Pt = [C.alloc(f"Pt{i}", [128, 512], BF16) for i in range(4)]
    rinv = C.alloc("rinv", [128, 512], F32)
    aost = [C.alloc(f"aost{i}", [128, 512], BF16) for i in range(2)]
    TCq, NKV = V.TCq, V.NKV
    kvkeys = [("KVn", i) for i in range(1 + 4 * NKV)]
    C.dma("sp", KVn[:, 0:CT], V.kvc.ap()[0:128, :], [], [kvkeys[0]])
    for r in range(4):
        for cj in range(NKV):
            c0 = CT + r * TC + cj * TCq
            C.dma("sp", KVn[:, c0:c0 + TCq], V.kv_all[cj].ap()[r * 160:r * 160 + 128, :], [], [kvkeys[1 + r * NKV + cj]])
    krkeys = {}
    for i in range(2):
        krkeys[i] = [("Kr", i, k) for k in range(1 + 4 * NKV)]
        C.dma("sp", Kt[i][64:96, 0:CT], V.kvc.ap()[128:160, :], [], [krkeys[i][0]])
        for r in range(4):
            for cj in range(NKV):
                c0 = CT + r * TC + cj * TCq
                C.dma("sp", Kt[i][64:96, c0:c0 + TCq], V.kv_all[cj].ap()[r * 160 + 128:r * 160 + 160, :],
                      [], [krkeys[i][1 + r * NKV + cj]])
        C.memset("dve", Vt[i][:, :, 64:128], 1.0, [("Vone", i)])
    state = {"ob": 0, "ao": 0}

    def attn_block(hp, qap, qkey, n, nkt, dst, hook=None):
        ob = 4 + state["ob"] % 2
        state["ob"] += 1
        LOOK = 2
        for i in range(nkt + LOOK):
            if i < nkt:
                sbk = i % 4
                C.mm(C.ps[sbk][:, 0:n], Kt[hp][0:96, i * 128:(i + 1) * 128], qap, True, True,
                     [("Kn", hp), qkey] + krkeys[hp], [f"ps{sbk}"])
                C.act(Pt[sbk][:, 0:n], C.ps[sbk][:, 0:n], AF.Exp, [f"ps{sbk}"], [f"Pt{sbk}"], scale=ATTN_SCALE)
            ii = i - LOOK
            if ii >= 0:
                C.mm(C.ps[ob][:, 0:n], Vt[hp][:, ii, :], Pt[ii % 4][:, 0:n], ii == 0, ii == nkt - 1,
                     [f"Pt{ii % 4}", ("Vv", hp), ("Vone", hp)], [f"ps{ob}"])
                if hook is not None:
                    hook()
        a = state["ao"] % 2
        state["ao"] += 1
        C.recip(rinv[64:128, 0:n], C.ps[ob][64:128, 0:n], [f"ps{ob}"], ["rinv"])
        C.tt("dve", aost[a][0:64, 0:n], C.ps[ob][0:64, 0:n], rinv[64:128, 0:n], ALU.mult, [f"ps{ob}", "rinv"], [f"aost{a}"])
        C.dma("sp", dst, aost[a][0:64, 0:n], [f"aost{a}"], [])

    def build_steps(h):
        hp = h % 2
        steps = []
        steps.append(lambda: C.dma("sp", Qt[hp][0:96, :], V.qx.ap()[h], [], [("Q", hp)]))
        for c0 in range(0, NK, 512):
            nb = min(512, NK - c0)

            def kstep(c0=c0, nb=nb):
                b = C.bank(6, 8)
                C.mm(C.ps[b][0:64, 0:nb], wukv[:, h * 128:h * 128 + 64], KVn[:, c0:c0 + nb], True, True,
                     kvkeys + ["wukv"], [f"ps{b}"])
                C.cp("dve", Kt[hp][0:64, c0:c0 + nb], C.ps[b][0:64, 0:nb], [f"ps{b}"], [("Kn", hp)])
            steps.append(kstep)
        for g in range(0, NKT, 8):
            ng = min(8, NKT - g)

            def vstep(g=g, ng=ng):
                b = C.bank(6, 8)
                for kt in range(g, g + ng):
                    C.mm(C.ps[b][:, (kt - g) * 64:(kt - g + 1) * 64], KVn[:, kt * 128:(kt + 1) * 128],
                         wukv[:, h * 128 + 64:h * 128 + 128], True, True, kvkeys + ["wukv"], [f"ps{b}"])
                C.cp("dve", Vt[hp][:, g:g + ng, 0:64], C.ps[b][:, 0:ng * 64].rearrange("p (t c) -> p t c", c=64),
                     [f"ps{b}"], [("Vv", hp)])
            steps.append(vstep)
        return steps

    for st in build_steps(0):
        st()
    for h in range(NH):
        hp = h % 2
        nxt = build_steps(h + 1) if h + 1 < NH else []
        total_tiles = NB * NKT
        every = max(1, total_tiles // (len(nxt) + 2)) if nxt else 0
        cnt = {"n": 0}

        def hook():
            cnt["n"] += 1
            if nxt and every and cnt["n"] % every == 0:
                nxt.pop(0)()
        for qb in range(NB):
            dst = V.mixx.ap()[2 + h // 2, (h % 2) * 64:(h % 2) * 64 + 64, qb * 512:(qb + 1) * 512]
            attn_block(hp, Qt[hp][0:96, qb * 512:(qb + 1) * 512], ("Q", hp), 512, NKT, dst, hook=hook)
        if not last:
            C.dma("sp", Qc[0:96, :], V.qc.ap()[h], [], ["Qc"])
            dst = V.mixc.ap()[2 + h // 2, (h % 2) * 64:(h % 2) * 64 + 64, :]
            attn_block(hp, Qc[0:96, :], "Qc", CT, 2, dst)
        while nxt:
            nxt.pop(0)()
    C.reset(m)


def phase_fourier(C, V):
    l, last, TC, NB, NT, TCt = V.l, V.last, V.TC, V.NB, V.NT, V.TCt
    m = C.mark()
    cs = C.alloc("cs", [128, 2 * NT], BF16)
    r1 = C.alloc("r1", [128, NT, 64], BF16)
    r2 = C.alloc("r2", [128, NT, 64], BF16)
    ccb = C.alloc("ccb", [128, 128], BF16); mscb = C.alloc("mscb", [128, 128], BF16)
    wf = C.alloc("wf", [128, 2, 256], BF16)
    Yt = C.alloc("Yt", [128, 2, TC], BF16)
    Z = C.alloc("Z", [128, 128, 128], BF16)
    T = C.alloc("T", [128, 128, 2 * NT], BF16)
    Ure = C.alloc("Ure", [128, TC], BF16)
    Vv = C.alloc("Vv", [128, TC], BF16)
    yst = [C.alloc(f"yst{i}", [128, 2, 512], BF16) for i in range(2)]
    C.dma("sp", cs[0:NT, :], V.fft_cs.ap(), [], ["cs"])
    C.dma("sp", r1[:, :, :], V.fft_r1.ap().rearrange("p (k c) -> p k c", c=64), [], ["r1"])
    C.dma("sp", r2[:, :, :], V.fft_r2.ap().rearrange("p (k c) -> p k c", c=64), [], ["r2"])
    C.dma("sp", ccb[:, :], V.ccb_d.ap(), [], ["ccb"])
    C.dma("sp", mscb[:, :], V.mscb_d.ap(), [], ["mscb"])
    C.dma("pool", wf[:, :, :], V.w_fourier.ap()[l].rearrange("(j p) n -> p j n", p=128), [], ["wf"])
    cnt = 0
    for gp in range(2):
        FR, NF = V.FR, V.NF
        zkeys = []
        for r in range(4):
            rows_per = min(FR, TC)
            for cj in range(TC // rows_per):
                g0 = gp * TC + cj * rows_per
                ch, off = g0 // FR, g0 % FR
                p0 = r * TCt + cj * (rows_per // 128)
                zk = ("Z", r, cj)
                zkeys.append(zk)
                C.dma("sp", Z[p0:p0 + rows_per // 128, :, :],
                      V.f_all[ch].ap()[r * FR + off:r * FR + off + rows_per, :].rearrange("(a b) c -> a b c", b=128),
                      [("f_all", ch)], [zk])
        cpb = 512 // (2 * NT)
        cpb = min(cpb, 128)
        for c0 in range(0, 128, cpb):
            b = C.bank()
            for c in range(c0, c0 + cpb):
                C.mm(C.ps[b][:, (c - c0) * 2 * NT:(c - c0 + 1) * 2 * NT], Z[0:NT, :, c], cs[0:NT, :], True, True,
                     zkeys + ["cs"], [f"ps{b}"])
            C.cp("act" if cnt % 2 else "dve", T[:, c0:c0 + cpb, :],
                 C.ps[b][:, 0:cpb * 2 * NT].rearrange("p (c k) -> p c k", k=2 * NT), [f"ps{b}"], ["T"])
            cnt += 1
        for k0 in range(0, NT, 8):
            b = C.bank()
            for k1 in range(k0, k0 + 8):
                o = C.ps[b][:, (k1 - k0) * 64:(k1 - k0 + 1) * 64]
                C.mm(o, T[:, :, k1], r1[:, k1, :], True, False, ["T", "r1"], [f"ps{b}"])
                C.mm(o, T[:, :, NT + k1], r2[:, k1, :], False, True, ["T", "r2"], [f"ps{b}"])
            psv = C.ps[b][:, :].rearrange("p (k1 two k2) -> p two k2 k1", k1=8, two=2, k2=32)
            C.cp("dve", Ure[:, :].rearrange("p (k2 k1) -> p k2 k1", k1=NT)[:, :, k0:k0 + 8], psv[:, 0], [f"ps{b}"], ["Ure"])
            C.cp("act", Vv[:, :].rearrange("p (k2 k1) -> p k2 k1", k1=NT)[:, :, k0:k0 + 8], psv[:, 1], [f"ps{b}"], ["Vv"])
        for tb in range(NB):
            b = C.bank()
            C.mm(C.ps[b][:, :], ccb[:, :], Ure[:, tb * 512:(tb + 1) * 512], True, False, ["Ure", "ccb"], [f"ps{b}"])
            C.mm(C.ps[b][:, :], mscb[:, :], Vv[:, tb * 512:(tb + 1) * 512], False, True, ["Vv", "mscb"], [f"ps{b}"])
            C.cp("act" if tb % 2 else "dve", Yt[:, gp, tb * 512:(tb + 1) * 512], C.ps[b][:, :], [f"ps{b}"], [("Yt", gp)])
    mixv = V.mixx.ap().rearrange("kc p n -> p kc n")
    for tb in range(NB):
        y = yst[tb % 2]
        for oc in range(2):
            b = C.bank()
            for gp in range(2):
                C.mm(C.ps[b][:, :], wf[:, gp, oc * 128:(oc + 1) * 128], Yt[:, gp, tb * 512:(tb + 1) * 512], gp == 0, gp == 1,
                     [("Yt", 0), ("Yt", 1), "wf"], [f"ps{b}"])
            C.cp("act" if oc else "dve", y[:, oc, :], C.ps[b][:, :], [f"ps{b}"], [f"yst{tb % 2}"])
        C.dma("sp", mixv[:, 0:2, tb * 512:(tb + 1) * 512], y[:, :, :], [f"yst{tb % 2}"], [])
    if not last:
        Zc = C.alloc("Zc", [128, 2, 256], BF16)
        w256 = C.alloc("w256", [128, 2, 512], BF16)
        ccbc = C.alloc("ccbc", [128, 128], BF16); mscbc = C.alloc("mscbc", [128, 128], BF16)
        UVc = C.alloc("UVc", [128, 512], BF16)
        Ytc = C.alloc("Ytc", [128, 2, 256], BF16)
        ystc = C.alloc("ystc", [128, 2, 256], BF16)
        C.dma("sp", Zc[:, :, :], V.fc.ap().rearrange("(tt p) c -> p tt c", p=128), [], ["Zc"])
        C.dma("sp", w256[:, :, :], V.w256_d.ap().rearrange("(tt p) k -> p tt k", p=128), [], ["w256"])
        C.dma("sp", ccbc[:, :], V.ccbc_d.ap(), [], ["ccbc"])
        C.dma("sp", mscbc[:, :], V.mscbc_d.ap(), [], ["mscbc"])
        for cp_ in range(2):
            b = C.bank()
            for nt in range(2):
                C.mm(C.ps[b][:, :], Zc[:, nt, cp_ * 128:(cp_ + 1) * 128], w256[:, nt, :], nt == 0, nt == 1,
                     ["Zc", "w256"], [f"ps{b}"])
            C.cp("dve", UVc[:, :], C.ps[b][:, :], [f"ps{b}"], ["UVc"])
            b = C.bank()
            C.mm(C.ps[b][:, 0:256], ccbc[:, :], UVc[:, 0:256], True, False, ["UVc", "ccbc"], [f"ps{b}"])
            C.mm(C.ps[b][:, 0:256], mscbc[:, :], UVc[:, 256:512], False, True, ["UVc", "mscbc"], [f"ps{b}"])
            C.cp("act", Ytc[:, cp_, :], C.ps[b][:, 0:256], [f"ps{b}"], [("Ytc", cp_)])
        for oc in range(2):
            b = C.bank()
            for gp in range(2):
                C.mm(C.ps[b][:, 0:256], wf[:, gp, oc * 128:(oc + 1) * 128], Ytc[:, gp, :], gp == 0, gp == 1,
                     [("Ytc", 0), ("Ytc", 1), "wf"], [f"ps{b}"])
            C.cp("act" if oc else "dve", ystc[:, oc, :], C.ps[b][:, 0:256], [f"ps{b}"], ["ystc"])
        C.dma("sp", V.mixc.ap().rearrange("kc p n -> p kc n")[:, 0:2, :], ystc[:, :, :], ["ystc"], [])
    C.reset(m)


def phase_conv(C, V):
    l, last, TC = V.l, V.last, V.TC
    modT, wdwT, bdwT, lngT, lnbT = V.modT, V.wdwT, V.bdwT, V.lngT, V.lnbT
    m = C.mark()
    wpw = C.alloc("wpw", [128, 2, 256], BF16)
    C.dma("pool", wpw[:, :, :], V.w_pw2.ap()[l].rearrange("(j p) n -> p j n", p=128), [], ["wpw"])
    E = C.alloc("E", [128, 4, 2, 32], BF16)
    hl = C.alloc("hl", [128, 2, 16], F32); hr = C.alloc("hr", [128, 2, 16], F32)
    hb = C.alloc("hb", [128, 2, 32], BF16)
    C.dma("sp", E[:, :, :, :], V.e_all.ap().rearrange("(r j p) n -> p r j n", r=4, j=2), ["e_all"], ["E"])
    masks = V.masks
    C.ts("dve", hl[:, :, :], E[:, 0, :, 16:32], masks[:, 0:1], None, ALU.mult, None, ["E", "masks"], ["hl"])
    C.ts("dve", hr[:, :, :], E[:, 0, :, 0:16], masks[:, 4:5], None, ALU.mult, None, ["E", "masks"], ["hr"])
    for r in range(1, 4):
        C.stt("dve", hl[:, :, :], E[:, r, :, 16:32], masks[:, r:r + 1], hl[:, :, :], ALU.mult, ALU.add, ["E", "masks", "hl"], ["hl"])
        C.stt("dve", hr[:, :, :], E[:, r, :, 0:16], masks[:, 4 + r:5 + r], hr[:, :, :], ALU.mult, ALU.add, ["E", "masks", "hr"], ["hr"])
    C.cp("dve", hb[:, :, 0:16], hl[:, :, :], ["hl"], ["hb"])
    C.cp("dve", hb[:, :, 16:32], hr[:, :, :], ["hr"], ["hb"])
    upv = V.upx.ap().rearrange("j p n -> p j n")
    C.dma("sp", upv[:, :, 0:16], hb[:, :, 0:16], ["hb"], ["uph"])
    C.dma("sp", upv[:, :, TC + 16:TC + 32], hb[:, :, 16:32], ["hb"], ["uph"])

    U = [C.alloc(f"U{i}", [128, 2, 544], BF16) for i in range(2)]
    cvb = [C.alloc(f"cv{i}", [128, 2, 512], F32) for i in range(2)]
    xc = C.alloc("xc", [128, 2, 512], F32)
    sqc = C.alloc("sqc", [128, 2, 512], F32)
    mean = C.alloc("mean", [128, 512], F32); rsc = C.alloc("rsc", [128, 512], F32)
    rstdc = C.alloc("rstdc", [128, 512], F32)
    tmpc = [C.alloc(f"tmpc{i}", [128, 512], F32) for i in range(2)]
    sl = C.alloc("sl", [128, 2, 512], BF16)
    ycst = [C.alloc(f"ycst{i}", [128, 2, 512], BF16) for i in range(2)]
    Dg = C.alloc("Dg", [128, 62, 128], BF16)
    for kj in range(62):
        C.ts("dve", Dg[:, kj, :], V.ident[:, :], wdwT[:, l, kj:kj + 1], None, ALU.mult, None,
             ["ident", "wdwT"], [("Dg", kj % 2)])
    bl = V.blocks(with_ctx=not last)

    def loadU(bi):
        kind, t0, n = bl[bi]
        up = V.kinds[kind]["up"].ap().rearrange("j p n -> p j n")
        C.dma("sp", U[bi % 2][:, :, 0:n + 32], up[:, :, t0:t0 + n + 32], ["uph"], [f"U{bi % 2}"])
    def convmm(bi):
        kind, t0, n = bl[bi]
        pb = bi % 2
        Ub = U[pb]
        uk = f"U{pb}"
        for j in range(2):
            b = C.bank()
            for k in range(31):
                C.mm(C.ps[b][:, 0:n], Dg[:, k * 2 + j, :], Ub[:, j, k + 1:k + 1 + n], k == 0, k == 30,
                     [uk, ("Dg", 0), ("Dg", 1)], [f"ps{b}"])
            C.act(cvb[pb][:, j, 0:n], C.ps[b][:, 0:n], AF.Identity, [f"ps{b}", "bdwT"], [("cv", pb, j)], bias=bdwT[:, l, j:j + 1])

    loadU(0)
    if len(bl) > 1:
        loadU(1)
    if V.after_first_loads is not None:
        V.after_first_loads(["E", "U0", "U1", "wpw"])
    convmm(0)
    for bi, (kind, t0, n) in enumerate(bl):
        pb = bi % 2
        kd = V.kinds[kind]
        cv = cvb[pb]
        if bi + 2 < len(bl):
            loadU(bi + 2)
        if bi + 1 < len(bl):
            convmm(bi + 1)
        b = C.bank()
        for j in range(2):
            C.mm(C.ps[b][:, 0:n], V.ones_f[:, :], cv[:, j, 0:n], j == 0, j == 1, [("cv", pb, j), "ones_f"], [f"ps{b}"])
        C.act(mean[:, 0:n], C.ps[b][:, 0:n], AF.Identity, [f"ps{b}"], ["mean"], scale=1.0 / 256)
        for j in range(2):
            C.tt("dve", xc[:, j, 0:n], cv[:, j, 0:n], mean[:, 0:n], ALU.subtract, [("cv", pb, j), "mean"], ["xc"])
        C.act(sqc[:, :, 0:n], xc[:, :, 0:n], AF.Square, ["xc"], ["sqc"])
        b = C.bank()
        for j in range(2):
            C.mm(C.ps[b][:, 0:n], V.ones_f[:, :], sqc[:, j, 0:n], j == 0, j == 1, ["sqc", "ones_f"], [f"ps{b}"])
        C.act(rsc[:, 0:n], C.ps[b][:, 0:n], AF.Sqrt, [f"ps{b}", "eps"], ["rsc"], bias=V.eps_t[:, 0:1], scale=1.0 / 256)
        C.recip(rstdc[:, 0:n], rsc[:, 0:n], ["rsc"], ["rstdc"])
        for j in range(2):
            C.tt("dve", tmpc[j][:, 0:n], xc[:, j, 0:n], rstdc[:, 0:n], ALU.mult, ["xc", "rstdc"], [f"tmpc{j}"])
            C.act(sl[:, j, 0:n], tmpc[j][:, 0:n], AF.Silu, [f"tmpc{j}", "lngT", "lnbT"], ["sl"],
                  bias=lnbT[:, l, j:j + 1], scale=lngT[:, l, j:j + 1])
        for oc in range(2):
            b = C.bank()
            for j in range(2):
                C.mm(C.ps[b][:, 0:n], wpw[:, j, oc * 128:(oc + 1) * 128], sl[:, j, 0:n], j == 0, j == 1, ["sl", "wpw"], [f"ps{b}"])
            C.cp("act" if oc else "dve", ycst[pb][:, oc, 0:n], C.ps[b][:, 0:n], [f"ps{b}"], [f"ycst{pb}"])
        C.dma("sp", kd["mix"].ap().rearrange("kc p n -> p kc n")[:, 6:8, t0:t0 + n], ycst[pb][:, :, 0:n], [f"ycst{pb}"], [])
    C.reset(m)


def _norm_mod(C, V, xbt, xkey, n, l, j, gm, sh_off, sq, rs, rstd, tmp, hout, hkey, sqkey="sqN"):
    C.act(sq[:, :, 0:n], xbt[:, :, 0:n], AF.Square, [xkey], [sqkey])
    b = C.bank()
    for kc in range(8):
        C.mm(C.ps[b][:, 0:n], V.ones_bf[:, :], sq[:, kc, 0:n], kc == 0, kc == 7, [sqkey, "ones_bf"], [f"ps{b}"])
    C.act(rs[:, 0:n], C.ps[b][:, 0:n], AF.Sqrt, [f"ps{b}", "eps"], ["rsN"], bias=V.eps_t[:, 0:1], scale=1.0 / D)
    C.recip(rstd[:, 0:n], rs[:, 0:n], ["rsN"], ["rstdN"])
    for kc in range(8):
        C.tt("dve", tmp[kc % 2][:, 0:n], xbt[:, kc, 0:n], rstd[:, 0:n], ALU.mult, [xkey, "rstdN"], [f"tmpN{kc % 2}"])
        C.act(hout[:, kc, 0:n], tmp[kc % 2][:, 0:n], AF.Identity, [f"tmpN{kc % 2}", "gm", "modT"], [hkey],
              bias=V.modT[:, l, j, sh_off + kc:sh_off + kc + 1], scale=gm[:, l, j, kc:kc + 1])


def phase_out(C, V):
    l, last = V.l, V.last
    m = C.mark()
    wo = V.wo
    M = [C.alloc(f"M{i}", [128, 8, 512], BF16) for i in range(2)]
    xb0 = C.alloc("xbO0", [128, 8, 512], F32)
    xb = [xb0, xb0]
    rs = C.alloc("rsO", [128, 512], F32); rstd = C.alloc("rstdO", [128, 512], F32)
    tmp = [C.alloc(f"tmpO{i}", [128, 512], F32) for i in range(2)]
    h20 = C.alloc("h2O0", [128, 8, 512], BF16)
    h2 = [h20, h20]
    bl = V.blocks(with_ctx=not last)

    def load(bi):
        kind, t0, n = bl[bi]
        kd = V.kinds[kind]
        C.dma("sp", M[bi % 2][:, :, 0:n], V.tview(kd["mix"], t0, n), [], [f"M{bi % 2}"])
    load(0)
    for bi, (kind, t0, n) in enumerate(bl):
        pb = bi % 2
        kd = V.kinds[kind]
        j = kd["j"]
        if bi + 1 < len(bl):
            load(bi + 1)
        C.dma("sp", xb[pb][:, :, 0:n], V.tview(kd["xT"], t0, n), [("xT", kind, t0)], ["xbO0"])
        for oc in range(8):
            b = C.bank()
            for kc in range(8):
                C.mm(C.ps[b][:, 0:n], wo[:, kc, oc * 128:(oc + 1) * 128], M[pb][:, kc, 0:n], kc == 0, kc == 7,
                     [f"M{pb}", ("wo", kc)], [f"ps{b}"])
            C.stt("dve", xb[pb][:, oc, 0:n], C.ps[b][:, 0:n], V.modT[:, l, j, 16 + oc:17 + oc], xb[pb][:, oc, 0:n],
                  ALU.mult, ALU.add, [f"ps{b}", "modT", "xbO0"], ["xbO0"])
        C.dma("sp", V.tview(kd["xT"], t0, n), xb[pb][:, :, 0:n], ["xbO0"], [("xT", kind, t0)])
        _norm_mod(C, V, xb[pb], "xbO0", n, l, j, V.gm2, 24, h2[pb], rs, rstd, tmp, h2[pb], "h2O0", sqkey="h2O0")
        C.dma("sp", V.tview(kd["h2"], t0, n), h2[pb][:, :, 0:n], ["h2O0"], [])
    C.reset(m)


def phase_mlp(C, V):
    l, last = V.l, V.last
    m = C.mark()
    w1 = V.w1
    w2 = V.w2
    h2 = [C.alloc(f"h2M{i}", [128, 8, 512], BF16) for i in range(2)]
    x1 = C.alloc("x1M", [128, 8, 512], F32)
    hid = C.alloc("hid", [128, 32, 512], BF16)
    rt = [C.alloc(f"rt{i}", [128, 512], F32) for i in range(2)]
    bl = V.blocks(with_ctx=not last)

    def load(bi):
        kind, t0, n = bl[bi]
        C.dma("sp", h2[bi % 2][:, :, 0:n], V.tview(V.kinds[kind]["h2"], t0, n), [], [f"h2M{bi % 2}"])
    load(0)
    for bi, (kind, t0, n) in enumerate(bl):
        pb = bi % 2
        kd = V.kinds[kind]
        j = kd["j"]
        if bi + 1 < len(bl):
            load(bi + 1)
        C.dma("sp", x1[:, :, 0:n], V.tview(kd["xT"], t0, n), [], ["x1M"])
        for jc in range(32):
            b = C.bank()
            for kc in range(8):
                C.mm(C.ps[b][:, 0:n], w1[:, kc, jc * 128:(jc + 1) * 128], h2[pb][:, kc, 0:n], kc == 0, kc == 7,
                     [f"h2M{pb}", ("w1", kc)], [f"ps{b}"])
            C.act(rt[jc % 2][:, 0:n], C.ps[b][:, 0:n], AF.Relu, [f"ps{b}"], [f"rt{jc % 2}"])
            C.tt("pool" if jc % 2 else "dve", hid[:, jc, 0:n], rt[jc % 2][:, 0:n], rt[jc % 2][:, 0:n], ALU.mult,
                 [f"rt{jc % 2}"], [("hid", jc % 2)])
        for oc in range(8):
            b = C.bank()
            for jc in range(32):
                C.mm(C.ps[b][:, 0:n], w2[:, jc, oc * 128:(oc + 1) * 128], hid[:, jc, 0:n], jc == 0, jc == 31,
                     [("hid", 0), ("hid", 1), ("w2", jc)], [f"ps{b}"])
            C.stt("dve", x1[:, oc, 0:n], C.ps[b][:, 0:n], V.modT[:, l, j, 40 + oc:41 + oc], x1[:, oc, 0:n],
                  ALU.mult, ALU.add, [f"ps{b}", "modT", "x1M"], ["x1M"])
        C.dma("sp", V.tview(kd["xT"], t0, n), x1[:, :, 0:n], ["x1M"], [])
    C.reset(m)


def phase_final(C, V):
    m = C.mark()
    xb = [C.alloc(f"xbF{i}", [128, 8, 512], F32) for i in range(2)]
    sq = C.alloc("sqF", [128, 8, 512], BF16)
    rs = C.alloc("rsF", [128, 512], F32); rstd = C.alloc("rstdF", [128, 512], F32)
    y = C.alloc("yF", [128, 8, 512], F32)
    otok = [C.alloc(f"otok{i}", [128, 4, D], F32) for i in range(2)]
    bl = V.blocks(with_ctx=False)

    def load(bi):
        kind, t0, n = bl[bi]
        C.dma("sp", xb[bi % 2][:, :, 0:n], V.tview(V.xT, t0, n), [], [f"xbF{bi % 2}"])
    load(0)
    cnt = 0
    for bi, (kind, t0, n) in enumerate(bl):
        pb = bi % 2
        if bi + 1 < len(bl):
            load(bi + 1)
        C.act(sq[:, :, 0:n], xb[pb][:, :, 0:n], AF.Square, [f"xbF{pb}"], ["sqF"])
        b = C.bank()
        for kc in range(8):
            C.mm(C.ps[b][:, 0:n], V.ones_bf[:, :], sq[:, kc, 0:n], kc == 0, kc == 7, ["sqF", "ones_bf"], [f"ps{b}"])
        C.act(rs[:, 0:n], C.ps[b][:, 0:n], AF.Sqrt, [f"ps{b}", "eps"], ["rsF"], bias=V.eps_t[:, 0:1], scale=1.0 / D)
        C.recip(rstd[:, 0:n], rs[:, 0:n], ["rsF"], ["rstdF"])
        for kc in range(8):
            C.stt("dve", y[:, kc, 0:n], xb[pb][:, kc, 0:n], V.fngT[:, kc:kc + 1], rstd[:, 0:n], ALU.mult, ALU.mult,
                  [f"xbF{pb}", "rstdF", "fngT"], ["yF"])
        for tt_ in range(n // 128):
            for kc2 in range(0, 8, 4):
                b = C.bank()
                for kc in range(kc2, kc2 + 4):
                    C.tr(C.ps[b][:, (kc - kc2) * 128:(kc - kc2 + 1) * 128], y[:, kc, tt_ * 128:(tt_ + 1) * 128], V.ident[:, :],
                         ["yF", "ident"], [f"ps{b}"])
                C.cp("act" if cnt % 2 else "dve", otok[pb][:, tt_, kc2 * 128:kc2 * 128 + 512], C.ps[b][:, :], [f"ps{b}"], [f"otok{pb}"])
                cnt += 1
        C.dma("sp", V.out_d.ap()[t0:t0 + n, :].rearrange("(tt p) d -> p tt d", p=128), otok[pb][:, 0:n // 128, :], [f"otok{pb}"], [])
    C.reset(m)


_BF = ml_dtypes.bfloat16


def _tables(NT, r):
    S = 128 * NT
    TC = S // 4
    N = S
    tb = {}
    t = np.arange(S)
    row = (t // 64).astype(np.float32)
    col = (t % 64).astype(np.float32)
    inv = (10000.0 ** (-np.arange(0, 16, 2, dtype=np.float32) / 16)).astype(np.float32)
    ang = np.concatenate([row[:, None] * inv, col[:, None] * inv], axis=-1).astype(np.float32)
    cos = np.cos(ang).astype(np.float32)[r * TC:(r + 1) * TC].T
    sin = np.sin(ang).astype(np.float32)[r * TC:(r + 1) * TC].T
    tb["rope1"] = np.ascontiguousarray(np.concatenate([cos, cos], 0))
    tb["rope2"] = np.ascontiguousarray(np.concatenate([-sin, sin], 0))
    n1 = np.arange(NT)[:, None]; k1 = np.arange(NT)[None, :]
    a = 2 * np.pi * ((n1 * k1) % NT) / NT
    tb["fft_cs"] = np.concatenate([np.cos(a), np.sin(a)], 1).astype(_BF)
    n2 = np.arange(128)[:, None, None]
    k1 = np.arange(NT)[None, :, None]
    k2 = np.arange(32)[None, None, :]
    k = k1 + NT * (32 * r + k2)
    a = 2 * np.pi * ((n2 * k) % N) / N
    wc, ws = np.cos(a), np.sin(a)
    tb["fft_r1"] = np.concatenate([wc, ws], 2).reshape(128, NT * 64).astype(_BF)
    tb["fft_r2"] = np.concatenate([-ws, wc], 2).reshape(128, NT * 64).astype(_BF)
    c = np.arange(128)[:, None]; mm_ = np.arange(128)[None, :]
    same = (c // 64) == (mm_ // 64)
    a = 2 * np.pi * (((c % 64) * (mm_ % 64)) % 64) / 64
    for nm, nn in (("", N), ("c", CT)):
        sc = 1.0 / np.sqrt(nn * 64.0)
        tb["ccb" + nm] = (np.where(same, np.cos(a), 0.0) * sc).astype(_BF)
        tb["mscb" + nm] = (np.where(same, -np.sin(a), 0.0) * sc).astype(_BF)
    n = np.arange(CT)[:, None]; kk = np.arange(CT)[None, :]
    a = 2 * np.pi * ((n * kk) % CT) / CT
    tb["w256"] = np.concatenate([np.cos(a), np.sin(a)], 1).astype(_BF)
    tb["ident"] = np.eye(128, dtype=np.float32)
    mk = np.zeros((128, 8), np.float32)
    if r > 0:
        mk[:, r - 1] = 1.0
    if r < 3:
        mk[:, 4 + r + 1] = 1.0
    tb["masks"] = mk
    return tb


_CACHE = {}


def run_model(inputs, NT, L, dbg=None, dump=None):
    key = (NT, L, dbg, tuple(sorted(dump)) if dump else None)
    if key not in _CACHE:
        _CACHE[key] = build_program(NT, L, dbg=dbg, dump=dump)
    nc = _CACHE[key]
    S = 128 * NT
    TC = S // 4
    f32 = lambda a: np.ascontiguousarray(np.asarray(a, dtype=np.float32))
    wnames = ["w_mod", "b_mod", "norm1_g", "w_in", "q_norm_g", "w_uq", "kv_norm_g", "w_ukv", "w_fourier", "w_dw",
              "b_dw", "conv_ln_g", "conv_ln_b", "w_pw2", "w_o", "norm2_g", "w_mlp1", "w_mlp2"]
    shared = {n: f32(inputs[n])[:L] for n in wnames if n != "w_mod"}
    wmod_full = f32(inputs["w_mod"])[:L]
    wmod_sh = [np.ascontiguousarray(wmod_full[:, :, r * 1536:(r + 1) * 1536]) for r in range(4)]
    shared["final_norm_g"] = f32(inputs["final_norm_g"])
    x = f32(inputs["x"]); ctx = f32(inputs["ctx"]); c = f32(inputs["c"]); cc = f32(inputs["c_ctx"])
    in_maps = []
    for core in range(8):
        b, r = core // 4, core % 4
        m = dict(shared)
        m["x_in"] = np.ascontiguousarray(x[b, r * TC:(r + 1) * TC])
        m["ctx_in"] = np.ascontiguousarray(ctx[b])
        m["cvec"] = np.ascontiguousarray(np.stack([c[b], cc], 0))
        m["w_mod"] = wmod_sh[r]
        m.update(_tables(NT, r))
        in_maps.append(m)
    res = run_bass_kernel_spmd(nc, in_maps, core_ids=list(range(8)))
    return res


def kernel(**inputs):
    res = run_model(inputs, 128, 4)
    S = 128 * 128
    TC = S // 4
    out = np.empty((2, S, D), np.float32)
    for core in range(8):
        b, r = core // 4, core % 4
        out[b, r * TC:(r + 1) * TC] = np.asarray(res.results[core]["out"], dtype=np.float32)
    return out
```

```python
import contextlib
import os
SKIP = set(os.environ.get('KSKIP', '').split(','))
from types import SimpleNamespace
import numpy as np
import ml_dtypes
import concourse.bass as bass
import concourse.mybir as mybir
from concourse.bass_utils import run_bass_kernel_spmd

F32 = mybir.dt.float32
BF16 = mybir.dt.bfloat16
AF = mybir.ActivationFunctionType
ALU = mybir.AluOpType

ENGS = ("pe", "act", "dve", "pool", "sp")
EPOCH = 20000
NDMASEM = 24
NCCSEM = 3


class Op:
    __slots__ = ("eng", "fn", "deps", "signal", "dma", "sem", "val", "idx", "dslot", "inc")

    def __init__(self, eng, fn, dma):
        self.eng = eng
        self.fn = fn
        self.deps = []
        self.signal = False
        self.dma = dma
        self.sem = None
        self.val = 0
        self.idx = 0
        self.dslot = None
        self.inc = 16


class Prog:
    def __init__(self):
        self.ops = {e: [] for e in ENGS}
        self.last_w = {}
        self.readers = {}
        self.pending_dma = []
        self.all_last = {e: None for e in ENGS}
        self.persist_w = {}

    def op(self, eng, fn, reads=(), writes=(), dma=False, extra_deps=(), inc=16, persist=False):
        o = Op(eng, fn, dma)
        o.inc = inc
        pr = [k for k in reads if isinstance(k, str) and k.startswith("ps") and k[2:].isdigit()]
        if pr:
            writes = list(writes) + pr
        o.idx = len(self.ops[eng])
        deps = []
        for k in reads:
            w = self.last_w.get(k)
            if w is not None:
                deps.append((w, "raw"))
        for k in writes:
            w = self.last_w.get(k)
            if w is not None:
                deps.append((w, "waw"))
            for r in self.readers.get(k, ()):
                deps.append((r, "war"))
        for d in extra_deps:
            if d is not None:
                deps.append((d, "raw"))
        seen = set()
        for d, kind in deps:
            if d is o or id(d) in seen:
                continue
            need = True
            if d.eng == eng and not d.dma and not dma:
                if eng == "pe":
                    need = False
            if need:
                seen.add(id(d))
                o.deps.append(d)
                d.signal = True
        for k in reads:
            self.readers.setdefault(k, []).append(o)
        for k in writes:
            self.last_w[k] = o
            self.readers[k] = []
        self.ops[eng].append(o)
        if persist:
            for k in writes:
                self.persist_w[k] = o
        else:
            self.all_last[eng] = o
            if dma:
                self.pending_dma.append(o)
        return o

    def barrier(self):
        lasts = [self.all_last[e] for e in ENGS if self.all_last[e] is not None]
        pend = list(self.pending_dma)
        self.pending_dma = []
        for e in ENGS:
            self.op(e, None, extra_deps=lasts + pend)
        self.last_w = dict(self.persist_w)
        self.readers = {}

    def emit(self, nc):
        n_sems_needed = 0
        plan = {}
        for e in ENGS:
            cnt = 0
            epoch = 0
            for o in self.ops[e]:
                if o.dma:
                    continue
                if o.signal:
                    if cnt >= EPOCH:
                        epoch += 1
                        cnt = 0
                    cnt += 1
                    o.sem = (e, epoch)
                    o.val = cnt
            plan[e] = epoch + 1
        for e in ENGS:
            k = 0
            kc_ = 0
            for o in self.ops[e]:
                if o.dma:
                    if o.inc == 16:
                        o.dslot = k % NDMASEM
                        k += 1
                    else:
                        o.dslot = NDMASEM + (kc_ % NCCSEM)
                        kc_ += 1
        with contextlib.ExitStack() as st:
            sems = {}
            for e in ENGS:
                for ep in range(plan[e]):
                    sems[(e, ep)] = st.enter_context(nc.semaphore(f"s_{e}_{ep}"))
            dsems = {}
            for e in ENGS:
                if any(o.dma for o in self.ops[e]):
                    for k in range(NDMASEM + NCCSEM):
                        dsems[(e, k)] = st.enter_context(nc.semaphore(f"d_{e}_{k}"))
            block = st.enter_context(nc.Block())
            engmap = {"pe": block.tensor, "act": block.scalar, "dve": block.vector,
                      "pool": block.gpsimd, "sp": block.sync}

            def make(e):
                def body(eng):
                    waited = {}
                    dcount = {k: 0 for k in range(NDMASEM + NCCSEM)}
                    dlast = {}
                    for o in self.ops[e]:
                        for d in o.deps:
                            if d.dma:
                                key = ("d", d.eng, d.dslot)
                                s = dsems[(d.eng, d.dslot)]
                                v = d.val
                            else:
                                key = d.sem
                                s = sems[d.sem]
                                v = d.val
                            if waited.get(key, 0) >= v:
                                continue
                            waited[key] = v
                            eng.wait_ge(s, v)
                        if o.dma:
                            key = ("d", e, o.dslot)
                            prev = dcount[o.dslot]
                            if prev > 0 and waited.get(key, 0) < prev:
                                eng.wait_ge(dsems[(e, o.dslot)], prev)
                                waited[key] = prev
                            ins = o.fn(eng)
                            dcount[o.dslot] = prev + o.inc
                            o.val = prev + o.inc
                            ins.then_inc(dsems[(e, o.dslot)], o.inc)
                        else:
                            if o.fn is None:
                                if o.signal:
                                    eng.nop().then_inc(sems[o.sem], 1)
                                continue
                            ins = o.fn(eng)
                            if o.signal:
                                ins.then_inc(sems[o.sem], 1)
                return body

            for e in ENGS:
                dc = {k: 0 for k in range(NDMASEM + NCCSEM)}
                for o in self.ops[e]:
                    if o.dma:
                        dc[o.dslot] += o.inc
                        o.val = dc[o.dslot]
            for e in ENGS:
                if self.ops[e]:
                    engmap[e](make(e))


D = 1024
KC = 8
DIN = 1184
DFF = 4096
NH = 8
CT = 256
EPS = 1e-6
ATTN_SCALE = 96.0 ** -0.5
SB_BASE = 16512
SB_TOP = 229376 - 64


def _dtsize(dt):
    return 4 if dt == F32 else 2


class Ctx:
    def __init__(self, nc):
        self.nc = nc
        self.P = Prog()
        self.off = SB_BASE
        self.limit = SB_TOP
        self.uid = 0
        self.ps = [nc.alloc_psum_tensor(f"ps{i}", [128, 512], F32) for i in range(8)]
        self.psi = 0

    def alloc(self, name, shape, dt):
        n = 1
        for s in shape[1:]:
            n *= s
        nbytes = n * _dtsize(dt)
        off = (self.off + 63) // 64 * 64
        assert off + nbytes <= self.limit, (name, off, nbytes, self.limit)
        self.off = off + nbytes
        self.uid += 1
        return self.nc.alloc_sbuf_tensor_at(f"{name}_{self.uid}", list(shape), dt, offset=off)

    def alloc_at(self, name, shape, dt, off):
        self.uid += 1
        return self.nc.alloc_sbuf_tensor_at(f"{name}_{self.uid}", list(shape), dt, offset=off)

    def mark(self):
        return self.off

    def reset(self, m):
        self.P.barrier()
        self.off = m

    def bank(self, lo=0, hi=8):
        i = lo + (self.psi % (hi - lo))
        self.psi += 1
        return i

    def mm(self, out, lhsT, rhs, start, stop, r, w):
        return self.P.op("pe", lambda e: e.matmul(out, lhsT=lhsT, rhs=rhs, start=start, stop=stop),
                         reads=r, writes=w)

    def tr(self, out, in_, idn, r, w):
        return self.P.op("pe", lambda e: e.transpose(out, in_, idn), reads=r, writes=w)

    def act(self, out, in_, func, r, w, bias=None, scale=None):
        kw = {}
        if bias is not None:
            kw["bias"] = bias
        if scale is not None:
            kw["scale"] = scale
        return self.P.op("act", lambda e: e.activation(out, in_, func, **kw), reads=r, writes=w)

    def tt(self, eng, out, in0, in1, op, r, w):
        return self.P.op(eng, lambda e: e.tensor_tensor(out, in0, in1, op), reads=r, writes=w)

    def ts(self, eng, out, in0, s1, s2, op0, op1, r, w):
        if s2 is None:
            return self.P.op(eng, lambda e: e.tensor_scalar(out, in0, s1, None, op0), reads=r, writes=w)
        return self.P.op(eng, lambda e: e.tensor_scalar(out, in0, s1, s2, op0, op1), reads=r, writes=w)

    def stt(self, eng, out, in0, scalar, in1, op0, op1, r, w):
        return self.P.op(eng, lambda e: e.scalar_tensor_tensor(out, in0, scalar, in1, op0, op1),
                         reads=r, writes=w)

    def cp(self, eng, out, in_, r, w):
        if eng == "act":
            return self.P.op("act", lambda e: e.copy(out, in_), reads=r, writes=w)
        return self.P.op(eng, lambda e: e.tensor_copy(out, in_), reads=r, writes=w)

    def recip(self, out, in_, r, w):
        return self.P.op("dve", lambda e: e.reciprocal(out, in_), reads=r, writes=w)

    def memset(self, eng, ap, val, w):
        return self.P.op(eng, lambda e: e.memset(ap, val), writes=w)

    def dma(self, q, out, in_, r, w, persist=False):
        return self.P.op(q, lambda e: e.dma_start(out=out, in_=in_), reads=r, writes=w, dma=True, persist=persist)

    def allgather(self, src, dst, r, w, persist=False):
        return self.P.op("pool", lambda e: e.collective_compute(
            "AllGather", ALU.bypass, replica_groups=[[0, 1, 2, 3], [4, 5, 6, 7]],
            ins=[src], outs=[dst]), reads=r, writes=w, dma=True, inc=1, persist=persist)


def build_program(NT, L, dbg=None, dump=None):
    S = 128 * NT
    TC = S // 4
    NB = TC // 512
    TCt = TC // 128
    NKT = 2 + NT
    NK = NKT * 128
    assert TC % 512 == 0

    nc = bass.Bass("TRN2", target_bir_lowering=False)
    C = Ctx(nc)
    P = C.P

    def din(name, shape, dt=F32):
        return nc.dram_tensor(name, list(shape), dt, kind="ExternalInput")

    def dtmp(name, shape, dt):
        if dump and name in dump:
            return nc.dram_tensor(name, list(shape), dt, kind="ExternalOutput")
        return nc.dram_tensor(name, list(shape), dt)

    x_in = din("x_in", [TC, D])
    ctx_in = din("ctx_in", [CT, D])
    cvec = din("cvec", [2, D])
    w_mod = din("w_mod", [L, D, 1536]); b_mod = din("b_mod", [L, 6 * D])
    norm1_g = din("norm1_g", [L, D]); w_in = din("w_in", [L, D, DIN])
    q_norm_g = din("q_norm_g", [L, 256]); w_uq = din("w_uq", [L, 256, 768])
    kv_norm_g = din("kv_norm_g", [L, 128]); w_ukv = din("w_ukv", [L, 128, 1024])
    w_fourier = din("w_fourier", [L, 256, 256]); w_dw = din("w_dw", [L, 31, 256])
    b_dw = din("b_dw", [L, 256]); conv_ln_g = din("conv_ln_g", [L, 256]); conv_ln_b = din("conv_ln_b", [L, 256])
    w_pw2 = din("w_pw2", [L, 256, 256]); w_o = din("w_o", [L, D, D]); norm2_g = din("norm2_g", [L, D])
    w_mlp1 = din("w_mlp1", [L, D, DFF]); w_mlp2 = din("w_mlp2", [L, DFF, D])
    final_norm_g = din("final_norm_g", [D])
    rope1 = din("rope1", [32, TC]); rope2 = din("rope2", [32, TC])
    fft_cs = din("fft_cs", [NT, 2 * NT], BF16)
    fft_r1 = din("fft_r1", [128, NT * 64], BF16); fft_r2 = din("fft_r2", [128, NT * 64], BF16)
    ccb_d = din("ccb", [128, 128], BF16); mscb_d = din("mscb", [128, 128], BF16)
    w256_d = din("w256", [256, 512], BF16)
    ccbc_d = din("ccbc", [128, 128], BF16); mscbc_d = din("mscbc", [128, 128], BF16)
    ident_d = din("ident", [128, 128]); masks_d = din("masks", [128, 8])
    out_d = nc.dram_tensor("out", [TC, D], F32, kind="ExternalOutput")

    xT = dtmp("xT", [KC, 128, TC], F32); cT = dtmp("cT", [KC, 128, CT], F32)
    h2x = dtmp("h2x", [KC, 128, TC], BF16); h2c = dtmp("h2c", [KC, 128, CT], BF16)
    qx = dtmp("qx", [NH, 96, TC], BF16); qc = dtmp("qc", [NH, 96, CT], BF16)
    TCq = min(TC, 1024); NKV = TC // TCq
    FR = min(2 * TC, 2048); NF = 2 * TC // FR
    kv_own = [dtmp(f"kv_own{i}", [160, TCq], BF16) for i in range(NKV)]
    kv_all = [dtmp(f"kv_all{i}", [640, TCq], BF16) for i in range(NKV)]
    kvc = dtmp("kvc", [160, CT], BF16)
    f_own = [dtmp(f"f_own{i}", [FR, 128], BF16) for i in range(NF)]
    f_all = [dtmp(f"f_all{i}", [4 * FR, 128], BF16) for i in range(NF)]
    fc = dtmp("fc", [CT, 256], BF16)
    upx = dtmp("upx", [2, 128, TC + 32], BF16); upc = dtmp("upc", [2, 128, CT + 32], BF16)
    e_own = dtmp("e_own", [256, 32], BF16); e_all = dtmp("e_all", [1024, 32], BF16)
    mixx = dtmp("mixx", [KC, 128, TC], BF16); mixc = dtmp("mixc", [KC, 128, CT], BF16)

    kinds = {
        "x": dict(j=0, T=TC, xT=xT, h2=h2x, q=qx, kv=None, up=upx, mix=mixx),
        "c": dict(j=1, T=CT, xT=cT, h2=h2c, q=qc, kv=kvc, up=upc, mix=mixc),
    }

    def blocks(with_ctx=True):
        bl = []
        if with_ctx:
            bl.append(("c", 0, CT))
        for b in range(NB):
            bl.append(("x", b * 512, 512))
        return bl

    def tview(dt_, t0, n):
        return dt_.ap().rearrange("kc p n -> p kc n")[:, :, t0:t0 + n]

    ident = C.alloc("ident", [128, 128], F32)
    ones_bf = C.alloc("ones_bf", [128, 128], BF16)
    ones_f = C.alloc("ones_f", [128, 128], F32)
    eps_t = C.alloc("eps", [128, 1], F32)
    masks = C.alloc("masks", [128, 8], F32)
    modT = C.alloc("modT", [128, L, 2, 48], F32)
    bmodT = C.alloc("bmodT", [128, L, 48], F32)
    n1gT = C.alloc("n1gT", [128, L, 8], F32); n2gT = C.alloc("n2gT", [128, L, 8], F32)
    gm1 = C.alloc("gm1", [128, L, 2, 8], F32); gm2 = C.alloc("gm2", [128, L, 2, 8], F32)
    qgT = C.alloc("qgT", [128, L, 2], F32); kvgT = C.alloc("kvgT", [128, L, 1], F32)
    bdwT = C.alloc("bdwT", [128, L, 2], F32); lngT = C.alloc("lngT", [128, L, 2], F32)
    lnbT = C.alloc("lnbT", [128, L, 2], F32); wdwT = C.alloc("wdwT", [128, L, 62], F32)
    fngT = C.alloc("fngT", [128, 8], F32); cvT = C.alloc("cvT", [128, 16], F32)
    scT = C.alloc("scT", [128, 16], BF16)
    wukv_t = C.alloc("wukv", [128, 1024], BF16)
    PERSIST = C.mark()

    C.dma("sp", ident[:, :], ident_d.ap(), [], ["ident"])
    C.dma("sp", masks[:, :], masks_d.ap(), [], ["masks"])
    C.memset("dve", ones_bf[:, :], 1.0, ["ones_bf"])
    C.memset("dve", ones_f[:, :], 1.0, ["ones_f"])
    C.memset("dve", eps_t[:, :], EPS, ["eps"])

    stg = C.alloc("stg", [128, 128], F32)
    stg2 = C.alloc("stg2", [128, 128], F32)

    def rows128(ap1d, n):
        return ap1d.rearrange("(r p) -> r p", p=128)

    def vec_transpose(stage, key, nrows, dsts):
        b = C.bank()
        C.tr(C.ps[b][:, 0:nrows], stage[0:nrows, :], ident[0:nrows, 0:nrows], [key, "ident"], [f"ps{b}"])
        for dst, c0, c1, wk in dsts:
            C.cp("dve", dst, C.ps[b][:, c0:c1], [f"ps{b}"], [wk])

    for l in range(L):
        specs = [(rows128(b_mod.ap()[l], 48), 48), (rows128(norm1_g.ap()[l], 8), 8),
                 (rows128(norm2_g.ap()[l], 8), 8), (rows128(q_norm_g.ap()[l], 2), 2),
                 (rows128(kv_norm_g.ap()[l], 1), 1), (rows128(b_dw.ap()[l], 2), 2),
                 (rows128(conv_ln_g.ap()[l], 2), 2), (rows128(conv_ln_b.ap()[l], 2), 2)]
        r0 = 0
        for src, n in specs:
            C.dma("sp", stg[r0:r0 + n, :], src, [], ["stg"])
            r0 += n
        vec_transpose(stg, "stg", 73, [
            (bmodT[:, l, :], 0, 48, "bmodT"), (n1gT[:, l, :], 48, 56, "n1gT"), (n2gT[:, l, :], 56, 64, "n2gT"),
            (qgT[:, l, :], 64, 66, "qgT"), (kvgT[:, l, :], 66, 67, "kvgT"), (bdwT[:, l, :], 67, 69, "bdwT"),
            (lngT[:, l, :], 69, 71, "lngT"), (lnbT[:, l, :], 71, 73, "lnbT")])
        C.dma("sp", stg2[0:62, :], w_dw.ap()[l].rearrange("k (j p) -> (k j) p", p=128), [], ["stg2"])
        vec_transpose(stg2, "stg2", 62, [(wdwT[:, l, :], 0, 62, "wdwT")])
    C.dma("sp", stg[0:16, :], cvec.ap().rearrange("j (kc p) -> (j kc) p", p=128), [], ["stg"])
    C.dma("sp", stg[16:24, :], rows128(final_norm_g.ap(), 8), [], ["stg"])
    vec_transpose(stg, "stg", 24, [(cvT[:, :], 0, 16, "cvT"), (fngT[:, :], 16, 24, "fngT")])
    C.act(scT[:, :], cvT[:, :], AF.Silu, ["cvT"], ["scT"])

    m0 = C.mark()
    wm = [C.alloc(f"wm{i}", [128, 8, 1536], BF16) for i in range(2)]
    mpart = C.alloc("mpart", [128, L, 2, 12], F32)
    mp_own = dtmp("mp_own", [128, L * 24], F32)
    mp_all = dtmp("mp_all", [512, L * 24], F32)
    for l in range(L):
        wmt = wm[l % 2]
        for kc in range(8):
            C.dma("pool", wmt[:, kc, :], w_mod.ap()[l, kc * 128:(kc + 1) * 128, :], [], [("wm", l % 2, kc)])
        bm = C.bank()
        for oc in range(12):
            for kc in range(8):
                C.mm(C.ps[bm][:, oc * 2:oc * 2 + 2], wmt[:, kc, oc * 128:(oc + 1) * 128],
                     bass.AP(scT, kc, [[16, 128], [8, 2]]), kc == 0, kc == 7,
                     [("wm", l % 2, kc), "scT"], [f"ps{bm}"])
        for j in range(2):
            C.cp("dve", mpart[:, l, j, :], bass.AP(C.ps[bm], j, [[512, 128], [2, 12]]), [f"ps{bm}"], ["mpart"])
    C.dma("sp", mp_own.ap(), mpart[:, :, :, :].rearrange("p l j o -> p (l j o)"), ["mpart"], ["mp_own"])
    C.allgather(mp_own.ap(), mp_all.ap(), ["mp_own"], ["mp_all"])
    for r in range(4):
        C.dma("sp", modT[:, :, :, r * 12:(r + 1) * 12],
              mp_all.ap()[r * 128:(r + 1) * 128, :].rearrange("p (l j o) -> p l j o", l=L, j=2), ["mp_all"], [("modraw", r)])
    mrk = [("modraw", r) for r in range(4)]
    for l in range(L):
        for j in range(2):
            C.tt("dve", modT[:, l, j, :], modT[:, l, j, :], bmodT[:, l, :], ALU.add, mrk + ["bmodT"], ["modT"])
        for j in range(2):
            C.stt("dve", gm1[:, l, j, :], modT[:, l, j, 8:16], 1.0, n1gT[:, l, :], ALU.add, ALU.mult,
                  ["modT", "n1gT"], ["gm1"])
            C.stt("dve", gm2[:, l, j, :], modT[:, l, j, 32:40], 1.0, n2gT[:, l, :], ALU.add, ALU.mult,
                  ["modT", "n2gT"], ["gm2"])
    C.reset(m0)

    m0 = C.mark()
    xtok = [C.alloc(f"xtok{i}", [128, 4, D], F32) for i in range(2)]
    xbt = [C.alloc(f"xbt{i}", [128, 8, 512], F32) for i in range(2)]
    for bi, (kind, t0, n) in enumerate(blocks()):
        pb = bi % 2
        src = ctx_in if kind == "c" else x_in
        ntt = n // 128
        C.dma("sp", xtok[pb][:, 0:ntt, :], src.ap()[t0:t0 + n, :].rearrange("(tt p) d -> p tt d", p=128),
              [], [f"xtok{pb}"])
        for kc in range(8):
            b = C.bank()
            for tt_ in range(ntt):
                C.tr(C.ps[b][:, tt_ * 128:(tt_ + 1) * 128], xtok[pb][:, tt_, kc * 128:(kc + 1) * 128], ident[:, :],
                     [f"xtok{pb}", "ident"], [f"ps{b}"])
            C.cp("act" if kc % 2 else "dve", xbt[pb][:, kc, 0:n], C.ps[b][:, 0:n], [f"ps{b}"], [f"xbt{pb}"])
        C.dma("sp", tview(kinds[kind]["xT"], t0, n), xbt[pb][:, :, 0:n], [f"xbt{pb}"], [("xT", kind, t0)])
    zt = C.alloc("zt", [128, 2, 16], BF16)
    C.memset("dve", zt[:, :, :], 0.0, ["zt"])
    C.dma("sp", upc.ap().rearrange("j p n -> p j n")[:, :, 0:16], zt[:, :, :], ["zt"], ["upc_h"])
    C.dma("sp", upc.ap().rearrange("j p n -> p j n")[:, :, CT + 16:CT + 32], zt[:, :, :], ["zt"], ["upc_h"])
    C.reset(m0)

    def rms_stats(src3, n, nk, inv_n, sqt, sskey_r, tag):
        C.act(sqt[:, 0:nk, 0:n], src3, AF.Square, [sskey_r], ["sq" + tag])
        b = C.bank()
        for kc in range(nk):
            C.mm(C.ps[b][:, 0:n], ones_bf[:, :], sqt[:, kc, 0:n], kc == 0, kc == nk - 1,
                 ["sq" + tag, "ones_bf"], [f"ps{b}"])
        return b

    for l in range(L):
        last = (l == L - 1)
        if dbg == "P":
            break
        mA = C.mark()
        w_in_t = C.alloc("w_in_t", [128, 8, DIN], BF16)
        krsw = C.alloc("krsw", [128, 8, 32], BF16)
        w_uq_t = C.alloc("w_uq_t", [128, 2, 768], BF16)
        w_uq_s = C.alloc("w_uq_s", [128, 2, 768], BF16)
        for kc in range(8):
            C.dma("pool", w_in_t[:, kc, :], w_in.ap()[l, kc * 128:(kc + 1) * 128, :], [], [("w_in", kc)])
        C.dma("pool", wukv_t[:, :], w_ukv.ap()[l], [], ["wukv"], persist=True)
        C.dma("pool", krsw[:, :, 0:16], w_in.ap()[l].rearrange("(kc p) n -> p kc n", p=128)[:, :, 656:672], [], ["krsw0"])
        C.dma("pool", krsw[:, :, 16:32], w_in.ap()[l].rearrange("(kc p) n -> p kc n", p=128)[:, :, 640:656], [], ["krsw1"])
        uq_v = w_uq.ap()[l].rearrange("(j p) n -> p j n", p=128)
        C.dma("pool", w_uq_t[:, :, :], uq_v, [], ["w_uq"])
        C.dma("pool", w_uq_s[:, :, :], uq_v, [], ["w_uq_s"])
        uq_v4 = w_uq.ap()[l].rearrange("(j p) (h e) -> p j h e", p=128, e=96)
        s4 = w_uq_s[:, :, :].rearrange("p j (h e) -> p j h e", e=96)
        for j in range(2):
            C.dma("pool", s4[:, j, :, 64:80], uq_v4[:, j, :, 80:96], [], ["w_uq_s"])
            C.dma("pool", s4[:, j, :, 80:96], uq_v4[:, j, :, 64:80], [], ["w_uq_s"])

        xb = [C.alloc(f"xbA{i}", [128, 8, 512], F32) for i in range(2)]
        sq = C.alloc("sqA", [128, 8, 512], BF16)
        rs = C.alloc("rsA", [128, 512], F32)
        rstd = C.alloc("rstdA", [128, 512], F32)
        tmpA = [C.alloc(f"tmpA{i}", [128, 512], F32) for i in range(2)]
        hA = [C.alloc(f"hA{i}", [128, 8, 512], BF16) for i in range(2)]
        ftok = C.alloc("ftok", [128, 4, 256], BF16)
        cq = C.alloc("cq", [128, 2, 512], F32)
        sqq = C.alloc("sqq", [128, 2, 512], BF16)
        rq = C.alloc("rq", [128, 512], F32)
        cqn = C.alloc("cqn", [128, 2, 512], BF16)
        qst = C.alloc("qst", [128, NH, 512], BF16)
        rt1 = C.alloc("rt1", [128, 512], F32)
        rt2 = C.alloc("rt2", [128, 512], F32)
        rp1 = C.alloc("rp1", [128, 512], F32)
        rp2 = C.alloc("rp2", [128, 512], F32)
        ckv = C.alloc("ckv", [128, 512], F32)
        ckvn = C.alloc("ckvn", [128, 512], BF16)
        krr = C.alloc("krr", [128, 512], BF16)
        sg = C.alloc("sg", [128, 512], F32)
        ut = C.alloc("ut", [128, 2, 512], BF16)

        bl = blocks()
        def loadA(bi):
            kind, t0, n = bl[bi]
            C.dma("sp", xb[bi % 2][:, :, 0:n], tview(kinds[kind]["xT"], t0, n), [("xT", kind, t0)], [f"xbA{bi % 2}"])
        rsN = C.alloc("rsN", [128, 512], F32)

        def normA(bi):
            kind, t0, n = bl[bi]
            pb = bi % 2
            j = kinds[kind]["j"]
            b = rms_stats(xb[pb][:, :, 0:n], n, 8, None, sq, f"xbA{pb}", "A")
            C.act(rsN[:, 0:n], C.ps[b][:, 0:n], AF.Sqrt, [f"ps{b}", "eps"], ["rsN"], bias=eps_t[:, 0:1], scale=1.0 / D)
            C.recip(rstd[:, 0:n], rsN[:, 0:n], ["rsN"], ["rstdA"])
            for kc in range(8):
                C.tt("dve", tmpA[kc % 2][:, 0:n], xb[pb][:, kc, 0:n], rstd[:, 0:n], ALU.mult,
                     [f"xbA{pb}", "rstdA"], [f"tmpA{kc % 2}"])
                C.act(hA[pb][:, kc, 0:n], tmpA[kc % 2][:, 0:n], AF.Identity, [f"tmpA{kc % 2}", "gm1", "modT"], [f"hA{pb}"],
                      bias=modT[:, l, j, kc:kc + 1], scale=gm1[:, l, j, kc:kc + 1])

        loadA(0)
        if len(bl) > 1:
            loadA(1)
        normA(0)
        pend_ag = []
        defer_ag = []
        for bi, (kind, t0, n) in enumerate(bl):
            pb = bi % 2
            kd = kinds[kind]
            j = kd["j"]
            if bi + 1 < len(bl):
                normA(bi + 1)
            if bi + 2 < len(bl):
                loadA(bi + 2)
            for ag in pend_ag:
                C.allgather(*ag)
            pend_ag = []
            if kind == "x":
                C.dma("sp", rp1[64:96, 0:n], rope1.ap()[:, t0:t0 + n], [], ["rp1q"])
                C.dma("sp", rp2[64:96, 0:n], rope2.ap()[:, t0:t0 + n], [], ["rp2q"])
                C.dma("sp", rp1[0:32, 0:n], rope1.ap()[:, t0:t0 + n], [], ["rp1k"])
                C.dma("sp", rp2[0:32, 0:n], rope2.ap()[:, t0:t0 + n], [], ["rp2k"])
            hk = f"hA{pb}"
            h = hA[pb]
            do_rest = not (kind == "c" and last)
            if do_rest and 'f' not in SKIP:
                ntt = n // 128
                for t2 in range(0, ntt, 2):
                    b = C.bank()
                    for tt_ in range(t2, min(t2 + 2, ntt)):
                        for kc in range(8):
                            C.mm(C.ps[b][:, (tt_ - t2) * 256:(tt_ - t2 + 1) * 256], h[:, kc, tt_ * 128:(tt_ + 1) * 128],
                                 w_in_t[:, kc, 0:256], kc == 0, kc == 7, [hk, ("w_in", kc)], [f"ps{b}"])
                    nn = min(2, ntt - t2)
                    C.cp("act", ftok[:, t2:t2 + nn, :], C.ps[b][:, 0:nn * 256].rearrange("p (t c) -> p t c", c=256),
                         [f"ps{b}"], ["ftok"])
                if kind == "x":
                    for gp in range(2):
                        g0 = gp * TC + t0
                        C.dma("sp", f_own[g0 // FR].ap()[g0 % FR:g0 % FR + n, :].rearrange("(tt p) c -> p tt c", p=128),
                              ftok[:, 0:ntt, gp * 128:(gp + 1) * 128], ["ftok"], [("fo", g0 // FR, g0 % FR)])
                else:
                    C.dma("sp", fc.ap().rearrange("(tt p) c -> p tt c", p=128), ftok[:, 0:ntt, :], ["ftok"], ["fc"])
            if do_rest and 'q' not in SKIP:
                for jq in range(2):
                    b = C.bank()
                    for kc in range(8):
                        C.mm(C.ps[b][:, 0:n], w_in_t[:, kc, 256 + jq * 128:256 + (jq + 1) * 128], h[:, kc, 0:n],
                             kc == 0, kc == 7, [hk, ("w_in", kc)], [f"ps{b}"])
                    C.cp("dve", cq[:, jq, 0:n], C.ps[b][:, 0:n], [f"ps{b}"], ["cq"])
                    C.act(sqq[:, jq, 0:n], C.ps[b][:, 0:n], AF.Square, [f"ps{b}"], ["sqq"])
                b = C.bank()
                for jq in range(2):
                    C.mm(C.ps[b][:, 0:n], ones_bf[:, :], sqq[:, jq, 0:n], jq == 0, jq == 1, ["sqq", "ones_bf"], [f"ps{b}"])
                C.act(rs[:, 0:n], C.ps[b][:, 0:n], AF.Sqrt, [f"ps{b}", "eps"], ["rsA"], bias=eps_t[:, 0:1], scale=1.0 / 256)
                C.recip(rq[:, 0:n], rs[:, 0:n], ["rsA"], ["rq"])
                for jq in range(2):
                    C.stt("dve", cqn[:, jq, 0:n], cq[:, jq, 0:n], qgT[:, l, jq:jq + 1], rq[:, 0:n], ALU.mult, ALU.mult,
                          ["cq", "rq", "qgT"], ["cqn"])
            if 'kv' in SKIP:
                continue
            b = C.bank()
            for kc in range(8):
                C.mm(C.ps[b][:, 0:n], w_in_t[:, kc, 512:640], h[:, kc, 0:n], kc == 0, kc == 7, [hk, ("w_in", kc)], [f"ps{b}"])
            C.cp("dve", ckv[:, 0:n], C.ps[b][:, 0:n], [f"ps{b}"], ["ckv"])
            C.act(sqq[:, 0, 0:n], C.ps[b][:, 0:n], AF.Square, [f"ps{b}"], ["sqq"])
            b = C.bank()
            C.mm(C.ps[b][:, 0:n], ones_bf[:, :], sqq[:, 0, 0:n], True, True, ["sqq", "ones_bf"], [f"ps{b}"])
            C.act(rs[:, 0:n], C.ps[b][:, 0:n], AF.Sqrt, [f"ps{b}", "eps"], ["rsA"], bias=eps_t[:, 0:1], scale=1.0 / 128)
            C.recip(rq[:, 0:n], rs[:, 0:n], ["rsA"], ["rq"])
            C.stt("dve", ckvn[:, 0:n], ckv[:, 0:n], kvgT[:, l, 0:1], rq[:, 0:n], ALU.mult, ALU.mult,
                  ["ckv", "rq", "kvgT"], ["ckvn"])
            kvdst = kvc.ap()[:, 0:n] if kind == "c" else kv_own[t0 // TCq].ap()[:, t0 % TCq:t0 % TCq + n]
            C.dma("sp", kvdst[0:128, :], ckvn[:, 0:n], ["ckvn"], [("kvo", kind, t0, 0)])
            if 'kr' in SKIP:
                continue
            ba = C.bank()
            for kc in range(8):
                C.mm(C.ps[ba][0:32, 0:n], w_in_t[:, kc, 640:672], h[:, kc, 0:n], kc == 0, kc == 7, [hk, ("w_in", kc)], [f"ps{ba}"])
            if kind == "x":
                bb = C.bank()
                for kc in range(8):
                    C.mm(C.ps[bb][0:32, 0:n], krsw[:, kc, :], h[:, kc, 0:n], kc == 0, kc == 7, [hk, "krsw0", "krsw1"], [f"ps{bb}"])
                C.tt("dve", rt1[0:32, 0:n], C.ps[ba][0:32, 0:n], rp1[0:32, 0:n], ALU.mult, [f"ps{ba}", "rp1k"], ["rt1"])
                C.tt("dve", rt2[0:32, 0:n], C.ps[bb][0:32, 0:n], rp2[0:32, 0:n], ALU.mult, [f"ps{bb}", "rp2k"], ["rt2"])
                C.tt("pool", krr[0:32, 0:n], rt1[0:32, 0:n], rt2[0:32, 0:n], ALU.add, ["rt1", "rt2"], ["krr"])
            else:
                C.cp("act", krr[0:32, 0:n], C.ps[ba][0:32, 0:n], [f"ps{ba}"], ["krr"])
            C.dma("sp", kvdst[128:160, :], krr[0:32, 0:n], ["krr"], [("kvo", kind, t0, 1)])
            if kind == "x" and 'ag' not in SKIP:
                if (t0 + n) % TCq == 0:
                    ci = t0 // TCq
                    rk = [("kvo", "x", tt0, pp) for tt0 in range(ci * TCq, (ci + 1) * TCq, 512) for pp in range(2)]
                    pend_ag.append((kv_own[ci].ap(), kv_all[ci].ap(), rk, []))
                if do_rest and 'f' not in SKIP:
                    for gp in range(2):
                        g1 = gp * TC + t0 + n
                        if g1 % FR == 0:
                            ci = g1 // FR - 1
                            rk = [("fo", ci, off) for off in range(0, FR, 512)]
                            defer_ag.append((f_own[ci].ap(), f_all[ci].ap(), rk, [("f_all", ci)]))

            if do_rest and 'glu' not in SKIP:
                for jg in range(2):
                    ba = C.bank()
                    bb = C.bank()
                    for kc in range(8):
                        C.mm(C.ps[ba][:, 0:n], w_in_t[:, kc, 672 + jg * 128:672 + (jg + 1) * 128], h[:, kc, 0:n],
                             kc == 0, kc == 7, [hk, ("w_in", kc)], [f"ps{ba}"])
                    for kc in range(8):
                        C.mm(C.ps[bb][:, 0:n], w_in_t[:, kc, 928 + jg * 128:928 + (jg + 1) * 128], h[:, kc, 0:n],
                             kc == 0, kc == 7, [hk, ("w_in", kc)], [f"ps{bb}"])
                    C.act(sg[:, 0:n], C.ps[bb][:, 0:n], AF.Sigmoid, [f"ps{bb}"], ["sg"])
                    C.tt("dve", ut[:, jg, 0:n], C.ps[ba][:, 0:n], sg[:, 0:n], ALU.mult, [f"ps{ba}", "sg"], ["ut"])
                C.dma("sp", kd["up"].ap().rearrange("j p n -> p j n")[:, :, 16 + t0:16 + t0 + n], ut[:, :, 0:n],
                      ["ut"], [("up", kind)])
            if do_rest and 'q' not in SKIP:
                for hh in range(NH):
                    ba = C.bank()
                    for jq in range(2):
                        C.mm(C.ps[ba][0:96, 0:n], w_uq_t[:, jq, hh * 96:(hh + 1) * 96], cqn[:, jq, 0:n], jq == 0, jq == 1,
                             ["cqn", "w_uq"], [f"ps{ba}"])
                    C.cp("act", qst[0:64, hh, 0:n], C.ps[ba][0:64, 0:n], [f"ps{ba}"], [("qst", "n")])
                    if kind == "x":
                        bb = C.bank()
                        for jq in range(2):
                            C.mm(C.ps[bb][0:96, 0:n], w_uq_s[:, jq, hh * 96:(hh + 1) * 96], cqn[:, jq, 0:n], jq == 0, jq == 1,
                                 ["cqn", "w_uq_s"], [f"ps{bb}"])
                        C.tt("dve", rt1[64:96, 0:n], C.ps[ba][64:96, 0:n], rp1[64:96, 0:n], ALU.mult,
                             [f"ps{ba}", "rp1q"], ["rt1"])
                        C.tt("dve", rt2[64:96, 0:n], C.ps[bb][64:96, 0:n], rp2[64:96, 0:n], ALU.mult,
                             [f"ps{bb}", "rp2q"], ["rt2"])
                        C.tt("pool", qst[64:96, hh, 0:n], rt1[64:96, 0:n], rt2[64:96, 0:n], ALU.add,
                             ["rt1", "rt2"], [("qst", "r")])
                    else:
                        C.cp("act", qst[64:96, hh, 0:n], C.ps[ba][64:96, 0:n], [f"ps{ba}"], [("qst", "r")])
                C.dma("sp", kd["q"].ap().rearrange("h e n -> e h n")[:, :, t0:t0 + n], qst[0:96, :, 0:n],
                      [("qst", "n"), ("qst", "r")], [("q", kind)])
        for ag in pend_ag:
            C.allgather(*ag)
        pend_ag = []
        upv = upx.ap().rearrange("j p n -> (j p) n")
        if 'edge' not in SKIP:
            C.dma("sp", e_own.ap()[:, 0:16], upv[:, 16:32], [("up", "x")], ["e_own"])
            C.dma("sp", e_own.ap()[:, 16:32], upv[:, TC:TC + 16], [("up", "x")], ["e_own"])
        if 'ag' not in SKIP:
            for ag in defer_ag:
                C.allgather(*ag, persist=True)
            C.allgather(e_own.ap(), e_all.ap(), ["e_own"], ["e_all"], persist=True)
        C.reset(mA)
        if dbg == "A":
            break
        V = SimpleNamespace(**locals())
        phase_attention(C, V)
        if dbg == "C":
            break
        phase_fourier(C, V)
        W2_OFF = SB_TOP - 65536
        W1_OFF = W2_OFF - 65536
        WO_OFF = W1_OFF - 16384
        V.wo = C.alloc_at("wo", [128, 8, D], BF16, WO_OFF)
        V.w1 = C.alloc_at("w1", [128, 8, DFF], BF16, W1_OFF)
        V.w2 = C.alloc_at("w2", [128, 32, D], BF16, W2_OFF)
        def prefetch_weights(after_keys, l=l, V=V):
            for kc in range(8):
                C.dma("pool", V.wo[:, kc, :], w_o.ap()[l, kc * 128:(kc + 1) * 128, :], after_keys if kc == 0 else [],
                      [("wo", kc)], persist=True)
            for kc in range(8):
                C.dma("pool", V.w1[:, kc, :], w_mlp1.ap()[l, kc * 128:(kc + 1) * 128, :], [], [("w1", kc)], persist=True)
            for jc in range(32):
                C.dma("pool", V.w2[:, jc, :], w_mlp2.ap()[l, jc * 128:(jc + 1) * 128, :], [], [("w2", jc)], persist=True)
        V.after_first_loads = prefetch_weights
        C.limit = WO_OFF
        phase_conv(C, V)
        if dbg == "E":
            break
        phase_out(C, V)
        C.limit = W1_OFF
        phase_mlp(C, V)
        C.limit = SB_TOP
        C.P.persist_w = {}

    if dbg is None:
        phase_final(C, SimpleNamespace(**locals()))
    P.barrier()
    P.emit(nc)
    return nc


def phase_attention(C, V):
    l, last, TC, NB, NK, NKT = V.l, V.last, V.TC, V.NB, V.NK, V.NKT
    m = C.mark()
    wukv = V.wukv_t
    KVn = C.alloc("KVn", [128, NK], BF16)
    Kt = [C.alloc(f"Kt{i}", [128, NK], BF16) for i in range(2)]
    Vt = [C.alloc(f"Vt{i}", [128, NKT, 128], BF16) for i in range(2)]
    Qt = [C.alloc(f"Qt{i}", [128, TC], BF16) for i in range(2)]
    Qc = C.alloc("Qc", [128, CT], BF16)
    Pt = [C.alloc(f"Pt{i}", [128, 512], BF16) for i in range(4)]
    rinv = C.alloc("rinv", [128, 512], F32)
    aost = [C.alloc(f"aost{i}", [128, 512], BF16) for i in range(2)]
    TCq, NKV = V.TCq, V.NKV
    kvkeys = [("KVn", i) for i in range(1 + 4 * NKV)]
    C.dma("sp", KVn[:, 0:CT], V.kvc.ap()[0:128, :], [], [kvkeys[0]])
    for r in range(4):
        for cj in range(NKV):
            c0 = CT + r * TC + cj * TCq
            C.dma("sp", KVn[:, c0:c0 + TCq], V.kv_all[cj].ap()[r * 160:r * 160 + 128, :], [], [kvkeys[1 + r * NKV + cj]])
    krkeys = {}
    for i in range(2):
        krkeys[i] = [("Kr", i, k) for k in range(1 + 4 * NKV)]
        C.dma("sp", Kt[i][64:96, 0:CT], V.kvc.ap()[128:160, :], [], [krkeys[i][0]])
        for r in range(4):
            for cj in range(NKV):
                c0 = CT + r * TC + cj * TCq
                C.dma("sp", Kt[i][64:96, c0:c0 + TCq], V.kv_all[cj].ap()[r * 160 + 128:r * 160 + 160, :],
                      [], [krkeys[i][1 + r * NKV + cj]])
        C.memset("dve", Vt[i][:, :, 64:128], 1.0, [("Vone", i)])
    state = {"ob": 0, "ao": 0}

    def attn_block(hp, qap, qkey, n, nkt, dst, hook=None):
        ob = 4 + state["ob"] % 2
        state["ob"] += 1
        LOOK = 2
        for i in range(nkt + LOOK):
            if i < nkt:
                sbk = i % 4
                C.mm(C.ps[sbk][:, 0:n], Kt[hp][0:96, i * 128:(i + 1) * 128], qap, True, True,
                     [("Kn", hp), qkey] + krkeys[hp], [f"ps{sbk}"])
                C.act(Pt[sbk][:, 0:n], C.ps[sbk][:, 0:n], AF.Exp, [f"ps{sbk}"], [f"Pt{sbk}"], scale=ATTN_SCALE)
            ii = i - LOOK
            if ii >= 0:
                C.mm(C.ps[ob][:, 0:n], Vt[hp][:, ii, :], Pt[ii % 4][:, 0:n], ii == 0, ii == nkt - 1,
                     [f"Pt{ii % 4}", ("Vv", hp), ("Vone", hp)], [f"ps{ob}"])
                if hook is not None:
                    hook()
        a = state["ao"] % 2
        state["ao"] += 1
        C.recip(rinv[64:128, 0:n], C.ps[ob][64:128, 0:n], [f"ps{ob}"], ["rinv"])
        C.tt("dve", aost[a][0:64, 0:n], C.ps[ob][0:64, 0:n], rinv[64:128, 0:n], ALU.mult, [f"ps{ob}", "rinv"], [f"aost{a}"])
        C.dma("sp", dst, aost[a][0:64, 0:n], [f"aost{a}"], [])

    def build_steps(h):
        hp = h % 2
        steps = []
        steps.append(lambda: C.dma("sp", Qt[hp][0:96, :], V.qx.ap()[h], [], [("Q", hp)]))
        for c0 in range(0, NK, 512):
            nb = min(512, NK - c0)

            def kstep(c0=c0, nb=nb):
                b = C.bank(6, 8)
                C.mm(C.ps[b][0:64, 0:nb], wukv[:, h * 128:h * 128 + 64], KVn[:, c0:c0 + nb], True, True,
                     kvkeys + ["wukv"], [f"ps{b}"])
                C.cp("dve", Kt[hp][0:64, c0:c0 + nb], C.ps[b][0:64, 0:nb], [f"ps{b}"], [("Kn", hp)])
            steps.append(kstep)
        for g in range(0, NKT, 8):
            ng = min(8, NKT - g)

            def vstep(g=g, ng=ng):
                b = C.bank(6, 8)
                for kt in range(g, g + ng):
                    C.mm(C.ps[b][:, (kt - g) * 64:(kt - g + 1) * 64], KVn[:, kt * 128:(kt + 1) * 128],
                         wukv[:, h * 128 + 64:h * 128 + 128], True, True, kvkeys + ["wukv"], [f"ps{b}"])
                C.cp("dve", Vt[hp][:, g:g + ng, 0:64], C.ps[b][:, 0:ng * 64].rearrange("p (t c) -> p t c", c=64),
                     [f"ps{b}"], [("Vv", hp)])
            steps.append(vstep)
        return steps

    for st in build_steps(0):
        st()
    for h in range(NH):
        hp = h % 2
        nxt = build_steps(h + 1) if h + 1 < NH else []
        total_tiles = NB * NKT
        every = max(1, total_tiles // (len(nxt) + 2)) if nxt else 0
        cnt = {"n": 0}

        def hook():
            cnt["n"] += 1
            if nxt and every and cnt["n"] % every == 0:
                nxt.pop(0)()
        for qb in range(NB):
            dst = V.mixx.ap()[2 + h // 2, (h % 2) * 64:(h % 2) * 64 + 64, qb * 512:(qb + 1) * 512]
            attn_block(hp, Qt[hp][0:96, qb * 512:(qb + 1) * 512], ("Q", hp), 512, NKT, dst, hook=hook)
        if not last:
            C.dma("sp", Qc[0:96, :], V.qc.ap()[h], [], ["Qc"])
            dst = V.mixc.ap()[2 + h // 2, (h % 2) * 64:(h % 2) * 64 + 64, :]
            attn_block(hp, Qc[0:96, :], "Qc", CT, 2, dst)
        while nxt:
            nxt.pop(0)()
    C.reset(m)


def phase_fourier(C, V):
    l, last, TC, NB, NT, TCt = V.l, V.last, V.TC, V.NB, V.NT, V.TCt
    m = C.mark()
    cs = C.alloc("cs", [128, 2 * NT], BF16)
    r1 = C.alloc("r1", [128, NT, 64], BF16)
    r2 = C.alloc("r2", [128, NT, 64], BF16)
    ccb = C.alloc("ccb", [128, 128], BF16); mscb = C.alloc("mscb", [128, 128], BF16)
    wf = C.alloc("wf", [128, 2, 256], BF16)
    Yt = C.alloc("Yt", [128, 2, TC], BF16)
    Z = C.alloc("Z", [128, 128, 128], BF16)
    T = C.alloc("T", [128, 128, 2 * NT], BF16)
    Ure = C.alloc("Ure", [128, TC], BF16)
    Vv = C.alloc("Vv", [128, TC], BF16)
    yst = [C.alloc(f"yst{i}", [128, 2, 512], BF16) for i in range(2)]
    C.dma("sp", cs[0:NT, :], V.fft_cs.ap(), [], ["cs"])
    C.dma("sp", r1[:, :, :], V.fft_r1.ap().rearrange("p (k c) -> p k c", c=64), [], ["r1"])
    C.dma("sp", r2[:, :, :], V.fft_r2.ap().rearrange("p (k c) -> p k c", c=64), [], ["r2"])
    C.dma("sp", ccb[:, :], V.ccb_d.ap(), [], ["ccb"])
    C.dma("sp", mscb[:, :], V.mscb_d.ap(), [], ["mscb"])
    C.dma("pool", wf[:, :, :], V.w_fourier.ap()[l].rearrange("(j p) n -> p j n", p=128), [], ["wf"])
    cnt = 0
    for gp in range(2):
        FR, NF = V.FR, V.NF
        zkeys = []
        for r in range(4):
            rows_per = min(FR, TC)
            for cj in range(TC // rows_per):
                g0 = gp * TC + cj * rows_per
                ch, off = g0 // FR, g0 % FR
                p0 = r * TCt + cj * (rows_per // 128)
                zk = ("Z", r, cj)
                zkeys.append(zk)
                C.dma("sp", Z[p0:p0 + rows_per // 128, :, :],
                      V.f_all[ch].ap()[r * FR + off:r * FR + off + rows_per, :].rearrange("(a b) c -> a b c", b=128),
                      [("f_all", ch)], [zk])
        cpb = 512 // (2 * NT)
        cpb = min(cpb, 128)
        for c0 in range(0, 128, cpb):
            b = C.bank()
            for c in range(c0, c0 + cpb):
                C.mm(C.ps[b][:, (c - c0) * 2 * NT:(c - c0 + 1) * 2 * NT], Z[0:NT, :, c], cs[0:NT, :], True, True,
                     zkeys + ["cs"], [f"ps{b}"])
            C.cp("act" if cnt % 2 else "dve", T[:, c0:c0 + cpb, :],
                 C.ps[b][:, 0:cpb * 2 * NT].rearrange("p (c k) -> p c k", k=2 * NT), [f"ps{b}"], ["T"])
            cnt += 1
        for k0 in range(0, NT, 8):
            b = C.bank()
            for k1 in range(k0, k0 + 8):
                o = C.ps[b][:, (k1 - k0) * 64:(k1 - k0 + 1) * 64]
                C.mm(o, T[:, :, k1], r1[:, k1, :], True, False, ["T", "r1"], [f"ps{b}"])
                C.mm(o, T[:, :, NT + k1], r2[:, k1, :], False, True, ["T", "r2"], [f"ps{b}"])
            psv = C.ps[b][:, :].rearrange("p (k1 two k2) -> p two k2 k1", k1=8, two=2, k2=32)
            C.cp("dve", Ure[:, :].rearrange("p (k2 k1) -> p k2 k1", k1=NT)[:, :, k0:k0 + 8], psv[:, 0], [f"ps{b}"], ["Ure"])
            C.cp("act", Vv[:, :].rearrange("p (k2 k1) -> p k2 k1", k1=NT)[:, :, k0:k0 + 8], psv[:, 1], [f"ps{b}"], ["Vv"])
        for tb in range(NB):
            b = C.bank()
            C.mm(C.ps[b][:, :], ccb[:, :], Ure[:, tb * 512:(tb + 1) * 512], True, False, ["Ure", "ccb"], [f"ps{b}"])
            C.mm(C.ps[b][:, :], mscb[:, :], Vv[:, tb * 512:(tb + 1) * 512], False, True, ["Vv", "mscb"], [f"ps{b}"])
            C.cp("act" if tb % 2 else "dve", Yt[:, gp, tb * 512:(tb + 1) * 512], C.ps[b][:, :], [f"ps{b}"], [("Yt", gp)])
    mixv = V.mixx.ap().rearrange("kc p n -> p kc n")
    for tb in range(NB):
        y = yst[tb % 2]
        for oc in range(2):
            b = C.bank()
            for gp in range(2):
                C.mm(C.ps[b][:, :], wf[:, gp, oc * 128:(oc + 1) * 128], Yt[:, gp, tb * 512:(tb + 1) * 512], gp == 0, gp == 1,
                     [("Yt", 0), ("Yt", 1), "wf"], [f"ps{b}"])
            C.cp("act" if oc else "dve", y[:, oc, :], C.ps[b][:, :], [f"ps{b}"], [f"yst{tb % 2}"])
        C.dma("sp", mixv[:, 0:2, tb * 512:(tb + 1) * 512], y[:, :, :], [f"yst{tb % 2}"], [])
    if not last:
        Zc = C.alloc("Zc", [128, 2, 256], BF16)
        w256 = C.alloc("w256", [128, 2, 512], BF16)
        ccbc = C.alloc("ccbc", [128, 128], BF16); mscbc = C.alloc("mscbc", [128, 128], BF16)
        UVc = C.alloc("UVc", [128, 512], BF16)
        Ytc = C.alloc("Ytc", [128, 2, 256], BF16)
        ystc = C.alloc("ystc", [128, 2, 256], BF16)
        C.dma("sp", Zc[:, :, :], V.fc.ap().rearrange("(tt p) c -> p tt c", p=128), [], ["Zc"])
        C.dma("sp", w256[:, :, :], V.w256_d.ap().rearrange("(tt p) k -> p tt k", p=128), [], ["w256"])
        C.dma("sp", ccbc[:, :], V.ccbc_d.ap(), [], ["ccbc"])
        C.dma("sp", mscbc[:, :], V.mscbc_d.ap(), [], ["mscbc"])
        for cp_ in range(2):
            b = C.bank()
            for nt in range(2):
                C.mm(C.ps[b][:, :], Zc[:, nt, cp_ * 128:(cp_ + 1) * 128], w256[:, nt, :], nt == 0, nt == 1,
                     ["Zc", "w256"], [f"ps{b}"])
            C.cp("dve", UVc[:, :], C.ps[b][:, :], [f"ps{b}"], ["UVc"])
            b = C.bank()
            C.mm(C.ps[b][:, 0:256], ccbc[:, :], UVc[:, 0:256], True, False, ["UVc", "ccbc"], [f"ps{b}"])
            C.mm(C.ps[b][:, 0:256], mscbc[:, :], UVc[:, 256:512], False, True, ["UVc", "mscbc"], [f"ps{b}"])
            C.cp("act", Ytc[:, cp_, :], C.ps[b][:, 0:256], [f"ps{b}"], [("Ytc", cp_)])
        for oc in range(2):
            b = C.bank()
            for gp in range(2):
                C.mm(C.ps[b][:, 0:256], wf[:, gp, oc * 128:(oc + 1) * 128], Ytc[:, gp, :], gp == 0, gp == 1,
                     [("Ytc", 0), ("Ytc", 1), "wf"], [f"ps{b}"])
            C.cp("act" if oc else "dve", ystc[:, oc, :], C.ps[b][:, 0:256], [f"ps{b}"], ["ystc"])
        C.dma("sp", V.mixc.ap().rearrange("kc p n -> p kc n")[:, 0:2, :], ystc[:, :, :], ["ystc"], [])
    C.reset(m)


def phase_conv(C, V):
    l, last, TC = V.l, V.last, V.TC
    modT, wdwT, bdwT, lngT, lnbT = V.modT, V.wdwT, V.bdwT, V.lngT, V.lnbT
    m = C.mark()
    wpw = C.alloc("wpw", [128, 2, 256], BF16)
    C.dma("pool", wpw[:, :, :], V.w_pw2.ap()[l].rearrange("(j p) n -> p j n", p=128), [], ["wpw"])
    E = C.alloc("E", [128, 4, 2, 32], BF16)
    hl = C.alloc("hl", [128, 2, 16], F32); hr = C.alloc("hr", [128, 2, 16], F32)
    hb = C.alloc("hb", [128, 2, 32], BF16)
    C.dma("sp", E[:, :, :, :], V.e_all.ap().rearrange("(r j p) n -> p r j n", r=4, j=2), ["e_all"], ["E"])
    masks = V.masks
    C.ts("dve", hl[:, :, :], E[:, 0, :, 16:32], masks[:, 0:1], None, ALU.mult, None, ["E", "masks"], ["hl"])
    C.ts("dve", hr[:, :, :], E[:, 0, :, 0:16], masks[:, 4:5], None, ALU.mult, None, ["E", "masks"], ["hr"])
    for r in range(1, 4):
        C.stt("dve", hl[:, :, :], E[:, r, :, 16:32], masks[:, r:r + 1], hl[:, :, :], ALU.mult, ALU.add, ["E", "masks", "hl"], ["hl"])
        C.stt("dve", hr[:, :, :], E[:, r, :, 0:16], masks[:, 4 + r:5 + r], hr[:, :, :], ALU.mult, ALU.add, ["E", "masks", "hr"], ["hr"])
    C.cp("dve", hb[:, :, 0:16], hl[:, :, :], ["hl"], ["hb"])
    C.cp("dve", hb[:, :, 16:32], hr[:, :, :], ["hr"], ["hb"])
    upv = V.upx.ap().rearrange("j p n -> p j n")
    C.dma("sp", upv[:, :, 0:16], hb[:, :, 0:16], ["hb"], ["uph"])
    C.dma("sp", upv[:, :, TC + 16:TC + 32], hb[:, :, 16:32], ["hb"], ["uph"])

    U = [C.alloc(f"U{i}", [128, 2, 544], BF16) for i in range(2)]
    cvb = [C.alloc(f"cv{i}", [128, 2, 512], F32) for i in range(2)]
    xc = C.alloc("xc", [128, 2, 512], F32)
    sqc = C.alloc("sqc", [128, 2, 512], F32)
    mean = C.alloc("mean", [128, 512], F32); rsc = C.alloc("rsc", [128, 512], F32)
    rstdc = C.alloc("rstdc", [128, 512], F32)
    tmpc = [C.alloc(f"tmpc{i}", [128, 512], F32) for i in range(2)]
    sl = C.alloc("sl", [128, 2, 512], BF16)
    ycst = [C.alloc(f"ycst{i}", [128, 2, 512], BF16) for i in range(2)]
    Dg = C.alloc("Dg", [128, 62, 128], BF16)
    for kj in range(62):
        C.ts("dve", Dg[:, kj, :], V.ident[:, :], wdwT[:, l, kj:kj + 1], None, ALU.mult, None,
             ["ident", "wdwT"], [("Dg", kj % 2)])
    bl = V.blocks(with_ctx=not last)

    def loadU(bi):
        kind, t0, n = bl[bi]
        up = V.kinds[kind]["up"].ap().rearrange("j p n -> p j n")
        C.dma("sp", U[bi % 2][:, :, 0:n + 32], up[:, :, t0:t0 + n + 32], ["uph"], [f"U{bi % 2}"])
    def convmm(bi):
        kind, t0, n = bl[bi]
        pb = bi % 2
        Ub = U[pb]
        uk = f"U{pb}"
        for j in range(2):
            b = C.bank()
            for k in range(31):
                C.mm(C.ps[b][:, 0:n], Dg[:, k * 2 + j, :], Ub[:, j, k + 1:k + 1 + n], k == 0, k == 30,
                     [uk, ("Dg", 0), ("Dg", 1)], [f"ps{b}"])
            C.act(cvb[pb][:, j, 0:n], C.ps[b][:, 0:n], AF.Identity, [f"ps{b}", "bdwT"], [("cv", pb, j)], bias=bdwT[:, l, j:j + 1])

    loadU(0)
    if len(bl) > 1:
        loadU(1)
    if V.after_first_loads is not None:
        V.after_first_loads(["E", "U0", "U1", "wpw"])
    convmm(0)
    for bi, (kind, t0, n) in enumerate(bl):
        pb = bi % 2
        kd = V.kinds[kind]
        cv = cvb[pb]
        if bi + 2 < len(bl):
            loadU(bi + 2)
        if bi + 1 < len(bl):
            convmm(bi + 1)
        b = C.bank()
        for j in range(2):
            C.mm(C.ps[b][:, 0:n], V.ones_f[:, :], cv[:, j, 0:n], j == 0, j == 1, [("cv", pb, j), "ones_f"], [f"ps{b}"])
        C.act(mean[:, 0:n], C.ps[b][:, 0:n], AF.Identity, [f"ps{b}"], ["mean"], scale=1.0 / 256)
        for j in range(2):
            C.tt("dve", xc[:, j, 0:n], cv[:, j, 0:n], mean[:, 0:n], ALU.subtract, [("cv", pb, j), "mean"], ["xc"])
        C.act(sqc[:, :, 0:n], xc[:, :, 0:n], AF.Square, ["xc"], ["sqc"])
        b = C.bank()
        for j in range(2):
            C.mm(C.ps[b][:, 0:n], V.ones_f[:, :], sqc[:, j, 0:n], j == 0, j == 1, ["sqc", "ones_f"], [f"ps{b}"])
        C.act(rsc[:, 0:n], C.ps[b][:, 0:n], AF.Sqrt, [f"ps{b}", "eps"], ["rsc"], bias=V.eps_t[:, 0:1], scale=1.0 / 256)
        C.recip(rstdc[:, 0:n], rsc[:, 0:n], ["rsc"], ["rstdc"])
        for j in range(2):
            C.tt("dve", tmpc[j][:, 0:n], xc[:, j, 0:n], rstdc[:, 0:n], ALU.mult, ["xc", "rstdc"], [f"tmpc{j}"])
            C.act(sl[:, j, 0:n], tmpc[j][:, 0:n], AF.Silu, [f"tmpc{j}", "lngT", "lnbT"], ["sl"],
                  bias=lnbT[:, l, j:j + 1], scale=lngT[:, l, j:j + 1])
        for oc in range(2):
            b = C.bank()
            for j in range(2):
                C.mm(C.ps[b][:, 0:n], wpw[:, j, oc * 128:(oc + 1) * 128], sl[:, j, 0:n], j == 0, j == 1, ["sl", "wpw"], [f"ps{b}"])
            C.cp("act" if oc else "dve", ycst[pb][:, oc, 0:n], C.ps[b][:, 0:n], [f"ps{b}"], [f"ycst{pb}"])
        C.dma("sp", kd["mix"].ap().rearrange("kc p n -> p kc n")[:, 6:8, t0:t0 + n], ycst[pb][:, :, 0:n], [f"ycst{pb}"], [])
    C.reset(m)


def _norm_mod(C, V, xbt, xkey, n, l, j, gm, sh_off, sq, rs, rstd, tmp, hout, hkey, sqkey="sqN"):
    C.act(sq[:, :, 0:n], xbt[:, :, 0:n], AF.Square, [xkey], [sqkey])
    b = C.bank()
    for kc in range(8):
        C.mm(C.ps[b][:, 0:n], V.ones_bf[:, :], sq[:, kc, 0:n], kc == 0, kc == 7, [sqkey, "ones_bf"], [f"ps{b}"])
    C.act(rs[:, 0:n], C.ps[b][:, 0:n], AF.Sqrt, [f"ps{b}", "eps"], ["rsN"], bias=V.eps_t[:, 0:1], scale=1.0 / D)
    C.recip(rstd[:, 0:n], rs[:, 0:n], ["rsN"], ["rstdN"])
    for kc in range(8):
        C.tt("dve", tmp[kc % 2][:, 0:n], xbt[:, kc, 0:n], rstd[:, 0:n], ALU.mult, [xkey, "rstdN"], [f"tmpN{kc % 2}"])
        C.act(hout[:, kc, 0:n], tmp[kc % 2][:, 0:n], AF.Identity, [f"tmpN{kc % 2}", "gm", "modT"], [hkey],
              bias=V.modT[:, l, j, sh_off + kc:sh_off + kc + 1], scale=gm[:, l, j, kc:kc + 1])


def phase_out(C, V):
    l, last = V.l, V.last
    m = C.mark()
    wo = V.wo
    M = [C.alloc(f"M{i}", [128, 8, 512], BF16) for i in range(2)]
    xb0 = C.alloc("xbO0", [128, 8, 512], F32)
    xb = [xb0, xb0]
    rs = C.alloc("rsO", [128, 512], F32); rstd = C.alloc("rstdO", [128, 512], F32)
    tmp = [C.alloc(f"tmpO{i}", [128, 512], F32) for i in range(2)]
    h20 = C.alloc("h2O0", [128, 8, 512], BF16)
    h2 = [h20, h20]
    bl = V.blocks(with_ctx=not last)

    def load(bi):
        kind, t0, n = bl[bi]
        kd = V.kinds[kind]
        C.dma("sp", M[bi % 2][:, :, 0:n], V.tview(kd["mix"], t0, n), [], [f"M{bi % 2}"])
    load(0)
    for bi, (kind, t0, n) in enumerate(bl):
        pb = bi % 2
        kd = V.kinds[kind]
        j = kd["j"]
        if bi + 1 < len(bl):
            load(bi + 1)
        C.dma("sp", xb[pb][:, :, 0:n], V.tview(kd["xT"], t0, n), [("xT", kind, t0)], ["xbO0"])
        for oc in range(8):
            b = C.bank()
            for kc in range(8):
                C.mm(C.ps[b][:, 0:n], wo[:, kc, oc * 128:(oc + 1) * 128], M[pb][:, kc, 0:n], kc == 0, kc == 7,
                     [f"M{pb}", ("wo", kc)], [f"ps{b}"])
            C.stt("dve", xb[pb][:, oc, 0:n], C.ps[b][:, 0:n], V.modT[:, l, j, 16 + oc:17 + oc], xb[pb][:, oc, 0:n],
                  ALU.mult, ALU.add, [f"ps{b}", "modT", "xbO0"], ["xbO0"])
        C.dma("sp", V.tview(kd["xT"], t0, n), xb[pb][:, :, 0:n], ["xbO0"], [("xT", kind, t0)])
        _norm_mod(C, V, xb[pb], "xbO0", n, l, j, V.gm2, 24, h2[pb], rs, rstd, tmp, h2[pb], "h2O0", sqkey="h2O0")
        C.dma("sp", V.tview(kd["h2"], t0, n), h2[pb][:, :, 0:n], ["h2O0"], [])
    C.reset(m)


def phase_mlp(C, V):
    l, last = V.l, V.last
    m = C.mark()
    w1 = V.w1
    w2 = V.w2
    h2 = [C.alloc(f"h2M{i}", [128, 8, 512], BF16) for i in range(2)]
    x1 = C.alloc("x1M", [128, 8, 512], F32)
    hid = C.alloc("hid", [128, 32, 512], BF16)
    rt = [C.alloc(f"rt{i}", [128, 512], F32) for i in range(2)]
    bl = V.blocks(with_ctx=not last)

    def load(bi):
        kind, t0, n = bl[bi]
        C.dma("sp", h2[bi % 2][:, :, 0:n], V.tview(V.kinds[kind]["h2"], t0, n), [], [f"h2M{bi % 2}"])
    load(0)
    for bi, (kind, t0, n) in enumerate(bl):
        pb = bi % 2
        kd = V.kinds[kind]
        j = kd["j"]
        if bi + 1 < len(bl):
            load(bi + 1)
        C.dma("sp", x1[:, :, 0:n], V.tview(kd["xT"], t0, n), [], ["x1M"])
        for jc in range(32):
            b = C.bank()
            for kc in range(8):
                C.mm(C.ps[b][:, 0:n], w1[:, kc, jc * 128:(jc + 1) * 128], h2[pb][:, kc, 0:n], kc == 0, kc == 7,
                     [f"h2M{pb}", ("w1", kc)], [f"ps{b}"])
            C.act(rt[jc % 2][:, 0:n], C.ps[b][:, 0:n], AF.Relu, [f"ps{b}"], [f"rt{jc % 2}"])
            C.tt("pool" if jc % 2 else "dve", hid[:, jc, 0:n], rt[jc % 2][:, 0:n], rt[jc % 2][:, 0:n], ALU.mult,
                 [f"rt{jc % 2}"], [("hid", jc % 2)])
        for oc in range(8):
            b = C.bank()
            for jc in range(32):
                C.mm(C.ps[b][:, 0:n], w2[:, jc, oc * 128:(oc + 1) * 128], hid[:, jc, 0:n], jc == 0, jc == 31,
                     [("hid", 0), ("hid", 1), ("w2", jc)], [f"ps{b}"])
            C.stt("dve", x1[:, oc, 0:n], C.ps[b][:, 0:n], V.modT[:, l, j, 40 + oc:41 + oc], x1[:, oc, 0:n],
                  ALU.mult, ALU.add, [f"ps{b}", "modT", "x1M"], ["x1M"])
        C.dma("sp", V.tview(kd["xT"], t0, n), x1[:, :, 0:n], ["x1M"], [])
    C.reset(m)


def phase_final(C, V):
    m = C.mark()
    xb = [C.alloc(f"xbF{i}", [128, 8, 512], F32) for i in range(2)]
    sq = C.alloc("sqF", [128, 8, 512], BF16)
    rs = C.alloc("rsF", [128, 512], F32); rstd = C.alloc("rstdF", [128, 512], F32)
    y = C.alloc("yF", [128, 8, 512], F32)
    otok = [C.alloc(f"otok{i}", [128, 4, D], F32) for i in range(2)]
    bl = V.blocks(with_ctx=False)

    def load(bi):
        kind, t0, n = bl[bi]
        C.dma("sp", xb[bi % 2][:, :, 0:n], V.tview(V.xT, t0, n), [], [f"xbF{bi % 2}"])
    load(0)
    cnt = 0
    for bi, (kind, t0, n) in enumerate(bl):
        pb = bi % 2
        if bi + 1 < len(bl):
            load(bi + 1)
        C.act(sq[:, :, 0:n], xb[pb][:, :, 0:n], AF.Square, [f"xbF{pb}"], ["sqF"])
        b = C.bank()
        for kc in range(8):
            C.mm(C.ps[b][:, 0:n], V.ones_bf[:, :], sq[:, kc, 0:n], kc == 0, kc == 7, ["sqF", "ones_bf"], [f"ps{b}"])
        C.act(rs[:, 0:n], C.ps[b][:, 0:n], AF.Sqrt, [f"ps{b}", "eps"], ["rsF"], bias=V.eps_t[:, 0:1], scale=1.0 / D)
        C.recip(rstd[:, 0:n], rs[:, 0:n], ["rsF"], ["rstdF"])
        for kc in range(8):
            C.stt("dve", y[:, kc, 0:n], xb[pb][:, kc, 0:n], V.fngT[:, kc:kc + 1], rstd[:, 0:n], ALU.mult, ALU.mult,
                  [f"xbF{pb}", "rstdF", "fngT"], ["yF"])
        for tt_ in range(n // 128):
            for kc2 in range(0, 8, 4):
                b = C.bank()
                for kc in range(kc2, kc2 + 4):
                    C.tr(C.ps[b][:, (kc - kc2) * 128:(kc - kc2 + 1) * 128], y[:, kc, tt_ * 128:(tt_ + 1) * 128], V.ident[:, :],
                         ["yF", "ident"], [f"ps{b}"])
                C.cp("act" if cnt % 2 else "dve", otok[pb][:, tt_, kc2 * 128:kc2 * 128 + 512], C.ps[b][:, :], [f"ps{b}"], [f"otok{pb}"])
                cnt += 1
        C.dma("sp", V.out_d.ap()[t0:t0 + n, :].rearrange("(tt p) d -> p tt d", p=128), otok[pb][:, 0:n // 128, :], [f"otok{pb}"], [])
    C.reset(m)


_BF = ml_dtypes.bfloat16


def _tables(NT, r):
    S = 128 * NT
    TC = S // 4
    N = S
    tb = {}
    t = np.arange(S)
    row = (t // 64).astype(np.float32)
    col = (t % 64).astype(np.float32)
    inv = (10000.0 ** (-np.arange(0, 16, 2, dtype=np.float32) / 16)).astype(np.float32)
    ang = np.concatenate([row[:, None] * inv, col[:, None] * inv], axis=-1).astype(np.float32)
    cos = np.cos(ang).astype(np.float32)[r * TC:(r + 1) * TC].T
    sin = np.sin(ang).astype(np.float32)[r * TC:(r + 1) * TC].T
    tb["rope1"] = np.ascontiguousarray(np.concatenate([cos, cos], 0))
    tb["rope2"] = np.ascontiguousarray(np.concatenate([-sin, sin], 0))
    n1 = np.arange(NT)[:, None]; k1 = np.arange(NT)[None, :]
    a = 2 * np.pi * ((n1 * k1) % NT) / NT
    tb["fft_cs"] = np.concatenate([np.cos(a), np.sin(a)], 1).astype(_BF)
    n2 = np.arange(128)[:, None, None]
    k1 = np.arange(NT)[None, :, None]
    k2 = np.arange(32)[None, None, :]
    k = k1 + NT * (32 * r + k2)
    a = 2 * np.pi * ((n2 * k) % N) / N
    wc, ws = np.cos(a), np.sin(a)
    tb["fft_r1"] = np.concatenate([wc, ws], 2).reshape(128, NT * 64).astype(_BF)
    tb["fft_r2"] = np.concatenate([-ws, wc], 2).reshape(128, NT * 64).astype(_BF)
    c = np.arange(128)[:, None]; mm_ = np.arange(128)[None, :]
    same = (c // 64) == (mm_ // 64)
    a = 2 * np.pi * (((c % 64) * (mm_ % 64)) % 64) / 64
    for nm, nn in (("", N), ("c", CT)):
        sc = 1.0 / np.sqrt(nn * 64.0)
        tb["ccb" + nm] = (np.where(same, np.cos(a), 0.0) * sc).astype(_BF)
        tb["mscb" + nm] = (np.where(same, -np.sin(a), 0.0) * sc).astype(_BF)
    n = np.arange(CT)[:, None]; kk = np.arange(CT)[None, :]
    a = 2 * np.pi * ((n * kk) % CT) / CT
    tb["w256"] = np.concatenate([np.cos(a), np.sin(a)], 1).astype(_BF)
    tb["ident"] = np.eye(128, dtype=np.float32)
    mk = np.zeros((128, 8), np.float32)
    if r > 0:
        mk[:, r - 1] = 1.0
    if r < 3:
        mk[:, 4 + r + 1] = 1.0
    tb["masks"] = mk
    return tb


_CACHE = {}


def run_model(inputs, NT, L, dbg=None, dump=None):
    key = (NT, L, dbg, tuple(sorted(dump)) if dump else None)
    if key not in _CACHE:
        _CACHE[key] = build_program(NT, L, dbg=dbg, dump=dump)
    nc = _CACHE[key]
    S = 128 * NT
    TC = S // 4
    f32 = lambda a: np.ascontiguousarray(np.asarray(a, dtype=np.float32))
    wnames = ["w_mod", "b_mod", "norm1_g", "w_in", "q_norm_g", "w_uq", "kv_norm_g", "w_ukv", "w_fourier", "w_dw",
              "b_dw", "conv_ln_g", "conv_ln_b", "w_pw2", "w_o", "norm2_g", "w_mlp1", "w_mlp2"]
    shared = {n: f32(inputs[n])[:L] for n in wnames if n != "w_mod"}
    wmod_full = f32(inputs["w_mod"])[:L]
    wmod_sh = [np.ascontiguousarray(wmod_full[:, :, r * 1536:(r + 1) * 1536]) for r in range(4)]
    shared["final_norm_g"] = f32(inputs["final_norm_g"])
    x = f32(inputs["x"]); ctx = f32(inputs["ctx"]); c = f32(inputs["c"]); cc = f32(inputs["c_ctx"])
    in_maps = []
    for core in range(8):
        b, r = core // 4, core % 4
        m = dict(shared)
        m["x_in"] = np.ascontiguousarray(x[b, r * TC:(r + 1) * TC])
        m["ctx_in"] = np.ascontiguousarray(ctx[b])
        m["cvec"] = np.ascontiguousarray(np.stack([c[b], cc], 0))
        m["w_mod"] = wmod_sh[r]
        m.update(_tables(NT, r))
        in_maps.append(m)
    res = run_bass_kernel_spmd(nc, in_maps, core_ids=list(range(8)))
    return res


def kernel(**inputs):
    res = run_model(inputs, 128, 4)
    S = 128 * 128
    TC = S // 4
    out = np.empty((2, S, D), np.float32)
    for core in range(8):
        b, r = core // 4, core % 4
        out[b, r * TC:(r + 1) * TC] = np.asarray(res.results[core]["out"], dtype=np.float32)
    return out
```
